# Optimizing a Trainium2 kernel written in Bass

```python
import math
import jax, jax.numpy as jnp
from jax import lax
import numpy as np

D_MODEL = 4096
BATCH = 4
SEQ = 2048
DEPTH = 2
DEC_BATCH = 8
DEC_SEQ = 4
PAST_LEN = 16384
PAGE_SIZE = 128

SSM_GROUP = 16
SSM_GROUPS = D_MODEL // SSM_GROUP
SSM_STATE = 64
SSM_CHUNK = 128
D_FF = ((8 * D_MODEL // 3 + 255) // 256) * 256
CONV_W = 3
HEAD_DIM = 128
N_HEADS = D_MODEL // HEAD_DIM
N_KV_HEADS = 4
GQA = N_HEADS // N_KV_HEADS
CMP_STRIDE = 16
CMP_BLOCK = 2 * CMP_STRIDE
CMP_HIDDEN = 2 * HEAD_DIM
SEL_BLOCK = 64
N_SEL = 16
WINDOW = 512
Q_BLOCK = 128
N_BRANCH_KV = 6
N_A_LAYERS = DEPTH // 2
N_B_LAYERS = DEPTH - N_A_LAYERS
RMS_EPS = 1e-6

kernel_name = 'yoco_s5_nsa_convffn_step'


def rmsnorm(x, g):
    xf = x.astype(jnp.float32)
    y = xf * lax.rsqrt(jnp.mean(xf * xf, axis=-1, keepdims=True) + RMS_EPS)
    return (y * g.astype(jnp.float32)).astype(x.dtype)


def _ssm_combine(e1, e2):
    a1, b1 = e1
    a2, b2 = e2
    return a2 * a1, a2 * b1 + b2


def s5_mixer(u, h0, lam_re, lam_im, log_step, b_re, b_im, c_re, c_im, d_skip, w_glu):
    f32 = jnp.float32
    bsz, length, _ = u.shape
    lam = lax.complex(lam_re.astype(f32), lam_im.astype(f32))
    dt = jnp.exp(log_step.astype(f32))[:, None]
    lam_bar = jnp.exp(lam * dt)
    b_bar = ((lam_bar - 1.0) / lam)[..., None] * lax.complex(b_re.astype(f32), b_im.astype(f32))
    c = lax.complex(c_re.astype(f32), c_im.astype(f32))
    chunk = math.gcd(length, SSM_CHUNK)
    n_chunks = length // chunk
    uc = u.astype(f32).reshape(bsz, n_chunks, chunk, SSM_GROUPS, SSM_GROUP).transpose(1, 0, 2, 3, 4)

    def scan_block(h, u_c):
        bu = jnp.einsum('bcgn,gpn->bcgp', u_c.astype(jnp.complex64), b_bar)
        bu = bu.at[:, 0].add(lam_bar * h)
        a = jnp.broadcast_to(lam_bar, bu.shape)
        _, hs = lax.associative_scan(_ssm_combine, (a, bu), axis=1)
        y = jnp.real(jnp.einsum('bcgp,gnp->bcgn', hs, c))
        return hs[:, -1], y

    h_last, ys = lax.scan(scan_block, h0, uc)
    y = ys.transpose(1, 0, 2, 3, 4).reshape(bsz, length, D_MODEL) + d_skip.astype(f32) * u.astype(f32)
    z = jax.nn.gelu(y).astype(u.dtype)
    vg = z @ w_glu
    out = vg[..., :D_MODEL] * jax.nn.sigmoid(vg[..., D_MODEL:])
    return out, h_last


def conv_ffn(h, buf, w_in, conv_w, conv_b, w_down):
    length = h.shape[1]
    up = h @ w_in
    val, gate = up[..., :D_FF], up[..., D_FF:]
    g_ext = jnp.concatenate([buf.astype(gate.dtype), gate], axis=1)
    gc = conv_b
    for k in range(CONV_W):
        gc = gc + conv_w[k] * g_ext[:, k:k + length]
    out = (jax.nn.silu(gc) * val) @ w_down
    return out, g_ext[:, -(CONV_W - 1):]


def shared_kv(x, kv_norm, w_kv):
    bsz, t, _ = x.shape
    s = rmsnorm(x, kv_norm)
    return (s @ w_kv).reshape(bsz, t, N_BRANCH_KV, N_KV_HEADS, HEAD_DIM)


def compress_tokens(k, w1, b1, w2, b2, pe):
    bsz, t = k.shape[:2]
    n16 = t // CMP_STRIDE
    half = CMP_STRIDE * HEAD_DIM
    ch = k[:, :n16 * CMP_STRIDE].reshape(bsz, n16, CMP_STRIDE, N_KV_HEADS, HEAD_DIM)
    ch = ch.transpose(0, 1, 3, 2, 4).reshape(bsz, n16, N_KV_HEADS, half)
    pre = ch[:, :-1] @ w1[:half] + ch[:, 1:] @ w1[half:] + (pe.reshape(-1) @ w1 + b1)
    return jax.nn.gelu(pre) @ w2 + b2


def to_blocks(k):
    bsz, t = k.shape[:2]
    nb = -(-t // SEL_BLOCK)
    k = jnp.pad(k, ((0, 0), (0, nb * SEL_BLOCK - t), (0, 0), (0, 0)))
    return k.reshape(bsz, nb, SEL_BLOCK, N_KV_HEADS, HEAD_DIM).transpose(0, 3, 1, 2, 4)


def kv_side(rows, cmp_w1, cmp_b1, cmp_w2, cmp_b2, cmp_pe):
    cmp_k = compress_tokens(rows[:, :, 0], cmp_w1[0], cmp_b1[0], cmp_w2[0], cmp_b2[0], cmp_pe[0])
    cmp_v = compress_tokens(rows[:, :, 1], cmp_w1[1], cmp_b1[1], cmp_w2[1], cmp_b2[1], cmp_pe[1])
    return cmp_k, cmp_v, to_blocks(rows[:, :, 2]), to_blocks(rows[:, :, 3])


def query_side(h, w_qg):
    bsz, t, _ = h.shape
    hd = N_HEADS * HEAD_DIM
    qg = h @ w_qg
    q = qg[..., :hd].reshape(bsz, t, N_HEADS, HEAD_DIM)
    gates = jax.nn.sigmoid(qg[..., hd:].astype(jnp.float32)).reshape(bsz, t, N_HEADS, 3)
    return q, gates


def masked_softmax(s, mask):
    s = jnp.where(mask, s, -jnp.inf)
    m = jnp.max(s, axis=-1, keepdims=True)
    m = jnp.where(jnp.isfinite(m), m, 0.0)
    e = jnp.exp(s - m)
    den = jnp.sum(e, axis=-1, keepdims=True)
    return e / jnp.where(den > 0, den, 1.0)


def nsa_attend(q, gates, q_pos, cmp_k, cmp_v, sel_kb, sel_vb, win_k, win_v, win_pos):
    f32 = jnp.float32
    bsz, tq = q.shape[:2]
    scale = HEAD_DIM ** -0.5
    qg = q.reshape(bsz, tq, N_KV_HEADS, GQA, HEAD_DIM)
    n_cmp = cmp_k.shape[1]
    cstart = jnp.arange(n_cmp) * CMP_STRIDE
    cmp_end = cstart + (CMP_BLOCK - 1)
    s = jnp.einsum('bqhgd,bchd->bhgqc', qg, cmp_k, preferred_element_type=f32) * scale
    p_cmp = masked_softmax(s, cmp_end[None, :] <= q_pos[:, None])
    o_cmp = jnp.einsum('bhgqc,bchd->bqhgd', p_cmp.astype(cmp_v.dtype), cmp_v, preferred_element_type=f32)
    n_blk = sel_kb.shape[2]
    bstart = jnp.arange(n_blk) * SEL_BLOCK
    overlap = ((cstart[:, None] < bstart[None, :] + SEL_BLOCK) & (cstart[:, None] + CMP_BLOCK > bstart[None, :])).astype(f32)
    imp = jnp.einsum('bhgqc,cj->bhqj', p_cmp, overlap)
    q_blk = q_pos[:, None] // SEL_BLOCK
    blk = jnp.arange(n_blk)[None, :]
    forced = (blk == 0) | (blk == q_blk) | (blk == q_blk - 1)
    future = bstart[None, :] > q_pos[:, None]
    imp = jnp.where(future, -jnp.inf, jnp.where(forced, jnp.inf, imp))
    n_sel = min(N_SEL, n_blk)
    _, idx = lax.top_k(imp, n_sel)
    bi = jnp.arange(bsz)[:, None, None, None]
    hi = jnp.arange(N_KV_HEADS)[None, :, None, None]
    ks = sel_kb[bi, hi, idx].reshape(bsz, N_KV_HEADS, tq, n_sel * SEL_BLOCK, HEAD_DIM)
    vs = sel_vb[bi, hi, idx].reshape(bsz, N_KV_HEADS, tq, n_sel * SEL_BLOCK, HEAD_DIM)
    kpos = (idx[..., None] * SEL_BLOCK + jnp.arange(SEL_BLOCK)).reshape(bsz, N_KV_HEADS, tq, n_sel * SEL_BLOCK)
    s = jnp.einsum('bqhgd,bhqkd->bhgqk', qg, ks, preferred_element_type=f32) * scale
    p = masked_softmax(s, (kpos <= q_pos[None, None, :, None])[:, :, None])
    o_sel = jnp.einsum('bhgqk,bhqkd->bqhgd', p.astype(vs.dtype), vs, preferred_element_type=f32)
    s = jnp.einsum('bqhgd,bkhd->bhgqk', qg, win_k, preferred_element_type=f32) * scale
    wmask = ((win_pos[None, :] <= q_pos[:, None]) & (win_pos[None, :] > q_pos[:, None] - WINDOW)
             & (win_pos[None, :] >= 0))
    p = masked_softmax(s, wmask)
    o_win = jnp.einsum('bhgqk,bkhd->bqhgd', p.astype(win_v.dtype), win_v, preferred_element_type=f32)
    g = gates.reshape(bsz, tq, N_KV_HEADS, GQA, 3)
    o = g[..., 0:1] * o_cmp + g[..., 1:2] * o_sel + g[..., 2:3] * o_win
    return o.reshape(bsz, tq, N_HEADS * HEAD_DIM).astype(q.dtype)


def nsa_prompt(h, w_qg, w_o, side, win_rows):
    cmp_k, cmp_v, sel_kb, sel_vb = side
    bsz, t, _ = h.shape
    q, gates = query_side(h, w_qg)
    qb_len = math.gcd(t, Q_BLOCK)
    win_pad = jnp.pad(win_rows, ((0, 0), (WINDOW, 0), (0, 0), (0, 0), (0, 0)))

    def one_block(qb):
        start = qb * qb_len
        q_b = lax.dynamic_slice_in_dim(q, start, qb_len, axis=1)
        g_b = lax.dynamic_slice_in_dim(gates, start, qb_len, axis=1)
        w_b = lax.dynamic_slice_in_dim(win_pad, start, WINDOW + qb_len, axis=1)
        q_pos = start + jnp.arange(qb_len)
        w_pos = start - WINDOW + jnp.arange(WINDOW + qb_len)
        return nsa_attend(q_b, g_b, q_pos, cmp_k, cmp_v, sel_kb, sel_vb, w_b[:, :, 0], w_b[:, :, 1], w_pos)

    o = lax.map(one_block, jnp.arange(t // qb_len))
    o = o.transpose(1, 0, 2, 3).reshape(bsz, t, N_HEADS * HEAD_DIM)
    return o @ w_o


def nsa_sample(h, w_qg, w_o, side, win_full, past_len):
    cmp_k, cmp_v, sel_kb, sel_vb = side
    tq = h.shape[1]
    win_buf = win_full.shape[1] - tq
    q, gates = query_side(h, w_qg)
    q_pos = past_len + jnp.arange(tq)
    w_pos = past_len - win_buf + jnp.arange(win_buf + tq)
    o = nsa_attend(q, gates, q_pos, cmp_k, cmp_v, sel_kb, sel_vb, win_full[:, :, 0], win_full[:, :, 1], w_pos)
    return o @ w_o


def setup_inputs(seed: int = 0) -> dict:
    key = jax.random.key(seed)
    keys = jax.random.split(key, 40)
    cnt = [0]

    def nk():
        cnt[0] += 1
        return keys[cnt[0] - 1]

    def nrm(shape, scale):
        return jax.random.normal(nk(), shape, jnp.float32) * scale

    f32 = jnp.float32
    n_pages = PAST_LEN // PAGE_SIZE
    n_used = DEC_BATCH * n_pages
    n_pool = n_used + max(1, n_used // 4)
    win_buf = min(WINDOW, PAST_LEN)
    hd = N_HEADS * HEAD_DIM
    x_prompt = nrm((BATCH, SEQ, D_MODEL), 1.0)
    x_sample = nrm((DEC_BATCH, DEC_SEQ, D_MODEL), 1.0)
    state_ssm_re = nrm((N_A_LAYERS, DEC_BATCH, SSM_GROUPS, SSM_STATE), 0.5)
    state_ssm_im = nrm((N_A_LAYERS, DEC_BATCH, SSM_GROUPS, SSM_STATE), 0.5)
    state_ffn_conv = nrm((DEPTH, DEC_BATCH, CONV_W - 1, D_FF), 1.0)
    cache_kv = nrm((n_pool, PAGE_SIZE, 4, N_KV_HEADS, HEAD_DIM), 1.0)
    cache_win = nrm((DEC_BATCH, win_buf, 2, N_KV_HEADS, HEAD_DIM), 1.0)
    perm = jax.random.permutation(nk(), n_pool)
    page_table = perm[:n_used].reshape(DEC_BATCH, n_pages).astype(jnp.int32)
    n_idx = jnp.arange(SSM_STATE, dtype=f32)
    return {
        'x_prompt': x_prompt,
        'x_sample': x_sample,
        'state_ssm_re': state_ssm_re,
        'state_ssm_im': state_ssm_im,
        'state_ffn_conv': state_ffn_conv,
        'cache_kv': cache_kv,
        'cache_win': cache_win,
        'page_table': page_table,
        'attn_norm': 1.0 + nrm((DEPTH, D_MODEL), 0.01),
        'ffn_norm': 1.0 + nrm((DEPTH, D_MODEL), 0.01),
        'final_norm': 1.0 + nrm((D_MODEL,), 0.01),
        'ssm_lam_re': -0.5 * jnp.exp(nrm((N_A_LAYERS, SSM_GROUPS, SSM_STATE), 0.02)),
        'ssm_lam_im': jnp.pi * n_idx + nrm((N_A_LAYERS, SSM_GROUPS, SSM_STATE), 0.01),
        'ssm_log_step': jax.random.uniform(nk(), (N_A_LAYERS, SSM_GROUPS), f32, math.log(1e-3), math.log(1e-1)),
        'ssm_b_re': nrm((N_A_LAYERS, SSM_GROUPS, SSM_STATE, SSM_GROUP), (2 * SSM_GROUP) ** -0.5),
        'ssm_b_im': nrm((N_A_LAYERS, SSM_GROUPS, SSM_STATE, SSM_GROUP), (2 * SSM_GROUP) ** -0.5),
        'ssm_c_re': nrm((N_A_LAYERS, SSM_GROUPS, SSM_GROUP, SSM_STATE), SSM_STATE ** -0.5),
        'ssm_c_im': nrm((N_A_LAYERS, SSM_GROUPS, SSM_GROUP, SSM_STATE), SSM_STATE ** -0.5),
        'ssm_d': nrm((N_A_LAYERS, D_MODEL), 1.0),
        'ssm_w_glu': nrm((N_A_LAYERS, D_MODEL, 2 * D_MODEL), D_MODEL ** -0.5),
        'ffn_w_in': nrm((DEPTH, D_MODEL, 2 * D_FF), D_MODEL ** -0.5),
        'ffn_conv_w': nrm((DEPTH, CONV_W, D_FF), CONV_W ** -0.5),
        'ffn_conv_b': nrm((DEPTH, D_FF), 0.01),
        'ffn_w_down': nrm((DEPTH, D_FF, D_MODEL), D_FF ** -0.5),
        'kv_norm': 1.0 + nrm((D_MODEL,), 0.01),
        'w_kv': nrm((D_MODEL, N_BRANCH_KV * N_KV_HEADS * HEAD_DIM), D_MODEL ** -0.5),
        'cmp_w1': nrm((2, CMP_BLOCK * HEAD_DIM, CMP_HIDDEN), (CMP_BLOCK * HEAD_DIM) ** -0.5),
        'cmp_b1': nrm((2, CMP_HIDDEN), 0.01),
        'cmp_w2': nrm((2, CMP_HIDDEN, HEAD_DIM), CMP_HIDDEN ** -0.5),
        'cmp_b2': nrm((2, HEAD_DIM), 0.01),
        'cmp_pe': nrm((2, CMP_BLOCK, HEAD_DIM), 0.02),
        'w_qg': nrm((N_B_LAYERS, D_MODEL, hd + 3 * N_HEADS), D_MODEL ** -0.5),
        'w_o': nrm((N_B_LAYERS, hd, D_MODEL), hd ** -0.5),
    }


def reference(x_prompt, x_sample, state_ssm_re, state_ssm_im, state_ffn_conv, cache_kv, cache_win, page_table,
              attn_norm, ffn_norm, final_norm, ssm_lam_re, ssm_lam_im, ssm_log_step, ssm_b_re, ssm_b_im,
              ssm_c_re, ssm_c_im, ssm_d, ssm_w_glu, ffn_w_in, ffn_conv_w, ffn_conv_b, ffn_w_down,
              kv_norm, w_kv, cmp_w1, cmp_b1, cmp_w2, cmp_b2, cmp_pe, w_qg, w_o):
    f32 = jnp.float32
    bsz = x_prompt.shape[0]
    dec_batch = x_sample.shape[0]
    n_pages = page_table.shape[1]
    past_len = n_pages * PAGE_SIZE
    win_buf = cache_win.shape[1]
    xp, xs = x_prompt, x_sample
    ssm_re_p, ssm_im_p, ssm_re_s, ssm_im_s = [], [], [], []
    conv_p, conv_s = [], []
    for layer in range(DEPTH):
        hp = rmsnorm(xp, attn_norm[layer])
        hs = rmsnorm(xs, attn_norm[layer])
        if layer < N_A_LAYERS:
            a = layer
            prm = (ssm_lam_re[a], ssm_lam_im[a], ssm_log_step[a], ssm_b_re[a], ssm_b_im[a],
                   ssm_c_re[a], ssm_c_im[a], ssm_d[a], ssm_w_glu[a])
            h0p = jnp.zeros((bsz, SSM_GROUPS, SSM_STATE), jnp.complex64)
            h0s = lax.complex(state_ssm_re[a].astype(f32), state_ssm_im[a].astype(f32))
            o_p, hl_p = s5_mixer(hp, h0p, *prm)
            o_s, hl_s = s5_mixer(hs, h0s, *prm)
            ssm_re_p.append(jnp.real(hl_p))
            ssm_im_p.append(jnp.imag(hl_p))
            ssm_re_s.append(jnp.real(hl_s))
            ssm_im_s.append(jnp.imag(hl_s))
        else:
            if layer == N_A_LAYERS:
                kv_p = shared_kv(xp, kv_norm, w_kv)
                kv_s = shared_kv(xs, kv_norm, w_kv)
                kv_rows_p = kv_p[:, :, :4]
                kv_rows_s = kv_s[:, :, :4]
                win_rows_p = kv_p[:, :, 4:]
                past_rows = cache_kv[page_table].reshape(dec_batch, past_len, 4, N_KV_HEADS, HEAD_DIM)
                full_s = jnp.concatenate([past_rows.astype(kv_rows_s.dtype), kv_rows_s], axis=1)
                side_p = kv_side(kv_rows_p, cmp_w1, cmp_b1, cmp_w2, cmp_b2, cmp_pe)
                side_s = kv_side(full_s, cmp_w1, cmp_b1, cmp_w2, cmp_b2, cmp_pe)
                win_full_s = jnp.concatenate([cache_win.astype(kv_s.dtype), kv_s[:, :, 4:]], axis=1)
                win_new_p = win_rows_p[:, max(win_rows_p.shape[1] - WINDOW, 0):]
                win_new_s = win_full_s[:, -win_buf:]
            bl = layer - N_A_LAYERS
            o_p = nsa_prompt(hp, w_qg[bl], w_o[bl], side_p, win_rows_p)
            o_s = nsa_sample(hs, w_qg[bl], w_o[bl], side_s, win_full_s, past_len)
        xp = xp + o_p
        xs = xs + o_s
        fp, cp = conv_ffn(rmsnorm(xp, ffn_norm[layer]), jnp.zeros((bsz, CONV_W - 1, D_FF), xp.dtype),
                          ffn_w_in[layer], ffn_conv_w[layer], ffn_conv_b[layer], ffn_w_down[layer])
        fs, cs = conv_ffn(rmsnorm(xs, ffn_norm[layer]), state_ffn_conv[layer],
                          ffn_w_in[layer], ffn_conv_w[layer], ffn_conv_b[layer], ffn_w_down[layer])
        xp = xp + fp
        xs = xs + fs
        conv_p.append(cp)
        conv_s.append(cs)
    y_prompt = rmsnorm(xp, final_norm)
    y_sample = rmsnorm(xs, final_norm)
    return (y_prompt, y_sample,
            jnp.stack(ssm_re_p), jnp.stack(ssm_im_p), jnp.stack(ssm_re_s), jnp.stack(ssm_im_s),
            jnp.stack(conv_p), jnp.stack(conv_s),
            kv_rows_p, kv_rows_s, win_new_p, win_new_s)
```

```python
import math, contextlib
import numpy as np
import concourse.bass as bass
import concourse.mybir as mybir
from concourse.bass_utils import run_bass_kernel_spmd

F32 = mybir.dt.float32; BF16 = mybir.dt.bfloat16; I32 = mybir.dt.int32
ALU = mybir.AluOpType
AF = mybir.ActivationFunctionType
TWO_PI = 2.0 * math.pi


class T:
    __slots__ = ("w", "r")
    def __init__(self):
        self.w = []; self.r = []


class Eng:
    def __init__(self, h, sem, name):
        self.h = h; self.sem = sem; self.count = 0; self.seen = {}; self.name = name


class K:
    def __init__(self, n_dma_sems=16):
        self.nc = bass.Bass("TRN2", target_bir_lowering=False)
        nc = self.nc
        self.es = contextlib.ExitStack()
        self.E = {}
        for nm, h in (("pe", nc.tensor), ("act", nc.scalar), ("dve", nc.vector), ("pool", nc.gpsimd), ("sp", nc.sync)):
            s = self.es.enter_context(nc.semaphore("sem_" + nm))
            self.E[nm] = Eng(h, s, nm)
        self.dsem = {}
        for q in ("sp", "pool"):
            self.dsem[q] = [[self.es.enter_context(nc.semaphore(f"d_{q}_{i}")), 0] for i in range(n_dma_sems)]
        self.dnext = {"sp": 0, "pool": 0}
        self.uid = 0

    def sb(self, es, shape, dt, name=None):
        self.uid += 1
        return es.enter_context(self.nc.sbuf_tensor(name or f"sb{self.uid}", list(shape), dt))

    def ps(self, es, shape, dt=F32, name=None):
        self.uid += 1
        return es.enter_context(self.nc.psum_tensor(name or f"ps{self.uid}", list(shape), dt))

    def _waits(self, E, reads, writes):
        waits = {}
        for t in reads:
            for (s, v) in t.w:
                if waits.get(id(s), (None, 0))[1] < v: waits[id(s)] = (s, v)
        for t in writes:
            for (s, v) in t.w + t.r:
                if waits.get(id(s), (None, 0))[1] < v: waits[id(s)] = (s, v)
        for s, v in waits.values():
            if s is E.sem and v > E.count:
                continue
            if E.seen.get(id(s), 0) >= v: continue
            E.h.wait_ge(s, v); E.seen[id(s)] = v

    def op(self, eng, fn, reads=(), writes=(), inc=True):
        E = self.E[eng]
        self._waits(E, reads, writes)
        ins = fn(E.h)
        if inc:
            E.count += 1
            ins.then_inc(E.sem, 1)
            ev = (E.sem, E.count)
        else:
            ev = (E.sem, E.count + 1)
        for t in reads:
            t.r = [e for e in t.r if e[0] is not ev[0]] + [ev]
        for t in writes:
            t.w = [ev]; t.r = []
        return ins

    def dma(self, q, out, in_, reads=(), writes=(), **kw):
        E = self.E[q]
        self._waits(E, reads, writes)
        lst = self.dsem[q]; i = self.dnext[q]; self.dnext[q] = (i + 1) % len(lst)
        s, v = lst[i]
        if v > 0 and E.seen.get(id(s), 0) < v:
            E.h.wait_ge(s, v); E.seen[id(s)] = v
        lst[i][1] = v + 16
        E.h.dma_start(out=out, in_=in_, **kw).then_inc(s, 16)
        ev = (s, v + 16)
        for t in reads:
            t.r = [e for e in t.r if e[0] is not ev[0]] + [ev]
        for t in writes:
            t.w = [ev]; t.r = []
        return ev

    def idma(self, out, in_, idx_ap, reads=(), writes=()):
        q = "pool"; E = self.E[q]
        self._waits(E, reads, writes)
        lst = self.dsem[q]; i = self.dnext[q]; self.dnext[q] = (i + 1) % len(lst)
        s, v = lst[i]
        if v > 0 and E.seen.get(id(s), 0) < v:
            E.h.wait_ge(s, v); E.seen[id(s)] = v
        lst[i][1] = v + 16
        E.h.indirect_dma_start(out=out, out_offset=None, in_=in_, in_offset=bass.IndirectOffsetOnAxis(ap=idx_ap, axis=0)).then_inc(s, 16)
        ev = (s, v + 16)
        for t in reads:
            t.r = [e for e in t.r if e[0] is not ev[0]] + [ev]
        for t in writes:
            t.w = [ev]; t.r = []
        return ev

    def barrier(self):
        for nm, E in self.E.items():
            for q in self.dsem:
                for s, v in self.dsem[q]:
                    if v > 0 and E.seen.get(id(s), 0) < v:
                        E.h.wait_ge(s, v); E.seen[id(s)] = v
            for nm2, E2 in self.E.items():
                if E2 is E or E2.count == 0: continue
                if E.seen.get(id(E2.sem), 0) < E2.count:
                    E.h.wait_ge(E2.sem, E2.count); E.seen[id(E2.sem)] = E2.count

    def finish(self):
        self.barrier()
        self.es.close()


class Cfg:
    def __init__(self, D=4096, TP=2048, NS=4, DFF=11008, NKV=3072, NPOOL=1280):
        self.D = D; self.TP = TP; self.NS = NS; self.DFF = DFF; self.NKV = NKV
        self.KC = D // 128; self.G = D // 16; self.Q = self.G // 2; self.FC = DFF // 128
        self.NT = TP + NS
        self.NPOOL = NPOOL
        self.tiles = [(i * 512, min(512, TP - i * 512), 0) for i in range((TP + 511) // 512)] + [(TP, NS, 1)]


def build(cfg):
    k = K(); nc = k.nc
    D, TP, NS, DFF, NKV, KC, G, Q, FC, NT = cfg.D, cfg.TP, cfg.NS, cfg.DFF, cfg.NKV, cfg.KC, cfg.G, cfg.Q, cfg.FC, cfg.NT
    tiles = cfg.tiles
    WT = 513

    def din(name, shape, dt=F32):
        return nc.dram_tensor(name, list(shape), dt, kind="ExternalInput").ap()
    def dout(name, shape, dt=F32):
        return nc.dram_tensor(name, list(shape), dt, kind="ExternalOutput").ap()
    def dscr(name, shape, dt=F32):
        return nc.dram_tensor(name, list(shape), dt, kind="Internal").ap()

    xp = din("xp", [TP, D]); xs = din("xs", [NS, D])
    st_re = din("st_re", [Q, 128]); st_im = din("st_im", [Q, 128])
    st_conv = din("st_conv", [2, DFF]); st_conv1 = din("st_conv1", [2, DFF])
    cache_win = din("cache_win", [512, 1024])
    NPOOL = cfg.NPOOL
    cache_kv = din("cache_kv", [NPOOL * 128, 2048]); page_tab = din("page_tab", [1, 128], I32)
    attn_g = din("attn_g", [D]); ffn_g = din("ffn_g", [D]); kv_g = din("kv_g", [D])
    attn_g1 = din("attn_g1", [D]); ffn_g1 = din("ffn_g1", [D]); fin_g = din("fin_g", [D])
    w_in1 = din("w_in1", [D, 2 * DFF]); w_down1 = din("w_down1", [DFF, D])
    conv_w1 = din("conv_w1", [3, DFF]); conv_b1 = din("conv_b1", [DFF])
    w_qg = din("w_qg", [D, D + 96]); w_o = din("w_o", [D, D])
    cmp_w1 = din("cmp_w1", [2, 4096, 256]); cmp_b1 = din("cmp_b1", [2, 256]); cmp_w2 = din("cmp_w2", [2, 256, 128])
    cmp_b2 = din("cmp_b2", [2, 128]); cmp_pe = din("cmp_pe", [2, 32, 128])
    lam_re = din("lam_re", [Q, 128]); lam_im = din("lam_im", [Q, 128]); log_step = din("log_step", [Q, 2])
    b_re = din("b_re", [G, 64, 16]); b_im = din("b_im", [G, 64, 16])
    c_re = din("c_re", [G * 16, 64]); c_im = din("c_im", [G * 16, 64])
    ssm_d = din("ssm_d", [D])
    w_glu = din("w_glu", [D, 2 * D]); w_in = din("w_in", [D, 2 * DFF]); w_down = din("w_down", [DFF, D])
    conv_w = din("conv_w", [3, DFF]); conv_b = din("conv_b", [DFF])
    w_kv = din("w_kv", [D, NKV])

    o_ssm_re_p = dout("o_ssm_re_p", [Q, 128]); o_ssm_im_p = dout("o_ssm_im_p", [Q, 128])
    o_ssm_re_s = dout("o_ssm_re_s", [Q, 128]); o_ssm_im_s = dout("o_ssm_im_s", [Q, 128])
    o_conv_p = dout("o_conv_p", [2, DFF]); o_conv_s = dout("o_conv_s", [2, DFF])
    o_conv_p1 = dout("o_conv_p1", [2, DFF]); o_conv_s1 = dout("o_conv_s1", [2, DFF])
    o_y_p = dout("o_y_p", [TP, D]); o_y_s = dout("o_y_s", [NS, D])
    o_kv_p = dout("o_kv_p", [TP, 2048]); o_kv_s = dout("o_kv_s", [NS, 2048])
    o_win_p = dout("o_win_p", [512, 1024]); o_win_s = dout("o_win_s", [512, 1024])

    xT = dscr("xT", [D, NT])
    zT = dscr("zT", [D, NT], BF16)
    actT = dscr("actT", [DFF, NT], BF16)
    t_xT = T(); t_zT = T(); t_act = T()
    xTv = xT.rearrange("(c p) t -> p c t", p=128)
    zTv = zT.rearrange("(c p) t -> p c t", p=128)
    actTv = actT.rearrange("(c p) t -> p c t", p=128)

    glob = contextlib.ExitStack()
    ident = k.sb(glob, [128, 128], F32); t_ident = T()
    ones = k.sb(glob, [128, 128], F32); t_ones = T()
    iot = k.sb(glob, [128, WT], F32); t_iot = T()
    iot_i = k.sb(glob, [128, WT], I32)
    halfpi = k.sb(glob, [128, 1], F32)
    k.op("pool", lambda h: h.memset(ident[:], 0.0), writes=[t_ident])
    k.op("pool", lambda h: h.affine_select(out=ident[:], in_=ident[:], pattern=[[-1, 128]], compare_op=ALU.not_equal,
                                           fill=1.0, base=0, channel_multiplier=1), reads=[t_ident], writes=[t_ident])
    k.op("pool", lambda h: h.memset(ones[:], 1.0), writes=[t_ones])
    k.op("pool", lambda h: h.iota(iot_i[:], pattern=[[1, WT]], base=0, channel_multiplier=0), writes=[t_iot])
    k.op("dve", lambda h: h.tensor_copy(iot[:], iot_i[:]), reads=[t_iot], writes=[t_iot])
    k.op("dve", lambda h: h.memset(halfpi[:], math.pi / 2), writes=[t_iot])
    epsc = k.sb(glob, [128, 1], F32)
    k.op("dve", lambda h: h.memset(epsc[:], 1e-6), writes=[t_iot])
    msel = k.sb(glob, [128, 4, 128], F32); t_msel = T()
    k.op("dve", lambda h: h.memset(msel[:], 0.0), writes=[t_msel])
    for j in range(4):
        k.op("dve", lambda h: h.memset(msel[0:64, j, 32 * j:32 * j + 16], 1.0), reads=[t_msel], writes=[t_msel])
        k.op("dve", lambda h: h.memset(msel[64:128, j, 32 * j + 16:32 * j + 32], 1.0), reads=[t_msel], writes=[t_msel])
    gains = k.sb(glob, [128, 6, KC], F32); t_gains = T()
    dskip = k.sb(glob, [128, KC], F32)
    cwt = k.sb(glob, [128, 2, 4, FC], F32); t_cw = T()
    natv = k.sb(glob, [128, 128], F32); t_natv = T()
    pbank = [k.ps(glob, [128, 512]) for _ in range(8)]
    t_pb = [T() for _ in range(8)]
    def load_cols(dst, t_dst, src, n):
        k.dma("sp", natv[0:n, :], src.rearrange("(c p) -> c p", p=128), writes=[t_natv])
        k.op("pe", lambda h: h.transpose(pbank[0][:, 0:n], natv[0:n, :], ident[0:n, 0:n]), reads=[t_natv, t_ident], writes=[t_pb[0]])
        k.op("dve", lambda h: h.tensor_copy(dst, pbank[0][:, 0:n]), reads=[t_pb[0]], writes=[t_dst])
    for i, g in enumerate((attn_g, ffn_g, kv_g, attn_g1, ffn_g1, fin_g)):
        load_cols(gains[:, i, :], t_gains, g, KC)
    load_cols(dskip[:], t_gains, ssm_d, KC)
    stc = k.sb(glob, [128, 2, FC, 2], F32); t_stc = T()
    convo = k.sb(glob, [128, 2, 2, FC], F32); t_convo = T()
    for L_, (cw_, cb_, sc_) in enumerate(((conv_w, conv_b, st_conv), (conv_w1, conv_b1, st_conv1))):
        for i in range(3):
            load_cols(cwt[:, L_, i, :], t_cw, cw_[i], FC)
        load_cols(cwt[:, L_, 3, :], t_cw, cb_, FC)
        for t_ in range(2):
            load_cols(stc[:, L_, :, t_], t_stc, sc_[t_], FC)

    with contextlib.ExitStack() as es:
        xin = [k.sb(es, [128, D], F32) for _ in range(2)]; t_xin = [T(), T()]
        xo = [k.sb(es, [128, KC, 128], F32) for _ in range(2)]; t_xo = [T(), T()]
        blocks = [(xp, i * 128, min(128, TP - i * 128), i * 128) for i in range((TP + 127) // 128)] + [(xs, 0, NS, TP)]
        for bi, (src, r0, nr, c0) in enumerate(blocks):
            b = bi % 2
            k.dma("sp", xin[b][0:nr, :], src[r0:r0 + nr, :], writes=[t_xin[b]])
            for c4 in range(0, KC, 4):
                pb = (c4 // 4) % 2
                n4 = min(4, KC - c4)
                for c in range(n4):
                    k.op("pe", lambda h: h.transpose(pbank[pb][:, c * 128:c * 128 + nr], xin[b][0:nr, (c4 + c) * 128:(c4 + c + 1) * 128], ident[0:nr, 0:nr]),
                         reads=[t_xin[b], t_ident], writes=[t_pb[pb]], inc=(c == n4 - 1))
                k.op("act", lambda h: h.copy(xo[b][:, c4:c4 + n4, 0:nr], pbank[pb][:, 0:n4 * 128].rearrange("p (c t) -> p c t", t=128)[:, :, 0:nr]),
                     reads=[t_pb[pb]], writes=[t_xo[b]])
            k.dma("sp", xTv[:, :, c0:c0 + nr], xo[b][:, :, 0:nr], reads=[t_xo[b]], writes=[t_xT])
    k.barrier()

    def rmsnorm(hbuf, t_h, gi, es):
        xt = [k.sb(es, [128, KC, 128], F32) for _ in range(2)]; t_xt = [T(), T()]
        sq = k.sb(es, [128, 256], F32); t_sq = T()
        rstd = k.sb(es, [128, 256], F32); t_rstd = T()
        ntile = 0
        for (c0, w, _s) in tiles:
            for h0 in range(0, w, 128):
                hw = min(128, w - h0); cc = c0 + h0
                b = ntile % 2; ntile += 1
                k.dma("sp", xt[b][:, :, 0:hw], xTv[:, :, cc:cc + hw], reads=[t_xT], writes=[t_xt[b]])
                pb = 2 + b
                for c in range(KC):
                    k.op("act", lambda h: h.activation(sq[:, 0:hw], xt[b][:, c, 0:hw], AF.Square), reads=[t_xt[b]], writes=[t_sq])
                    k.op("pe", lambda h: h.matmul(pbank[pb][:, 0:hw], ones[:], sq[:, 0:hw], start=(c == 0), stop=(c == KC - 1)),
                         reads=[t_sq, t_ones], writes=[t_pb[pb]])
                k.op("act", lambda h: h.activation(rstd[:, 0:hw], pbank[pb][:, 0:hw], AF.Sqrt, scale=1.0 / D, bias=epsc[:, 0:1]),
                     reads=[t_pb[pb], t_iot], writes=[t_rstd])
                k.op("dve", lambda h: h.reciprocal(rstd[:, 0:hw], rstd[:, 0:hw]),
                     reads=[t_rstd], writes=[t_rstd])
                for c in range(KC):
                    k.op("dve", lambda h: h.scalar_tensor_tensor(hbuf[:, c, cc:cc + hw], xt[b][:, c, 0:hw], gains[:, gi, c:c + 1], rstd[:, 0:hw], ALU.mult, ALU.mult),
                         reads=[t_xt[b], t_rstd, t_gains], writes=[t_h])

    def dense_ws(es, W, col_lists, hbuf, t_h, epilogue):
        nw = len(col_lists[0])
        wb = [[k.sb(es, [128, KC, 128], BF16) for _ in range(nw)] for _ in range(2)]
        t_wb = [[T() for _ in range(nw)] for _ in range(2)]
        Wv = W.rearrange("(c p) m -> p c m", p=128)
        cnt = 0
        for idx, cols in enumerate(col_lists):
            b = idx % 2
            for wi, col in enumerate(cols):
                k.dma("pool", wb[b][wi][:], Wv[:, :, col * 128:(col + 1) * 128], writes=[t_wb[b][wi]])
            for ti, (c0, w, _s) in enumerate(tiles):
                pbs = []
                for wi in range(nw):
                    pb = 4 + (cnt % 4); cnt += 1
                    for c in range(KC):
                        k.op("pe", lambda h: h.matmul(pbank[pb][:, 0:w], wb[b][wi][:, c, :], hbuf[:, c, c0:c0 + w], start=(c == 0), stop=(c == KC - 1)),
                             reads=[t_wb[b][wi], t_h], writes=[t_pb[pb]], inc=(c == KC - 1))
                    pbs.append(pb)
                epilogue(idx, ti, pbs)

    with contextlib.ExitStack() as es:
        hbuf = k.sb(es, [128, KC, NT], BF16); t_h = T()
        with contextlib.ExitStack() as es2:
            rmsnorm(hbuf, t_h, 0, es2)
        k.barrier()
        with contextlib.ExitStack() as es2:
            nat = {n: k.sb(es2, [128, 128], F32) for n in ("lr", "li", "dt", "a", "th", "r", "fr", "sn", "cs", "lbr", "lbi", "den", "kr", "ki", "t1", "t2")}
            nat_i = k.sb(es2, [128, 128], I32)
            t_nat = T()
            k.dma("sp", nat["lr"][0:Q, :], lam_re[:, :], writes=[t_nat])
            k.dma("sp", nat["li"][0:Q, :], lam_im[:, :], writes=[t_nat])
            k.dma("sp", nat["t2"][0:Q, 0:2], log_step[:, :], writes=[t_nat])
            for hf in range(2):
                k.op("dve", lambda h: h.tensor_copy(nat["dt"][0:Q, hf * 64:(hf + 1) * 64], nat["t2"][0:Q, hf:hf + 1].to_broadcast([Q, 64])), reads=[t_nat], writes=[t_nat])
            R = [t_nat]
            def dv(fn): k.op("dve", fn, reads=R, writes=R)
            def ac(fn): k.op("act", fn, reads=R, writes=R)
            N = lambda n: nat[n][0:Q, :]
            ac(lambda h: h.activation(N("dt"), N("dt"), AF.Exp))
            dv(lambda h: h.tensor_tensor(N("a"), N("lr"), N("dt"), ALU.mult))
            dv(lambda h: h.tensor_tensor(N("th"), N("li"), N("dt"), ALU.mult))
            ac(lambda h: h.activation(N("r"), N("a"), AF.Exp))
            dv(lambda h: h.tensor_scalar(N("th"), N("th"), 1.0 / TWO_PI, None, ALU.mult))
            dv(lambda h: h.tensor_copy(nat_i[0:Q, :], N("th")))
            dv(lambda h: h.tensor_copy(N("t1"), nat_i[0:Q, :]))
            dv(lambda h: h.tensor_tensor(N("fr"), N("th"), N("t1"), ALU.subtract))
            ac(lambda h: h.activation(N("sn"), N("fr"), AF.Sin, scale=TWO_PI))
            ac(lambda h: h.activation(N("t1"), N("fr"), AF.Abs))
            ac(lambda h: h.activation(N("cs"), N("t1"), AF.Sin, scale=-TWO_PI, bias=halfpi[0:Q, :]))
            dv(lambda h: h.tensor_tensor(N("lbr"), N("r"), N("cs"), ALU.mult))
            dv(lambda h: h.tensor_scalar(N("lbr"), N("lbr"), -1.0, None, ALU.add))
            dv(lambda h: h.tensor_tensor(N("lbi"), N("r"), N("sn"), ALU.mult))
            dv(lambda h: h.tensor_tensor(N("den"), N("lr"), N("lr"), ALU.mult))
            dv(lambda h: h.tensor_tensor(N("t1"), N("li"), N("li"), ALU.mult))
            dv(lambda h: h.tensor_tensor(N("den"), N("den"), N("t1"), ALU.add))
            dv(lambda h: h.reciprocal(N("den"), N("den")))
            dv(lambda h: h.tensor_tensor(N("t1"), N("lbr"), N("lr"), ALU.mult))
            dv(lambda h: h.tensor_tensor(N("t2"), N("lbi"), N("li"), ALU.mult))
            dv(lambda h: h.tensor_tensor(N("kr"), N("t1"), N("t2"), ALU.add))
            dv(lambda h: h.tensor_tensor(N("kr"), N("kr"), N("den"), ALU.mult))
            dv(lambda h: h.tensor_tensor(N("t1"), N("lbi"), N("lr"), ALU.mult))
            dv(lambda h: h.tensor_tensor(N("t2"), N("lbr"), N("li"), ALU.mult))
            dv(lambda h: h.tensor_tensor(N("ki"), N("t1"), N("t2"), ALU.subtract))
            dv(lambda h: h.tensor_tensor(N("ki"), N("ki"), N("den"), ALU.mult))
            PL = {n: k.sb(es2, [128, 128], F32) for n in ("r", "fr", "kr", "ki", "sre", "sim")}
            t_PL = T()
            k.dma("sp", nat["t1"][0:Q, :], st_re[:, :], reads=R, writes=R)
            k.dma("sp", nat["t2"][0:Q, :], st_im[:, :], reads=R, writes=R)
            for n, srcn in (("r", "r"), ("fr", "fr"), ("kr", "kr"), ("ki", "ki"), ("sre", "t1"), ("sim", "t2")):
                k.op("pe", lambda h: h.transpose(pbank[0][:, 0:Q], nat[srcn][0:Q, :], ident[0:Q, 0:Q]), reads=R + [t_ident], writes=[t_pb[0]])
                k.op("dve", lambda h: h.tensor_copy(PL[n][:, 0:Q], pbank[0][:, 0:Q]), reads=[t_pb[0]], writes=[t_PL])
            carry = k.sb(es2, [128, 2, 2, 128], F32); t_carry = T()
            fin = k.sb(es2, [128, 2, 2, 128], F32); t_fin = T()
            k.op("dve", lambda h: h.memset(carry[:], 0.0), writes=[t_carry])
            k.op("dve", lambda h: h.memset(fin[:], 0.0), writes=[t_fin])
            tmpc = k.sb(es2, [128, 4], F32); t_tmpc = T()
            cosT = k.sb(es2, [128, WT], F32); sinT = k.sb(es2, [128, WT], F32); prT = k.sb(es2, [128, WT], F32); piT = k.sb(es2, [128, WT], F32)
            mT = k.sb(es2, [128, WT], F32); mI = k.sb(es2, [128, WT], I32); t_tab = T(); t_tabw = T()
            nb = [k.sb(es2, [128, 128], F32) for _ in range(2)]; t_nb = T()
            ncn = [k.sb(es2, [128, 128], F32) for _ in range(2)]; t_ncn = T()
            xc = [k.sb(es2, [128, 128], F32) for _ in range(2)]; t_xc = T()
            mb = k.sb(es2, [128, 128], F32); t_mb = T()
            lB = [k.sb(es2, [128, 128], BF16) for _ in range(2)]; t_lB = T()
            lC = [k.sb(es2, [128, 128], BF16) for _ in range(3)]; t_lC = T()
            wk = {n: k.sb(es2, [128, 512], F32) for n in ("a", "b", "c", "d", "tre", "tim", "gre", "gim")}
            t_wk = {n: T() for n in wk}
            gk = {n: k.sb(es2, [128, 512], BF16) for n in ("g1", "g2", "g3", "g4")}; t_gk = {n: T() for n in gk}
            yb = k.sb(es2, [128, 512], F32); t_yb = T()
            zb = [k.sb(es2, [128, 512], BF16) for _ in range(2)]; t_zb = [T(), T()]
            ntl = len(tiles)
            assert ntl <= 5
            for kc in range(KC):
                for ri, bsrc in enumerate((b_re, b_im)):
                    for hf in range(2):
                        k.dma("sp", nb[ri][hf * 64:(hf + 1) * 64, :].rearrange("p (g n) -> p g n", n=16),
                              bsrc[kc * 8:(kc + 1) * 8].rearrange("g p n -> p g n"), writes=[t_nb])
                for ri, csrc in enumerate((c_re, c_im)):
                    for hf in range(2):
                        k.dma("sp", ncn[ri][:, hf * 64:(hf + 1) * 64], csrc[kc * 128:(kc + 1) * 128, :], writes=[t_ncn])
                for ri in range(2):
                    k.op("pe", lambda h: h.transpose(pbank[0][:, 0:128], ncn[ri][:], ident[:]), reads=[t_ncn, t_ident], writes=[t_pb[0]])
                    k.op("dve", lambda h: h.tensor_copy(xc[ri][:], pbank[0][:, 0:128]), reads=[t_pb[0]], writes=[t_xc])
                for j in range(4):
                    q = kc * 4 + j
                    k.op("dve", lambda h: h.tensor_scalar(mT[:], iot[:], PL["fr"][:, q:q + 1], None, ALU.mult), reads=[t_iot, t_PL], writes=[t_tabw])
                    k.op("dve", lambda h: h.tensor_copy(mI[:], mT[:]), reads=[t_tabw], writes=[t_tabw])
                    k.op("dve", lambda h: h.tensor_copy(prT[:], mI[:]), reads=[t_tabw, t_tab], writes=[t_tab])
                    k.op("dve", lambda h: h.tensor_tensor(mT[:], mT[:], prT[:], ALU.subtract), reads=[t_tabw, t_tab], writes=[t_tabw])
                    k.op("act", lambda h: h.activation(sinT[:], mT[:], AF.Sin, scale=TWO_PI), reads=[t_tabw, t_tab], writes=[t_tab])
                    k.op("act", lambda h: h.activation(mT[:], mT[:], AF.Abs), reads=[t_tabw, t_tab], writes=[t_tabw])
                    k.op("act", lambda h: h.activation(cosT[:], mT[:], AF.Sin, scale=-TWO_PI, bias=halfpi[:]), reads=[t_tabw, t_tab], writes=[t_tab])
                    k.op("dve", lambda h: h.tensor_scalar(prT[:], sinT[:], PL["ki"][:, q:q + 1], None, ALU.mult), reads=[t_tab, t_PL], writes=[t_tab])
                    k.op("dve", lambda h: h.scalar_tensor_tensor(prT[:], cosT[:], PL["kr"][:, q:q + 1], prT[:], ALU.mult, ALU.add), reads=[t_tab, t_PL], writes=[t_tab])
                    k.op("dve", lambda h: h.tensor_scalar(piT[:], sinT[:], PL["kr"][:, q:q + 1], None, ALU.mult), reads=[t_tab, t_PL], writes=[t_tab])
                    k.op("dve", lambda h: h.scalar_tensor_tensor(piT[:], cosT[:], PL["ki"][:, q:q + 1], piT[:], ALU.mult, ALU.subtract), reads=[t_tab, t_PL], writes=[t_tab])
                    for ri in range(2):
                        k.op("dve", lambda h: h.tensor_tensor(mb[:], nb[ri][:], msel[:, j, :], ALU.mult), reads=[t_nb, t_msel], writes=[t_mb])
                        k.op("pe", lambda h: h.transpose(pbank[0][:, 0:128], mb[:], ident[:]), reads=[t_mb, t_ident], writes=[t_pb[0]])
                        k.op("dve", lambda h: h.tensor_copy(lB[ri][:], pbank[0][:, 0:128]), reads=[t_pb[0]], writes=[t_lB])
                    k.op("dve", lambda h: h.tensor_tensor(lC[0][:], xc[0][:], msel[:, j, :], ALU.mult), reads=[t_xc, t_msel], writes=[t_lC])
                    k.op("dve", lambda h: h.scalar_tensor_tensor(lC[1][:], xc[0][:], -1.0, msel[:, j, :], ALU.mult, ALU.mult), reads=[t_xc, t_msel], writes=[t_lC])
                    k.op("dve", lambda h: h.scalar_tensor_tensor(lC[2][:], xc[1][:], -1.0, msel[:, j, :], ALU.mult, ALU.mult), reads=[t_xc, t_msel], writes=[t_lC])
                    for ti, (c0, w, sq_) in enumerate(tiles):
                        first = (ti == 0) or (tiles[ti - 1][2] != sq_)
                        last = (ti == ntl - 1) or (tiles[ti + 1][2] != sq_)
                        if first and sq_ == 1:
                            k.op("dve", lambda h: h.tensor_tensor(tmpc[:, 0:1], PL["sim"][:, q:q + 1], sinT[:, 1:2], ALU.mult), reads=[t_PL, t_tab], writes=[t_tmpc])
                            k.op("dve", lambda h: h.scalar_tensor_tensor(carry[:, 1, 0, q:q + 1], PL["sre"][:, q:q + 1], cosT[:, 1:2], tmpc[:, 0:1], ALU.mult, ALU.subtract), reads=[t_PL, t_tab, t_tmpc], writes=[t_carry])
                            k.op("dve", lambda h: h.tensor_tensor(tmpc[:, 1:2], PL["sre"][:, q:q + 1], sinT[:, 1:2], ALU.mult), reads=[t_PL, t_tab], writes=[t_tmpc])
                            k.op("dve", lambda h: h.scalar_tensor_tensor(carry[:, 1, 1, q:q + 1], PL["sim"][:, q:q + 1], cosT[:, 1:2], tmpc[:, 1:2], ALU.mult, ALU.add), reads=[t_PL, t_tab, t_tmpc], writes=[t_carry])
                        for ri in range(2):
                            k.op("pe", lambda h: h.matmul(pbank[1 + ri][:, 0:w], lB[ri][:], hbuf[:, kc, c0:c0 + w], start=True, stop=True),
                                 reads=[t_lB, t_h], writes=[t_pb[1 + ri]])
                        W_ = slice(0, w)
                        k.op("dve", lambda h: h.tensor_tensor(wk["a"][:, W_], pbank[1][:, W_], prT[:, W_], ALU.mult), reads=[t_pb[1], t_tab], writes=[t_wk["a"]])
                        k.op("dve", lambda h: h.tensor_tensor(wk["b"][:, W_], pbank[2][:, W_], piT[:, W_], ALU.mult), reads=[t_pb[2], t_tab], writes=[t_wk["b"]])
                        k.op("dve", lambda h: h.tensor_tensor(wk["c"][:, W_], pbank[2][:, W_], prT[:, W_], ALU.mult), reads=[t_pb[2], t_tab], writes=[t_wk["c"]])
                        k.op("dve", lambda h: h.tensor_tensor(wk["d"][:, W_], pbank[1][:, W_], piT[:, W_], ALU.mult), reads=[t_pb[1], t_tab], writes=[t_wk["d"]])
                        k.op("pool", lambda h: h.tensor_tensor(wk["tre"][:, W_], wk["a"][:, W_], wk["b"][:, W_], ALU.subtract), reads=[t_wk["a"], t_wk["b"]], writes=[t_wk["tre"]])
                        k.op("pool", lambda h: h.tensor_tensor(wk["tim"][:, W_], wk["c"][:, W_], wk["d"][:, W_], ALU.add), reads=[t_wk["c"], t_wk["d"]], writes=[t_wk["tim"]])
                        rb = PL["r"][:, q:q + 1].to_broadcast([128, w])
                        k.op("dve", lambda h: h.tensor_tensor_scan(wk["gre"][:, W_], rb, wk["tre"][:, W_], carry[:, sq_, 0, q:q + 1], ALU.mult, ALU.add),
                             reads=[t_wk["tre"], t_PL, t_carry], writes=[t_wk["gre"]])
                        k.op("dve", lambda h: h.tensor_tensor_scan(wk["gim"][:, W_], rb, wk["tim"][:, W_], carry[:, sq_, 1, q:q + 1], ALU.mult, ALU.add),
                             reads=[t_wk["tim"], t_PL, t_carry], writes=[t_wk["gim"]])
                        gl_re = wk["gre"][:, w - 1:w]; gl_im = wk["gim"][:, w - 1:w]
                        RG = [t_wk["gre"], t_wk["gim"], t_tab]
                        if not last:
                            k.op("dve", lambda h: h.tensor_tensor(tmpc[:, 0:1], gl_im, sinT[:, w:w + 1], ALU.mult), reads=RG, writes=[t_tmpc])
                            k.op("dve", lambda h: h.scalar_tensor_tensor(carry[:, sq_, 0, q:q + 1], gl_re, cosT[:, w:w + 1], tmpc[:, 0:1], ALU.mult, ALU.subtract), reads=RG + [t_tmpc], writes=[t_carry])
                            k.op("dve", lambda h: h.tensor_tensor(tmpc[:, 1:2], gl_re, sinT[:, w:w + 1], ALU.mult), reads=RG, writes=[t_tmpc])
                            k.op("dve", lambda h: h.scalar_tensor_tensor(carry[:, sq_, 1, q:q + 1], gl_im, cosT[:, w:w + 1], tmpc[:, 1:2], ALU.mult, ALU.add), reads=RG + [t_tmpc], writes=[t_carry])
                        else:
                            k.op("dve", lambda h: h.tensor_tensor(tmpc[:, 2:3], gl_im, sinT[:, w - 1:w], ALU.mult), reads=RG, writes=[t_tmpc])
                            k.op("dve", lambda h: h.scalar_tensor_tensor(fin[:, sq_, 0, q:q + 1], gl_re, cosT[:, w - 1:w], tmpc[:, 2:3], ALU.mult, ALU.subtract), reads=RG + [t_tmpc], writes=[t_fin])
                            k.op("dve", lambda h: h.tensor_tensor(tmpc[:, 3:4], gl_re, sinT[:, w - 1:w], ALU.mult), reads=RG, writes=[t_tmpc])
                            k.op("dve", lambda h: h.scalar_tensor_tensor(fin[:, sq_, 1, q:q + 1], gl_im, cosT[:, w - 1:w], tmpc[:, 3:4], ALU.mult, ALU.add), reads=RG + [t_tmpc], writes=[t_fin])
                        k.op("pool", lambda h: h.tensor_tensor(gk["g1"][:, W_], wk["gre"][:, W_], cosT[:, W_], ALU.mult), reads=[t_wk["gre"], t_tab], writes=[t_gk["g1"]])
                        k.op("pool", lambda h: h.tensor_tensor(gk["g2"][:, W_], wk["gim"][:, W_], sinT[:, W_], ALU.mult), reads=[t_wk["gim"], t_tab], writes=[t_gk["g2"]])
                        k.op("pool", lambda h: h.tensor_tensor(gk["g3"][:, W_], wk["gim"][:, W_], cosT[:, W_], ALU.mult), reads=[t_wk["gim"], t_tab], writes=[t_gk["g3"]])
                        k.op("pool", lambda h: h.tensor_tensor(gk["g4"][:, W_], wk["gre"][:, W_], sinT[:, W_], ALU.mult), reads=[t_wk["gre"], t_tab], writes=[t_gk["g4"]])
                        yb_ = 3 + ti
                        for mi, (lc, gn) in enumerate(((0, "g1"), (1, "g2"), (2, "g3"), (2, "g4"))):
                            k.op("pe", lambda h: h.matmul(pbank[yb_][:, W_], lC[lc][:], gk[gn][:, W_], start=(j == 0 and mi == 0), stop=(j == 3 and mi == 3)),
                                 reads=[t_lC, t_gk[gn]], writes=[t_pb[yb_]])
                for ti, (c0, w, sq_) in enumerate(tiles):
                    W_ = slice(0, w); yb_ = 3 + ti; zi = ti % 2
                    k.op("dve", lambda h: h.scalar_tensor_tensor(yb[:, W_], hbuf[:, kc, c0:c0 + w], dskip[:, kc:kc + 1], pbank[yb_][:, W_], ALU.mult, ALU.add),
                         reads=[t_h, t_gains, t_pb[yb_]], writes=[t_yb])
                    k.op("pool", lambda h: h.tensor_tensor(wk["a"][:, W_], yb[:, W_], yb[:, W_], ALU.mult), reads=[t_yb], writes=[t_wk["a"]])
                    k.op("pool", lambda h: h.tensor_scalar(wk["a"][:, W_], wk["a"][:, W_], 0.044715, 1.0, ALU.mult, ALU.add), reads=[t_wk["a"]], writes=[t_wk["a"]])
                    k.op("pool", lambda h: h.tensor_tensor(wk["a"][:, W_], wk["a"][:, W_], yb[:, W_], ALU.mult), reads=[t_wk["a"], t_yb], writes=[t_wk["a"]])
                    k.op("act", lambda h: h.activation(wk["b"][:, W_], wk["a"][:, W_], AF.Sigmoid, scale=2.0 * math.sqrt(2.0 / math.pi)), reads=[t_wk["a"]], writes=[t_wk["b"]])
                    k.op("pool", lambda h: h.tensor_tensor(zb[zi][:, W_], wk["b"][:, W_], yb[:, W_], ALU.mult), reads=[t_wk["b"], t_yb], writes=[t_zb[zi]])
                    k.dma("sp", zTv[:, kc, c0:c0 + w], zb[zi][:, W_], reads=[t_zb[zi]], writes=[t_zT])
            for sq_, (ore, oim) in enumerate(((o_ssm_re_p, o_ssm_im_p), (o_ssm_re_s, o_ssm_im_s))):
                for ri, od in enumerate((ore, oim)):
                    k.op("pe", lambda h: h.transpose(pbank[0][0:Q, 0:128], fin[:, sq_, ri, 0:Q], ident[:]), reads=[t_fin, t_ident], writes=[t_pb[0]])
                    k.op("dve", lambda h: h.tensor_copy(nat["t1"][0:Q, :], pbank[0][0:Q, 0:128]), reads=[t_pb[0]] + R, writes=R)
                    k.dma("sp", od[:, :], nat["t1"][0:Q, :], reads=R, writes=[])
        k.barrier()

        k.dma("sp", hbuf[:], zTv[:, :, :], reads=[t_zT], writes=[t_h])
        with contextlib.ExitStack() as es2:
            xc_ = [k.sb(es2, [128, 512], F32) for _ in range(2)]; t_xc_ = [T(), T()]
            sg = k.sb(es2, [128, 512], F32); t_sg = T()
            cnt = [0]
            def glu_epi(idx, ti, pbs):
                c0, w, _s = tiles[ti]; b = cnt[0] % 2; cnt[0] += 1
                k.dma("sp", xc_[b][:, 0:w], xTv[:, idx, c0:c0 + w], reads=[t_xT], writes=[t_xc_[b]])
                k.op("act", lambda h: h.activation(sg[:, 0:w], pbank[pbs[1]][:, 0:w], AF.Sigmoid), reads=[t_pb[pbs[1]]], writes=[t_sg])
                k.op("dve", lambda h: h.tensor_tensor(sg[:, 0:w], sg[:, 0:w], pbank[pbs[0]][:, 0:w], ALU.mult), reads=[t_sg, t_pb[pbs[0]]], writes=[t_sg])
                k.op("dve", lambda h: h.tensor_tensor(xc_[b][:, 0:w], xc_[b][:, 0:w], sg[:, 0:w], ALU.add), reads=[t_sg, t_xc_[b]], writes=[t_xc_[b]])
                k.dma("sp", xTv[:, idx, c0:c0 + w], xc_[b][:, 0:w], reads=[t_xc_[b]], writes=[t_xT])
            dense_ws(es2, w_glu, [[m, KC + m] for m in range(KC)], hbuf, t_h, glu_epi)
        k.barrier()

    def ffn_layer(L):
        gi = (1, 4)[L]
        w_in_L, w_down_L = (w_in, w_in1)[L], (w_down, w_down1)[L]
        o_cp, o_cs = ((o_conv_p, o_conv_s), (o_conv_p1, o_conv_s1))[L]
        with contextlib.ExitStack() as es:
            hbuf = k.sb(es, [128, KC, NT], BF16); t_h = T()
            with contextlib.ExitStack() as es2:
                rmsnorm(hbuf, t_h, gi, es2)
            k.barrier()
            with contextlib.ExitStack() as es2:
                gb = k.sb(es2, [128, 514], F32); t_gb = T()
                gc = k.sb(es2, [128, 512], F32); t_gc = T()
                ab_ = [k.sb(es2, [128, 512], BF16) for _ in range(2)]; t_ab = [T(), T()]
                halo = k.sb(es2, [128, 2, 2], F32); t_halo = T()
                cnt = [0]
                def up_epi(f, ti, pbs):
                    c0, w, sq_ = tiles[ti]; b = cnt[0] % 2; cnt[0] += 1
                    first = (ti == 0) or (tiles[ti - 1][2] != sq_)
                    last = (ti == len(tiles) - 1) or (tiles[ti + 1][2] != sq_)
                    if first:
                        if sq_ == 0:
                            k.op("dve", lambda h: h.memset(gb[:, 0:2], 0.0), writes=[t_gb])
                        else:
                            k.op("dve", lambda h: h.tensor_copy(gb[:, 0:2], stc[:, L, f, :]), reads=[t_stc], writes=[t_gb])
                    else:
                        k.op("dve", lambda h: h.tensor_copy(gb[:, 0:2], halo[:, sq_, :]), reads=[t_halo], writes=[t_gb])
                    k.op("act", lambda h: h.copy(gb[:, 2:2 + w], pbank[pbs[1]][:, 0:w]), reads=[t_pb[pbs[1]]], writes=[t_gb])
                    k.op("dve", lambda h: h.tensor_copy(halo[:, sq_, :], gb[:, w:w + 2]), reads=[t_gb], writes=[t_halo])
                    if last:
                        k.op("dve", lambda h: h.tensor_copy(convo[:, sq_, :, f], gb[:, w:w + 2]), reads=[t_gb], writes=[t_convo])
                    k.op("dve", lambda h: h.tensor_scalar(gc[:, 0:w], gb[:, 2:2 + w], cwt[:, L, 2, f:f + 1], cwt[:, L, 3, f:f + 1], ALU.mult, ALU.add), reads=[t_gb, t_cw], writes=[t_gc])
                    k.op("dve", lambda h: h.scalar_tensor_tensor(gc[:, 0:w], gb[:, 1:1 + w], cwt[:, L, 1, f:f + 1], gc[:, 0:w], ALU.mult, ALU.add), reads=[t_gb, t_cw, t_gc], writes=[t_gc])
                    k.op("dve", lambda h: h.scalar_tensor_tensor(gc[:, 0:w], gb[:, 0:w], cwt[:, L, 0, f:f + 1], gc[:, 0:w], ALU.mult, ALU.add), reads=[t_gb, t_cw, t_gc], writes=[t_gc])
                    k.op("act", lambda h: h.activation(gc[:, 0:w], gc[:, 0:w], AF.Silu), reads=[t_gc], writes=[t_gc])
                    k.op("dve", lambda h: h.tensor_tensor(ab_[b][:, 0:w], gc[:, 0:w], pbank[pbs[0]][:, 0:w], ALU.mult), reads=[t_gc, t_pb[pbs[0]]], writes=[t_ab[b]])
                    k.dma("sp", actTv[:, f, c0:c0 + w], ab_[b][:, 0:w], reads=[t_ab[b]], writes=[t_act])
                dense_ws(es2, w_in_L, [[f, FC + f] for f in range(FC)], hbuf, t_h, up_epi)
                for sq_, od in enumerate((o_cp, o_cs)):
                    for t_ in range(2):
                        k.op("pe", lambda h: h.transpose(pbank[0][0:FC, 0:128], convo[:, sq_, t_, :], ident[:]), reads=[t_convo, t_ident], writes=[t_pb[0]])
                        k.op("dve", lambda h: h.tensor_copy(natv[0:FC, :], pbank[0][0:FC, 0:128]), reads=[t_pb[0]], writes=[t_natv])
                        k.dma("sp", od[t_].rearrange("(c p) -> c p", p=128), natv[0:FC, :], reads=[t_natv], writes=[])
        k.barrier()
        with contextlib.ExitStack() as es:
            at = k.sb(es, [128, FC, 512], BF16); t_at = T()
            wd = [k.sb(es, [128, FC, 128], BF16) for _ in range(2)]; t_wd = [T(), T()]
            xc_ = [k.sb(es, [128, 512], F32) for _ in range(2)]; t_xc_ = [T(), T()]
            Wd = w_down_L.rearrange("(c p) m -> p c m", p=128)
            cnt = 0
            for ti, (c0, w, sq_) in enumerate(tiles):
                k.dma("sp", at[:, :, 0:w], actTv[:, :, c0:c0 + w], reads=[t_act], writes=[t_at])
                for m in range(KC):
                    b = cnt % 2; cnt += 1
                    k.dma("pool", wd[b][:], Wd[:, :, m * 128:(m + 1) * 128], writes=[t_wd[b]])
                    k.dma("sp", xc_[b][:, 0:w], xTv[:, m, c0:c0 + w], reads=[t_xT], writes=[t_xc_[b]])
                    pb = 4 + b
                    for f in range(FC):
                        k.op("pe", lambda h: h.matmul(pbank[pb][:, 0:w], wd[b][:, f, :], at[:, f, 0:w], start=(f == 0), stop=(f == FC - 1)),
                             reads=[t_wd[b], t_at], writes=[t_pb[pb]], inc=(f == FC - 1))
                    k.op("dve", lambda h: h.tensor_tensor(xc_[b][:, 0:w], xc_[b][:, 0:w], pbank[pb][:, 0:w], ALU.add), reads=[t_pb[pb], t_xc_[b]], writes=[t_xc_[b]])
                    k.dma("sp", xTv[:, m, c0:c0 + w], xc_[b][:, 0:w], reads=[t_xc_[b]], writes=[t_xT])
        k.barrier()

    ffn_layer(0)

    KT = dscr("KT", [4, 4, 128, TP], BF16); t_KT = T()
    Vtok = dscr("Vtok", [2, TP, 512], BF16); t_Vtok = T()
    kvS = dscr("kvS", [NS, NKV], F32); t_kvS = T()
    ktmap = {0: 0, 1: 1, 2: 2, 4: 3}; vmap = {3: 0, 5: 1}
    with contextlib.ExitStack() as es:
        hbuf = k.sb(es, [128, KC, NT], BF16); t_h = T()
        with contextlib.ExitStack() as es2:
            rmsnorm(hbuf, t_h, 2, es2)
        k.barrier()
        wkb = [k.sb(es, [128, KC, 256], BF16) for _ in range(2)]; t_wkb = [T(), T()]
        ob = [k.sb(es, [128, 256], F32) for _ in range(2)]; t_ob = [T(), T()]
        ktb = [k.sb(es, [128, 2, 128], BF16) for _ in range(2)]; t_ktb = [T(), T()]
        vb = [k.sb(es, [128, 256], BF16) for _ in range(2)]; t_vb = [T(), T()]
        Wk = w_kv.rearrange("(c p) m -> p c m", p=128)
        blocks = [(i * 128, min(128, TP - i * 128), 0) for i in range((TP + 127) // 128)] + [(TP, NS, 1)]
        cnt = 0
        k.dma("sp", o_win_s[0:512 - NS, :], cache_win[NS:512, :])
        for nb_ in range(NKV // 256):
            b = nb_ % 2; br = nb_ // 2; kp = nb_ % 2
            k.dma("pool", wkb[b][:], Wk[:, :, nb_ * 256:(nb_ + 1) * 256], writes=[t_wkb[b]])
            for (c0, nr, sq_) in blocks:
                ob_i = cnt % 2; pb = 4 + ob_i; cnt += 1
                for c in range(KC):
                    k.op("pe", lambda h: h.matmul(pbank[pb][0:nr, 0:256], hbuf[:, c, c0:c0 + nr], wkb[b][:, c, :], start=(c == 0), stop=(c == KC - 1)),
                         reads=[t_wkb[b], t_h], writes=[t_pb[pb]], inc=(c == KC - 1))
                k.op("act", lambda h: h.copy(ob[ob_i][0:nr, :], pbank[pb][0:nr, 0:256]), reads=[t_pb[pb]], writes=[t_ob[ob_i]])
                if sq_ == 1:
                    k.dma("sp", kvS[0:nr, nb_ * 256:(nb_ + 1) * 256], ob[ob_i][0:nr, :], reads=[t_ob[ob_i]], writes=[t_kvS])
                elif br in ktmap:
                    for i2 in range(2):
                        k.op("pe", lambda h: h.transpose(pbank[2 + ob_i][:, i2 * 128:i2 * 128 + nr], ob[ob_i][0:nr, i2 * 128:(i2 + 1) * 128], ident[0:nr, 0:nr]),
                             reads=[t_ob[ob_i], t_ident], writes=[t_pb[2 + ob_i]], inc=(i2 == 1))
                    k.op("dve", lambda h: h.tensor_copy(ktb[ob_i][:, :, 0:nr], pbank[2 + ob_i][:, 0:256].rearrange("p (a t) -> p a t", t=128)[:, :, 0:nr]),
                         reads=[t_pb[2 + ob_i]], writes=[t_ktb[ob_i]])
                    k.dma("sp", KT[ktmap[br], 2 * kp:2 * kp + 2, :, c0:c0 + nr].rearrange("a p t -> p a t"), ktb[ob_i][:, :, 0:nr], reads=[t_ktb[ob_i]], writes=[t_KT])
                else:
                    k.op("dve", lambda h: h.tensor_copy(vb[ob_i][0:nr, :], ob[ob_i][0:nr, :]), reads=[t_ob[ob_i]], writes=[t_vb[ob_i]])
                    k.dma("sp", Vtok[vmap[br], c0:c0 + nr, kp * 256:(kp + 1) * 256], vb[ob_i][0:nr, :], reads=[t_vb[ob_i]], writes=[t_Vtok])
                if nb_ < 8:
                    od = o_kv_p[c0:c0 + nr, nb_ * 256:(nb_ + 1) * 256] if sq_ == 0 else o_kv_s[0:nr, nb_ * 256:(nb_ + 1) * 256]
                    k.dma("sp", od, ob[ob_i][0:nr, :], reads=[t_ob[ob_i]])
                else:
                    wc = (nb_ - 8) * 256
                    if sq_ == 0:
                        lo = max(c0, TP - 512)
                        if c0 + nr > lo:
                            k.dma("sp", o_win_p[lo - (TP - 512):c0 + nr - (TP - 512), wc:wc + 256], ob[ob_i][lo - c0:nr, :], reads=[t_ob[ob_i]])
                    else:
                        k.dma("sp", o_win_s[512 - NS:512, wc:wc + 256], ob[ob_i][0:nr, :], reads=[t_ob[ob_i]])
    k.barrier()

    SCALE = 128 ** -0.5
    NEG = 30000.0
    BIG = 1.0e30
    NQB = TP // 128; NC_ = TP // 16 - 1; NB = TP // 64
    AXX = mybir.AxisListType.X
    QT = dscr("QT", [32, 128, NT], BF16); t_QT = T()
    OT = dscr("OT", [D, NT], BF16); t_OT = T()
    OTv = OT.rearrange("(c p) t -> p c t", p=128)
    gat = k.sb(glob, [128, NQB + 1, 96], F32); t_gat = T()
    ident_bf = k.sb(glob, [128, 128], BF16)
    k.op("dve", lambda h: h.tensor_copy(ident_bf[:], ident[:]), reads=[t_ident], writes=[t_ident])
    pT7 = pbank[7][:, :].bitcast(BF16)

    with contextlib.ExitStack() as es:
        hbuf = k.sb(es, [128, KC, NT], BF16); t_h = T()
        with contextlib.ExitStack() as es2:
            rmsnorm(hbuf, t_h, 3, es2)
        k.barrier()
        wg = k.sb(es, [128, KC, 96], BF16); t_wg = T()
        k.dma("pool", wg[:], w_qg.rearrange("(c p) m -> p c m", p=128)[:, :, D:D + 96], writes=[t_wg])
        blocks = [(i * 128, min(128, TP - i * 128)) for i in range(NQB)] + [(TP, NS)]
        for bi, (c0, nr) in enumerate(blocks):
            pb = 2 + bi % 2
            for c in range(KC):
                k.op("pe", lambda h: h.matmul(pbank[pb][0:nr, 0:96], hbuf[:, c, c0:c0 + nr], wg[:, c, :], start=(c == 0), stop=(c == KC - 1)),
                     reads=[t_wg, t_h], writes=[t_pb[pb]], inc=(c == KC - 1))
            k.op("act", lambda h: h.activation(gat[0:nr, bi, :], pbank[pb][0:nr, 0:96], AF.Sigmoid), reads=[t_pb[pb]], writes=[t_gat])
        qb_ = [k.sb(es, [128, 512], BF16) for _ in range(2)]; t_qb = [T(), T()]
        cnt = [0]
        def q_epi(idx, ti, pbs):
            c0, w, _s = tiles[ti]; b = cnt[0] % 2; cnt[0] += 1
            k.op("act", lambda h: h.copy(qb_[b][:, 0:w], pbank[pbs[0]][:, 0:w]), reads=[t_pb[pbs[0]]], writes=[t_qb[b]])
            k.dma("sp", QT[idx, :, c0:c0 + w], qb_[b][:, 0:w], reads=[t_qb[b]], writes=[t_QT])
        dense_ws(es, w_qg, [[m] for m in range(D // 128)], hbuf, t_h, q_epi)
    k.barrier()

    def gelu_to(dst, src, tmp_a, tmp_b, t_src, t_tmp, t_dst):
        k.op("pool", lambda h: h.tensor_tensor(tmp_a, src, src, ALU.mult), reads=[t_src], writes=[t_tmp])
        k.op("pool", lambda h: h.tensor_scalar(tmp_a, tmp_a, 0.044715, 1.0, ALU.mult, ALU.add), reads=[t_tmp], writes=[t_tmp])
        k.op("pool", lambda h: h.tensor_tensor(tmp_a, tmp_a, src, ALU.mult), reads=[t_tmp, t_src], writes=[t_tmp])
        k.op("act", lambda h: h.activation(tmp_b, tmp_a, AF.Sigmoid, scale=2.0 * math.sqrt(2.0 / math.pi)), reads=[t_tmp], writes=[t_tmp])
        k.op("pool", lambda h: h.tensor_tensor(dst, tmp_b, src, ALU.mult), reads=[t_tmp, t_src], writes=[t_dst])

    with contextlib.ExitStack() as es:
        cmpKT = k.sb(es, [128, 4, 128], BF16); t_cK = T()
        cmpV = k.sb(es, [128, 4, 128], BF16); t_cV = T()
        cmpKT_s = k.sb(es, [128, 4, 1024], BF16); cmpV_s = k.sb(es, [128, 8, 4, 128], BF16); t_cKs = T()
        idx = k.sb(es, [128, 128], I32); t_idx = T()
        st = k.sb(es, [128, 8], F32); t_st = T()
        with contextlib.ExitStack() as es2:
            w1b = [k.sb(es2, [128, 32, 256], BF16) for _ in range(2)]; t_w1 = T()
            w2b = [k.sb(es2, [128, 2, 128], BF16) for _ in range(2)]; t_w2 = T()
            b1c = k.sb(es2, [128, 2, 2], F32); b2c = k.sb(es2, [128, 2], F32); t_bc = T()
            b2r = k.sb(es2, [1, 2, 128], F32)
            peT = k.sb(es2, [128, 2, 32], BF16); t_pe = T()
            cb = k.sb(es2, [128, 2, 2], F32); t_cb = T()
            for b in range(2):
                k.dma("pool", w1b[b][:], cmp_w1[b].rearrange("(t p) m -> p t m", p=128), writes=[t_w1])
                k.dma("pool", w2b[b][:], cmp_w2[b].rearrange("(c p) m -> p c m", p=128), writes=[t_w2])
                load_cols(b1c[:, b, :], t_bc, cmp_b1[b], 2)
                load_cols(b2c[:, b:b + 1], t_bc, cmp_b2[b], 1)
                k.dma("sp", b2r[0:1, b, :], cmp_b2[b:b + 1, :], writes=[t_bc])
                k.dma("sp", natv[0:32, :], cmp_pe[b], writes=[t_natv])
                k.op("pe", lambda h: h.transpose(pbank[0][:, 0:32], natv[0:32, :], ident[0:32, 0:32]), reads=[t_natv, t_ident], writes=[t_pb[0]])
                k.op("dve", lambda h: h.tensor_copy(peT[:, b, :], pbank[0][:, 0:32]), reads=[t_pb[0]], writes=[t_pe])
            for b in range(2):
                for hc in range(2):
                    for t_ in range(32):
                        k.op("pe", lambda h: h.matmul(pbank[0][:, 0:1], w1b[b][:, t_, hc * 128:(hc + 1) * 128], peT[:, b, t_:t_ + 1], start=(t_ == 0), stop=(t_ == 31)),
                             reads=[t_w1, t_pe], writes=[t_pb[0]], inc=(t_ == 31))
                    k.op("dve", lambda h: h.tensor_tensor(cb[:, b, hc:hc + 1], pbank[0][:, 0:1], b1c[:, b, hc:hc + 1], ALU.add), reads=[t_pb[0], t_bc], writes=[t_cb])
            xTt = k.sb(es2, [128, TP], BF16); t_xTt = T()
            pre = k.sb(es2, [128, 128], F32); t_pre = T()
            ta = k.sb(es2, [128, 128], F32); tb = k.sb(es2, [128, 128], F32); t_tt = T()
            hid = k.sb(es2, [128, 2, 128], BF16); t_hid = T()

            def compress(b, rhs_fn, t_src, nblk, dstK, dstV, t_dst):
                for hc in range(2):
                    for t_ in range(32):
                        k.op("pe", lambda h: h.matmul(pbank[1 + hc][:, 0:nblk], w1b[b][:, t_, hc * 128:(hc + 1) * 128], rhs_fn(t_), start=(t_ == 0), stop=(t_ == 31)),
                             reads=[t_w1, t_src], writes=[t_pb[1 + hc]], inc=(t_ == 31))
                    k.op("dve", lambda h: h.tensor_scalar(pre[:, 0:nblk], pbank[1 + hc][:, 0:nblk], cb[:, b, hc:hc + 1], None, ALU.add), reads=[t_pb[1 + hc], t_cb], writes=[t_pre])
                    gelu_to(hid[:, hc, 0:nblk], pre[:, 0:nblk], ta[:, 0:nblk], tb[:, 0:nblk], t_pre, t_tt, t_hid)
                if b == 0:
                    for hc in range(2):
                        k.op("pe", lambda h: h.matmul(pbank[3][:, 0:nblk], w2b[0][:, hc, :], hid[:, hc, 0:nblk], start=(hc == 0), stop=(hc == 1)),
                             reads=[t_w2, t_hid], writes=[t_pb[3]], inc=(hc == 1))
                    k.op("dve", lambda h: h.tensor_scalar(dstK, pbank[3][:, 0:nblk], b2c[:, 0:1], None, ALU.add), reads=[t_pb[3], t_bc], writes=[t_dst])
                else:
                    for hc in range(2):
                        k.op("pe", lambda h: h.matmul(pbank[3][0:nblk, 0:128], hid[:, hc, 0:nblk], w2b[1][:, hc, :], start=(hc == 0), stop=False),
                             reads=[t_w2, t_hid], writes=[t_pb[3]], inc=False)
                    k.op("pe", lambda h: h.matmul(pbank[3][0:nblk, 0:128], ones[0:1, 0:nblk], b2r[0:1, 1, :], start=False, stop=True),
                         reads=[t_ones, t_bc], writes=[t_pb[3]])
                    k.op("dve", lambda h: h.tensor_copy(dstV, pbank[3][0:nblk, 0:128]), reads=[t_pb[3]], writes=[t_dst])

            for b in range(2):
                for kvh in range(4):
                    k.dma("sp", xTt[:], KT[b, kvh], reads=[t_KT], writes=[t_xTt])
                    compress(b, lambda t_: xTt[:, t_:t_ + 16 * (NC_ - 1) + 1:16], t_xTt, NC_,
                             cmpKT[:, kvh, 0:NC_], cmpV[0:NC_, kvh, :], t_cK if b == 0 else t_cV)

            pt_i = k.sb(es2, [128, 128], I32); ptf = k.sb(es2, [128, 128], F32); pidxf = k.sb(es2, [128, 1], F32); pidxi = k.sb(es2, [128, 1], I32)
            k.dma("sp", pt_i[:], page_tab[0:1, :].to_broadcast([128, 128]), writes=[t_idx])
            k.op("pool", lambda h: h.iota(pidxi[:], pattern=[[0, 1]], base=0, channel_multiplier=1), writes=[t_idx])
            k.op("dve", lambda h: h.tensor_copy(pidxf[:], pidxi[:]), reads=[t_idx], writes=[t_idx])
            k.op("dve", lambda h: h.tensor_copy(ptf[:], pt_i[:]), reads=[t_idx], writes=[t_idx])
            k.op("dve", lambda h: h.tensor_scalar(ptf[:], ptf[:], 128.0, pidxf[:, 0:1], ALU.mult, ALU.add), reads=[t_idx], writes=[t_idx])
            k.op("dve", lambda h: h.tensor_copy(idx[:], ptf[:]), reads=[t_idx], writes=[t_idx])
            pgb = [k.sb(es2, [128, 2048], F32) for _ in range(2)]; t_pgb = [T(), T()]
            XT = k.sb(es2, [128, 8, 2064], BF16); t_XT = T()
            k.op("dve", lambda h: h.memset(XT[:, :, 0:16], 0.0), writes=[t_XT])
            for gi in range(8):
                for pl in range(16):
                    pg_ = gi * 16 + pl; b2 = pg_ % 2
                    k.idma(pgb[b2][:], cache_kv, idx[:, pg_:pg_ + 1], reads=[t_idx], writes=[t_pgb[b2]])
                    for b4 in range(2):
                        pbx = 4 + b4
                        for c in range(4):
                            bk = b4 * 4 + c
                            k.op("pe", lambda h: h.transpose(pbank[pbx][:, c * 128:(c + 1) * 128], pgb[b2][:, bk * 128:(bk + 1) * 128], ident[:]),
                                 reads=[t_pgb[b2], t_ident], writes=[t_pb[pbx]], inc=(c == 3))
                        k.op("act" if b4 == 0 else "dve", lambda h: (h.copy if b4 == 0 else h.tensor_copy)(XT[:, b4 * 4:b4 * 4 + 4, 16 + pl * 128:16 + (pl + 1) * 128], pbank[pbx][:, 0:512].rearrange("p (a t) -> p a t", t=128)),
                             reads=[t_pb[pbx]], writes=[t_XT])
                nblk = 127 if gi == 0 else 128; j0 = 1 if gi == 0 else 0; cbase = 0 if gi == 0 else 128 * gi - 1
                for b in range(2):
                    for kvh in range(4):
                        bk = b * 4 + kvh
                        compress(b, lambda t_: XT[:, bk, t_ + 16 * j0:t_ + 16 * j0 + 16 * (nblk - 1) + 1:16], t_XT, nblk,
                                 cmpKT_s[:, kvh, cbase:cbase + nblk], cmpV_s[0:nblk, gi, kvh, :], t_cKs)
                k.op("dve", lambda h: h.tensor_copy(XT[:, :, 0:16], XT[:, :, 2048:2064]), reads=[t_XT], writes=[t_XT])
        k.barrier()

        esA = contextlib.ExitStack()
        mi = k.sb(esA, [128, NQB * 128], I32); t_mi = T()
        cm01 = k.sb(esA, [128, NQB, NC_], F32); cmadd = k.sb(esA, [128, NQB, NC_], F32); t_cm = T()
        k.op("pool", lambda h: h.iota(mi[:, 0:NQB * NC_].rearrange("p (a c) -> p a c", c=NC_), pattern=[[128, NQB], [-16, NC_]], base=-31, channel_multiplier=1), writes=[t_mi])
        k.op("dve", lambda h: h.tensor_copy(cm01[:], mi[:, 0:NQB * NC_].rearrange("p (a c) -> p a c", c=NC_)), reads=[t_mi], writes=[t_cm])
        k.op("dve", lambda h: h.tensor_single_scalar(cm01[:], cm01[:], 0.0, ALU.is_ge), reads=[t_cm], writes=[t_cm])
        k.op("dve", lambda h: h.tensor_scalar(cmadd[:], cm01[:], NEG, -NEG, ALU.mult, ALU.add), reads=[t_cm], writes=[t_cm])
        keep = k.sb(esA, [128, NQB, NB], F32); over = k.sb(esA, [128, NQB, NB], F32); t_ko = T()
        dT = k.sb(esA, [128, NQB, NB], F32); fT = k.sb(esA, [128, NQB, NB], F32); pidx = k.sb(esA, [128, 1], F32)
        V3 = lambda: mi[:, 0:NQB * NB].rearrange("p (a c) -> p a c", c=NB)
        k.op("pool", lambda h: h.iota(mi[:, 0:1], pattern=[[0, 1]], base=0, channel_multiplier=1), reads=[t_cm], writes=[t_mi])
        k.op("dve", lambda h: h.tensor_copy(pidx[:], mi[:, 0:1]), reads=[t_mi], writes=[t_ko])
        k.op("dve", lambda h: h.tensor_single_scalar(pidx[:], pidx[:], 64.0, ALU.is_ge), reads=[t_ko], writes=[t_ko])
        k.op("pool", lambda h: h.iota(V3(), pattern=[[-2, NQB], [1, NB]], base=0, channel_multiplier=0), reads=[t_ko], writes=[t_mi])
        k.op("dve", lambda h: h.tensor_copy(dT[:], V3()), reads=[t_mi], writes=[t_ko])
        k.op("dve", lambda h: h.tensor_scalar(dT[:], dT[:], pidx[:, 0:1], None, ALU.subtract), reads=[t_ko], writes=[t_ko])
        k.op("dve", lambda h: h.tensor_single_scalar(over[:], dT[:], 0.0, ALU.is_gt), reads=[t_ko], writes=[t_ko])
        k.op("dve", lambda h: h.tensor_single_scalar(keep[:], dT[:], 0.0, ALU.is_equal), reads=[t_ko], writes=[t_ko])
        k.op("dve", lambda h: h.tensor_single_scalar(fT[:], dT[:], -1.0, ALU.is_equal), reads=[t_ko], writes=[t_ko])
        k.op("dve", lambda h: h.tensor_tensor(keep[:], keep[:], fT[:], ALU.max), reads=[t_ko], writes=[t_ko])
        k.op("pool", lambda h: h.iota(V3(), pattern=[[0, NQB], [1, NB]], base=0, channel_multiplier=0), reads=[t_ko], writes=[t_mi])
        k.op("dve", lambda h: h.tensor_copy(fT[:], V3()), reads=[t_mi], writes=[t_ko])
        k.op("dve", lambda h: h.tensor_single_scalar(fT[:], fT[:], 0.0, ALU.is_equal), reads=[t_ko], writes=[t_ko])
        k.op("dve", lambda h: h.tensor_tensor(keep[:], keep[:], fT[:], ALU.max), reads=[t_ko], writes=[t_ko])
        k.op("dve", lambda h: h.tensor_tensor(fT[:], keep[:], over[:], ALU.subtract), reads=[t_ko], writes=[t_ko])
        k.op("dve", lambda h: h.tensor_tensor(keep[:], keep[:], over[:], ALU.add), reads=[t_ko], writes=[t_ko])
        k.op("dve", lambda h: h.tensor_scalar(keep[:], keep[:], -1.0, 1.0, ALU.mult, ALU.add), reads=[t_ko], writes=[t_ko])
        k.op("dve", lambda h: h.tensor_scalar(over[:], fT[:], BIG, None, ALU.mult), reads=[t_ko], writes=[t_ko])
        tri = k.sb(esA, [128, 128], F32); W640 = k.sb(esA, [128, 640], F32); wtmp = k.sb(esA, [128, 640], F32); t_tri = T()
        k.op("pool", lambda h: h.iota(mi[:, 0:128], pattern=[[-1, 128]], base=0, channel_multiplier=1), reads=[t_ko], writes=[t_mi])
        k.op("dve", lambda h: h.tensor_copy(tri[:], mi[:, 0:128]), reads=[t_mi], writes=[t_tri])
        k.op("dve", lambda h: h.tensor_single_scalar(tri[:], tri[:], 0.0, ALU.is_ge), reads=[t_tri], writes=[t_tri])
        k.op("dve", lambda h: h.tensor_scalar(tri[:], tri[:], NEG, -NEG, ALU.mult, ALU.add), reads=[t_tri], writes=[t_tri])
        k.op("pool", lambda h: h.iota(mi[:, 0:640], pattern=[[-1, 640]], base=512, channel_multiplier=1), reads=[t_tri], writes=[t_mi])
        k.op("dve", lambda h: h.tensor_copy(W640[:], mi[:, 0:640]), reads=[t_mi], writes=[t_tri])
        k.op("dve", lambda h: h.tensor_single_scalar(W640[:], W640[:], 0.0, ALU.is_ge), reads=[t_tri], writes=[t_tri])
        k.op("pool", lambda h: h.iota(mi[:, 0:640], pattern=[[1, 640]], base=-1, channel_multiplier=-1), reads=[t_tri], writes=[t_mi])
        k.op("dve", lambda h: h.tensor_copy(wtmp[:], mi[:, 0:640]), reads=[t_mi], writes=[t_tri])
        k.op("dve", lambda h: h.tensor_single_scalar(wtmp[:], wtmp[:], 0.0, ALU.is_ge), reads=[t_tri], writes=[t_tri])
        k.op("dve", lambda h: h.tensor_tensor(W640[:], W640[:], wtmp[:], ALU.mult), reads=[t_tri], writes=[t_tri])
        k.op("dve", lambda h: h.tensor_scalar(W640[:], W640[:], NEG, -NEG, ALU.mult, ALU.add), reads=[t_tri], writes=[t_tri])
        ovl = k.sb(esA, [128, NB], F32); ovt = k.sb(esA, [128, NB], F32); t_ovl = T()
        k.op("pool", lambda h: h.iota(mi[:, 0:NB], pattern=[[64, NB]], base=64, channel_multiplier=-16), reads=[t_tri], writes=[t_mi])
        k.op("dve", lambda h: h.tensor_copy(ovl[:], mi[:, 0:NB]), reads=[t_mi], writes=[t_ovl])
        k.op("dve", lambda h: h.tensor_single_scalar(ovl[:], ovl[:], 0.0, ALU.is_gt), reads=[t_ovl], writes=[t_ovl])
        k.op("pool", lambda h: h.iota(mi[:, 0:NB], pattern=[[-64, NB]], base=32, channel_multiplier=16), reads=[t_ovl], writes=[t_mi])
        k.op("dve", lambda h: h.tensor_copy(ovt[:], mi[:, 0:NB]), reads=[t_mi], writes=[t_ovl])
        k.op("dve", lambda h: h.tensor_single_scalar(ovt[:], ovt[:], 0.0, ALU.is_gt), reads=[t_ovl], writes=[t_ovl])
        k.op("dve", lambda h: h.tensor_tensor(ovl[:], ovl[:], ovt[:], ALU.mult), reads=[t_ovl], writes=[t_ovl])

        KsT = k.sb(esA, [128, 4, TP], BF16); KwT = k.sb(esA, [128, 4, TP], BF16); t_Kx = T()
        Vs = k.sb(esA, [128, NQB, 512], BF16); Vw = k.sb(esA, [128, NQB, 512], BF16); t_Vx = T()
        for kvh in range(4):
            k.dma("sp", KsT[:, kvh, :], KT[2, kvh], reads=[t_KT], writes=[t_Kx])
            k.dma("sp", KwT[:, kvh, :], KT[3, kvh], reads=[t_KT], writes=[t_Kx])
        k.dma("sp", Vs[:], Vtok[0].rearrange("(b p) m -> p b m", p=128), reads=[t_Vtok], writes=[t_Vx])
        k.dma("sp", Vw[:], Vtok[1].rearrange("(b p) m -> p b m", p=128), reads=[t_Vtok], writes=[t_Vx])
        _q = k.sb(esA, [128, 32, 128], BF16); _tq = T(); QTb = [_q, _q]; t_QTb = [_tq, _tq]
        _o = k.sb(esA, [128, 32, 128], BF16); _to = T(); OTb = [_o, _o]; t_OTb = [_to, _to]
        sm = k.sb(esA, [128, TP], F32); t_sm = T()
        pg = k.sb(esA, [128, TP], BF16); t_pg = T()
        smw = k.sb(esA, [128, 640], F32); t_smw = T()
        pgw = k.sb(esA, [128, 640], BF16); t_pgw = T()
        PTs = k.sb(esA, [128, NQB, 128], BF16); t_PTs = T()
        PTw = k.sb(esA, [128, 5, 128], BF16); t_PTw = T()
        pcT = k.sb(esA, [128, 8, 128], BF16); t_pcT = T()
        smc = k.sb(esA, [128, 128], F32); exc = k.sb(esA, [128, 128], F32); t_smc = T()
        pgc = k.sb(esA, [128, 128], BF16); t_pgc = T()
        psP = k.sb(esA, [128, 128], F32); t_psP = T()
        psT = k.sb(esA, [128, 128], F32); t_psT = T()
        impm = k.sb(esA, [128, NB], F32); impw = k.sb(esA, [128, NB], F32); m8 = k.sb(esA, [128, 16], F32); t_imp = T()
        selb = k.sb(esA, [128, NB], F32); t_selb = T()
        bfull = k.sb(esA, [128, TP], F32); t_bf = T()

        def softmax_gated(sm_ap, ex_ap, pg_ap, gate_ap, nq, t_s, t_e, t_p):
            k.op("dve", lambda h: h.reduce_max(st[0:nq, 0:1], sm_ap, AXX), reads=[t_s], writes=[t_st])
            k.op("dve", lambda h: h.tensor_scalar(st[0:nq, 1:2], st[0:nq, 0:1], -1.0, None, ALU.mult), reads=[t_st], writes=[t_st])
            k.op("act", lambda h: h.activation(ex_ap, sm_ap, AF.Exp, bias=st[0:nq, 1:2], scale=1.0, accum_out=st[0:nq, 2:3]), reads=[t_s, t_st], writes=[t_e, t_st])
            k.op("dve", lambda h: h.reciprocal(st[0:nq, 3:4], st[0:nq, 2:3]), reads=[t_st], writes=[t_st])
            k.op("dve", lambda h: h.tensor_tensor(st[0:nq, 4:5], st[0:nq, 3:4], gate_ap, ALU.mult), reads=[t_st, t_gat], writes=[t_st])
            k.op("pool", lambda h: h.tensor_scalar(pg_ap, ex_ap, st[0:nq, 4:5], None, ALU.mult), reads=[t_e, t_st], writes=[t_p])

        def transposes(dst, t_dst, src, t_src, nblk, nq):
            for j0 in range(0, nblk, 4):
                n4 = min(4, nblk - j0)
                for j in range(n4):
                    k.op("pe", lambda h: h.transpose(pT7[:, j * 128:j * 128 + nq], src[0:nq, (j0 + j) * 128:(j0 + j + 1) * 128], ident_bf[0:nq, 0:nq]),
                         reads=[t_src, t_ident], writes=[t_pb[7]], inc=(j == n4 - 1))
                k.op("act", lambda h: h.copy(dst[:, j0:j0 + n4, 0:nq], pT7[:, 0:n4 * 128].rearrange("p (a t) -> p a t", t=128)[:, :, 0:nq]),
                     reads=[t_pb[7]], writes=[t_dst])

        for qb in range(NQB):
            qi = qb % 2; nk = 128 * (qb + 1); nq = 128
            k.dma("sp", QTb[qi][:], QT[:, :, qb * 128:(qb + 1) * 128].rearrange("h p t -> p h t"), reads=[t_QT], writes=[t_QTb[qi]])
            for kvh in range(4):
                for g in range(8):
                    hh = kvh * 8 + g
                    k.op("pe", lambda h: h.matmul(pbank[0][:, 0:NC_], QTb[qi][:, hh, :], cmpKT[:, kvh, 0:NC_], start=True, stop=True), reads=[t_QTb[qi], t_cK], writes=[t_pb[0]])
                    k.op("dve", lambda h: h.scalar_tensor_tensor(smc[:, 0:NC_], pbank[0][:, 0:NC_], SCALE, cmadd[:, qb, :], ALU.mult, ALU.add), reads=[t_pb[0], t_cm], writes=[t_smc])
                    k.op("dve", lambda h: h.reduce_max(st[:, 0:1], smc[:, 0:NC_], AXX), reads=[t_smc], writes=[t_st])
                    k.op("dve", lambda h: h.tensor_scalar(st[:, 1:2], st[:, 0:1], -1.0, None, ALU.mult), reads=[t_st], writes=[t_st])
                    k.op("act", lambda h: h.activation(exc[:, 0:NC_], smc[:, 0:NC_], AF.Exp, bias=st[:, 1:2], scale=1.0), reads=[t_smc, t_st], writes=[t_smc])
                    k.op("dve", lambda h: h.tensor_tensor(exc[:, 0:NC_], exc[:, 0:NC_], cm01[:, qb, :], ALU.mult), reads=[t_smc, t_cm], writes=[t_smc])
                    k.op("dve", lambda h: h.reduce_sum(st[:, 2:3], exc[:, 0:NC_], AXX), reads=[t_smc], writes=[t_st])
                    k.op("dve", lambda h: h.tensor_scalar(st[:, 2:3], st[:, 2:3], 1e-30, None, ALU.max), reads=[t_st], writes=[t_st])
                    k.op("dve", lambda h: h.reciprocal(st[:, 3:4], st[:, 2:3]), reads=[t_st], writes=[t_st])
                    k.op("dve", lambda h: h.tensor_scalar(exc[:, 0:NC_], exc[:, 0:NC_], st[:, 3:4], None, ALU.mult), reads=[t_smc, t_st], writes=[t_smc])
                    if g == 0:
                        k.op("pool", lambda h: h.tensor_copy(psP[:, 0:NC_], exc[:, 0:NC_]), reads=[t_smc], writes=[t_psP])
                    else:
                        k.op("pool", lambda h: h.tensor_tensor(psP[:, 0:NC_], psP[:, 0:NC_], exc[:, 0:NC_], ALU.add), reads=[t_smc, t_psP], writes=[t_psP])
                    k.op("dve", lambda h: h.memset(pgc[:], 0.0), writes=[t_pgc])
                    k.op("dve", lambda h: h.tensor_scalar(pgc[:, 0:NC_], exc[:, 0:NC_], gat[:, qb, hh * 3:hh * 3 + 1], None, ALU.mult), reads=[t_smc, t_gat], writes=[t_pgc])
                    k.op("pe", lambda h: h.transpose(pT7[:, 0:128], pgc[:, :], ident_bf[:]), reads=[t_pgc, t_ident], writes=[t_pb[7]])
                    k.op("act", lambda h: h.copy(pcT[:, g, :], pT7[:, 0:128]), reads=[t_pb[7]], writes=[t_pcT])
                k.op("dve", lambda h: h.memset(psT[:], 0.0), writes=[t_psT])
                k.op("pe", lambda h: h.transpose(pbank[1][0:NC_, 0:128], psP[:, 0:NC_], ident[:]), reads=[t_psP, t_ident], writes=[t_pb[1]])
                k.op("dve", lambda h: h.tensor_copy(psT[0:NC_, :], pbank[1][0:NC_, 0:128]), reads=[t_pb[1]], writes=[t_psT])
                k.op("pe", lambda h: h.matmul(pbank[1][:, 256:256 + NB], psT[0:NC_, :], ovl[0:NC_, :], start=True, stop=True), reads=[t_psT, t_ovl], writes=[t_pb[1]])
                k.op("dve", lambda h: h.tensor_tensor(impm[:], pbank[1][:, 256:256 + NB], keep[:, qb, :], ALU.mult), reads=[t_pb[1], t_ko], writes=[t_imp])
                k.op("dve", lambda h: h.tensor_tensor(impm[:], impm[:], over[:, qb, :], ALU.add), reads=[t_imp, t_ko], writes=[t_imp])
                k.op("dve", lambda h: h.max(m8[:, 0:8], impm[:]), reads=[t_imp], writes=[t_imp])
                k.op("dve", lambda h: h.match_replace(impw[:], m8[:, 0:8], impm[:], -3.0e38), reads=[t_imp], writes=[t_imp])
                k.op("dve", lambda h: h.max(m8[:, 8:16], impw[:]), reads=[t_imp], writes=[t_imp])
                k.op("dve", lambda h: h.tensor_scalar(selb[:], impm[:], m8[:, 15:16], None, ALU.is_ge), reads=[t_imp], writes=[t_selb])
                k.op("dve", lambda h: h.tensor_scalar(selb[:], selb[:], NEG, -NEG, ALU.mult, ALU.add), reads=[t_selb], writes=[t_selb])
                nbk = 2 * (qb + 1)
                k.op("dve", lambda h: h.tensor_copy(bfull[:, 0:nk].rearrange("p (j c) -> p j c", c=64), selb[:, 0:nbk].unsqueeze(2).to_broadcast([128, nbk, 64])), reads=[t_selb], writes=[t_bf])
                k.op("dve", lambda h: h.tensor_tensor(bfull[:, nk - 128:nk], bfull[:, nk - 128:nk], tri[:], ALU.add), reads=[t_bf, t_tri], writes=[t_bf])
                kb0 = max(0, qb - 4); nkw = 128 * (qb + 1 - kb0); woff = 640 - nkw; nwb = qb + 1 - kb0
                for g in range(8):
                    hh = kvh * 8 + g
                    for ci, ch in enumerate(range(0, nk, 512)):
                        w = min(512, nk - ch); pb = 2 + ci % 2
                        k.op("pe", lambda h: h.matmul(pbank[pb][:, 0:w], QTb[qi][:, hh, :], KsT[:, kvh, ch:ch + w], start=True, stop=True), reads=[t_QTb[qi], t_Kx], writes=[t_pb[pb]])
                        k.op("dve", lambda h: h.scalar_tensor_tensor(sm[:, ch:ch + w], pbank[pb][:, 0:w], SCALE, bfull[:, ch:ch + w], ALU.mult, ALU.add), reads=[t_pb[pb], t_bf], writes=[t_sm])
                    softmax_gated(sm[:, 0:nk], sm[:, 0:nk], pg[:, 0:nk], gat[:, qb, hh * 3 + 1:hh * 3 + 2], 128, t_sm, t_sm, t_pg)
                    for ci, ch in enumerate(range(0, nkw, 512)):
                        w = min(512, nkw - ch); pb = 4 + ci % 2
                        k.op("pe", lambda h: h.matmul(pbank[pb][:, 0:w], QTb[qi][:, hh, :], KwT[:, kvh, kb0 * 128 + ch:kb0 * 128 + ch + w], start=True, stop=True), reads=[t_QTb[qi], t_Kx], writes=[t_pb[pb]])
                        k.op("dve", lambda h: h.scalar_tensor_tensor(smw[:, ch:ch + w], pbank[pb][:, 0:w], SCALE, W640[:, woff + ch:woff + ch + w], ALU.mult, ALU.add), reads=[t_pb[pb], t_tri], writes=[t_smw])
                    softmax_gated(smw[:, 0:nkw], smw[:, 0:nkw], pgw[:, 0:nkw], gat[:, qb, hh * 3 + 2:hh * 3 + 3], 128, t_smw, t_smw, t_pgw)
                    transposes(PTs, t_PTs, pg, t_pg, qb + 1, 128)
                    transposes(PTw, t_PTw, pgw, t_pgw, nwb, 128)
                    k.op("pe", lambda h: h.matmul(pbank[6][:, 0:128], cmpV[0:NC_, kvh, :], pcT[0:NC_, g, :], start=True, stop=False), reads=[t_cV, t_pcT], writes=[t_pb[6]], inc=False)
                    for j in range(qb + 1):
                        k.op("pe", lambda h: h.matmul(pbank[6][:, 0:128], Vs[:, j, kvh * 128:(kvh + 1) * 128], PTs[:, j, :], start=False, stop=False), reads=[t_Vx, t_PTs], writes=[t_pb[6]], inc=False)
                    for j in range(nwb):
                        k.op("pe", lambda h: h.matmul(pbank[6][:, 0:128], Vw[:, kb0 + j, kvh * 128:(kvh + 1) * 128], PTw[:, j, :], start=False, stop=(j == nwb - 1)), reads=[t_Vx, t_PTw], writes=[t_pb[6]], inc=(j == nwb - 1))
                    k.op("act", lambda h: h.copy(OTb[qi][:, hh, :], pbank[6][:, 0:128]), reads=[t_pb[6]], writes=[t_OTb[qi]])
            k.dma("sp", OTv[:, :, qb * 128:(qb + 1) * 128], OTb[qi][:], reads=[t_OTb[qi]], writes=[t_OT])
        k.barrier()
        esA.close()
        esB = contextlib.ExitStack()
        NKP = 128 * 128; NKs = NKP + NS; NCs = 1023; NBs = 257
        gS = dscr("gS", [NS, 96]); t_gS = T()
        Qs = k.sb(esB, [128, 32, NS], BF16); QsT = k.sb(esB, [128, 4, 32], BF16); t_Qs = T()
        k.dma("sp", Qs[:], QT[:, :, TP:TP + NS].rearrange("h p t -> p h t"), reads=[t_QT], writes=[t_Qs])
        for kvh in range(4):
            k.op("dve", lambda h: h.tensor_copy(QsT[:, kvh, :].rearrange("p (i g) -> p i g", g=8), Qs[:, kvh * 8:(kvh + 1) * 8, :].rearrange("p g i -> p i g")), reads=[t_Qs], writes=[t_Qs])
        k.dma("sp", gS[:, :], gat[0:NS, NQB, :], reads=[t_gat], writes=[t_gS])
        gate_s = k.sb(esB, [32, 4, 8, 3], F32); t_gs = T()
        for i in range(NS):
            k.dma("sp", gate_s[8 * i:8 * i + 8, :, 0, :], gS[i].rearrange("(k g b) -> g k b", k=4, g=8), reads=[t_gS], writes=[t_gs])
        R32 = 8 * NS
        mis = k.sb(esB, [128, 520], I32); t_mis = T()
        keep_s = k.sb(esB, [32, NBs], F32); over_s = k.sb(esB, [32, NBs], F32); t_kos = T()
        k.op("dve", lambda h: h.memset(keep_s[:], 1.0), writes=[t_kos])
        k.op("dve", lambda h: h.memset(over_s[:], 0.0), writes=[t_kos])
        for (lo, hi) in ((0, 1), (NBs - 2, NBs)):
            k.op("dve", lambda h: h.memset(keep_s[:, lo:hi], 0.0), reads=[t_kos], writes=[t_kos])
            k.op("dve", lambda h: h.memset(over_s[:, lo:hi], BIG), reads=[t_kos], writes=[t_kos])
        tri_s = k.sb(esB, [32, NS], F32); wmask_s = k.sb(esB, [32, 512 + NS], F32); t_ms = T()
        k.op("pool", lambda h: h.iota(mis[0:32, 0:NS], pattern=[[-8, NS]], base=0, channel_multiplier=1), writes=[t_mis])
        k.op("dve", lambda h: h.tensor_copy(tri_s[:], mis[0:32, 0:NS]), reads=[t_mis], writes=[t_ms])
        k.op("dve", lambda h: h.tensor_single_scalar(tri_s[:], tri_s[:], 0.0, ALU.is_ge), reads=[t_ms], writes=[t_ms])
        k.op("dve", lambda h: h.tensor_scalar(tri_s[:], tri_s[:], NEG, -NEG, ALU.mult, ALU.add), reads=[t_ms], writes=[t_ms])
        k.op("pool", lambda h: h.iota(mis[0:32, 0:512], pattern=[[8, 512]], base=0, channel_multiplier=-1), reads=[t_ms], writes=[t_mis])
        k.op("dve", lambda h: h.tensor_copy(wmask_s[:, 0:512], mis[0:32, 0:512]), reads=[t_mis], writes=[t_ms])
        k.op("dve", lambda h: h.tensor_single_scalar(wmask_s[:, 0:512], wmask_s[:, 0:512], 0.0, ALU.is_gt), reads=[t_ms], writes=[t_ms])
        k.op("dve", lambda h: h.tensor_scalar(wmask_s[:, 0:512], wmask_s[:, 0:512], NEG, -NEG, ALU.mult, ALU.add), reads=[t_ms], writes=[t_ms])
        k.op("dve", lambda h: h.tensor_copy(wmask_s[:, 512:512 + NS], tri_s[:]), reads=[t_ms], writes=[t_ms])
        Gsum = k.sb(esB, [32, NS, 8], F32); gtmp = k.sb(esB, [32, NS, 8], F32)
        k.op("pool", lambda h: h.iota(mis[0:32, 0:32].rearrange("p (i g) -> p i g", g=8), pattern=[[-8, NS], [0, 8]], base=0, channel_multiplier=1), reads=[t_ms], writes=[t_mis])
        k.op("dve", lambda h: h.tensor_copy(Gsum[:], mis[0:32, 0:32].rearrange("p (i g) -> p i g", g=8)), reads=[t_mis], writes=[t_ms])
        k.op("dve", lambda h: h.tensor_single_scalar(gtmp[:], Gsum[:], 7.0, ALU.is_le), reads=[t_ms], writes=[t_ms])
        k.op("dve", lambda h: h.tensor_single_scalar(Gsum[:], Gsum[:], 0.0, ALU.is_ge), reads=[t_ms], writes=[t_ms])
        k.op("dve", lambda h: h.tensor_tensor(Gsum[:], Gsum[:], gtmp[:], ALU.mult), reads=[t_ms], writes=[t_ms])
        ovl_s = k.sb(esB, [128, 8, NBs], F32); ovt_s = k.sb(esB, [128, NBs], F32); t_ovs = T()
        for gi in range(8):
            cbase = 0 if gi == 0 else 128 * gi - 1
            k.op("pool", lambda h: h.iota(mis[:, 0:NBs], pattern=[[64, NBs]], base=64 - 16 * cbase, channel_multiplier=-16), reads=[t_ovs, t_ms], writes=[t_mis])
            k.op("dve", lambda h: h.tensor_copy(ovl_s[:, gi, :], mis[:, 0:NBs]), reads=[t_mis], writes=[t_ovs])
            k.op("dve", lambda h: h.tensor_single_scalar(ovl_s[:, gi, :], ovl_s[:, gi, :], 0.0, ALU.is_gt), reads=[t_ovs], writes=[t_ovs])
            k.op("pool", lambda h: h.iota(mis[:, 0:NBs], pattern=[[-64, NBs]], base=16 * cbase + 32, channel_multiplier=16), reads=[t_ovs], writes=[t_mis])
            k.op("dve", lambda h: h.tensor_copy(ovt_s[:], mis[:, 0:NBs]), reads=[t_mis], writes=[t_ovs])
            k.op("dve", lambda h: h.tensor_single_scalar(ovt_s[:], ovt_s[:], 0.0, ALU.is_gt), reads=[t_ovs], writes=[t_ovs])
            k.op("dve", lambda h: h.tensor_tensor(ovl_s[:, gi, :], ovl_s[:, gi, :], ovt_s[:], ALU.mult), reads=[t_ovs], writes=[t_ovs])

        pgb = [k.sb(esB, [128, 2048], F32) for _ in range(2)]; t_pgb = [T(), T()]
        kTp = [k.sb(esB, [128, 128], BF16) for _ in range(2)]; t_kTp = [T(), T()]
        vbp = [k.sb(esB, [128, 128], BF16) for _ in range(2)]; t_vbp = [T(), T()]
        sm_s = k.sb(esB, [32, NKs], F32); t_sms = T()
        pg_s = k.sb(esB, [32, NKs], BF16); t_pgs = T()
        PT_s = k.sb(esB, [128, 129, 32], BF16); t_PTs2 = T()
        smc_s = k.sb(esB, [32, 1024], F32); t_smcs = T()
        pgc_s = k.sb(esB, [32, 1024], BF16); t_pgcs = T()
        PcT_s = k.sb(esB, [128, 8, 32], BF16); t_PcTs = T()
        PTf = k.sb(esB, [128, 8, 32], F32); t_PTf = T()
        impr = k.sb(esB, [32, NBs], F32); impm_s = k.sb(esB, [32, NBs], F32); impw_s = k.sb(esB, [32, NBs], F32); m8s = k.sb(esB, [32, 16], F32); t_imps = T()
        selb_s = k.sb(esB, [32, NBs + 7], F32); t_selbs = T()
        smw_s = k.sb(esB, [32, 512 + NS], F32); t_smws = T()
        pgw_s = k.sb(esB, [32, 512 + NS], BF16); t_pgws = T()
        PTw_s = k.sb(esB, [128, 5, 32], BF16); t_PTws = T()
        KwT_s = k.sb(esB, [128, 512 + NS], BF16); Vw_s = k.sb(esB, [128, 5, 128], BF16); t_kws = T()
        cwk = k.sb(esB, [128, 4, 128], F32); cwv = k.sb(esB, [128, 4, 128], F32); t_cw2 = T()
        nr_t = k.sb(esB, [NS, 4, 128], F32); t_nr = T()
        OTs = k.sb(esB, [128, 32, NS], BF16); t_OTs = T()
        cwin = cache_win.rearrange("(a p) m -> p a m", p=128)

        def chunk_T(dst_ap, src_ap, nq, ncol, t_src, t_dst):
            k.op("pe", lambda h: h.transpose(pT7[0:ncol, 0:nq], src_ap, ident_bf[0:nq, 0:nq]), reads=[t_src, t_ident], writes=[t_pb[7]])
            k.op("act", lambda h: h.copy(dst_ap, pT7[0:ncol, 0:nq]), reads=[t_pb[7]], writes=[t_dst])

        for kvh in range(4):
            q_l = QsT[:, kvh, :]
            for j_, col in enumerate((1024, 1536, 2048, 2560)):
                k.dma("sp", nr_t[:, j_, :], kvS[:, col + kvh * 128:col + (kvh + 1) * 128], reads=[t_kvS], writes=[t_nr])
            for ci, ch in enumerate(range(0, NCs, 512)):
                w = min(512, NCs - ch); pb = 2 + ci % 2
                k.op("pe", lambda h: h.matmul(pbank[pb][0:R32, 0:w], q_l, cmpKT_s[:, kvh, ch:ch + w], start=True, stop=True), reads=[t_Qs, t_cKs], writes=[t_pb[pb]])
                k.op("dve", lambda h: h.tensor_scalar(smc_s[:, ch:ch + w], pbank[pb][0:R32, 0:w], SCALE, None, ALU.mult), reads=[t_pb[pb]], writes=[t_smcs])
            k.op("dve", lambda h: h.reduce_max(st[0:R32, 0:1], smc_s[:, 0:NCs], AXX), reads=[t_smcs], writes=[t_st])
            k.op("dve", lambda h: h.tensor_scalar(st[0:R32, 1:2], st[0:R32, 0:1], -1.0, None, ALU.mult), reads=[t_st], writes=[t_st])
            k.op("act", lambda h: h.activation(smc_s[:, 0:NCs], smc_s[:, 0:NCs], AF.Exp, bias=st[0:R32, 1:2], scale=1.0, accum_out=st[0:R32, 2:3]), reads=[t_smcs, t_st], writes=[t_smcs, t_st])
            k.op("dve", lambda h: h.reciprocal(st[0:R32, 3:4], st[0:R32, 2:3]), reads=[t_st], writes=[t_st])
            k.op("dve", lambda h: h.tensor_scalar(smc_s[:, 0:NCs], smc_s[:, 0:NCs], st[0:R32, 3:4], None, ALU.mult), reads=[t_smcs, t_st], writes=[t_smcs])
            k.op("dve", lambda h: h.tensor_scalar(pgc_s[:, 0:NCs], smc_s[:, 0:NCs], gate_s[:, kvh, 0, 0:1], None, ALU.mult), reads=[t_smcs, t_gs], writes=[t_pgcs])
            for gi in range(8):
                nblk = 127 if gi == 0 else 128; cbase = 0 if gi == 0 else 128 * gi - 1
                chunk_T(PcT_s[0:nblk, gi, :], pgc_s[:, cbase:cbase + nblk], R32, nblk, t_pgcs, t_PcTs)
                k.op("pe", lambda h: h.transpose(pbank[1][0:nblk, 0:R32], smc_s[:, cbase:cbase + nblk], ident[0:R32, 0:R32]), reads=[t_smcs, t_ident], writes=[t_pb[1]])
                k.op("dve", lambda h: h.tensor_copy(PTf[0:nblk, gi, :], pbank[1][0:nblk, 0:R32]), reads=[t_pb[1]], writes=[t_PTf])
            for gi in range(8):
                nblk = 127 if gi == 0 else 128
                k.op("pe", lambda h: h.matmul(pbank[0][0:R32, 0:NBs], PTf[0:nblk, gi, :], ovl_s[0:nblk, gi, :], start=(gi == 0), stop=(gi == 7)), reads=[t_PTf, t_ovs], writes=[t_pb[0]], inc=(gi == 7))
            k.op("dve", lambda h: h.tensor_copy(impr[:], pbank[0][0:R32, 0:NBs]), reads=[t_pb[0]], writes=[t_imps])
            k.op("pe", lambda h: h.matmul(pbank[0][0:R32, 0:NBs], Gsum[:].rearrange("p i g -> p (i g)"), impr[:], start=True, stop=True), reads=[t_imps, t_ms], writes=[t_pb[0]])
            k.op("dve", lambda h: h.tensor_tensor(impm_s[:], pbank[0][0:R32, 0:NBs], keep_s[:], ALU.mult), reads=[t_pb[0], t_kos], writes=[t_imps])
            k.op("dve", lambda h: h.tensor_tensor(impm_s[:], impm_s[:], over_s[:], ALU.add), reads=[t_imps, t_kos], writes=[t_imps])
            k.op("dve", lambda h: h.max(m8s[:, 0:8], impm_s[:]), reads=[t_imps], writes=[t_imps])
            k.op("dve", lambda h: h.match_replace(impw_s[:], m8s[:, 0:8], impm_s[:], -3.0e38), reads=[t_imps], writes=[t_imps])
            k.op("dve", lambda h: h.max(m8s[:, 8:16], impw_s[:]), reads=[t_imps], writes=[t_imps])
            k.op("dve", lambda h: h.memset(selb_s[:], 0.0), writes=[t_selbs])
            k.op("dve", lambda h: h.tensor_scalar(selb_s[:, 0:NBs], impm_s[:], m8s[:, 15:16], None, ALU.is_ge), reads=[t_imps], writes=[t_selbs])
            k.op("dve", lambda h: h.tensor_scalar(selb_s[:, 0:NBs], selb_s[:, 0:NBs], NEG, -NEG, ALU.mult, ALU.add), reads=[t_selbs], writes=[t_selbs])
            for pg_ in range(128):
                b2 = pg_ % 2
                k.idma(pgb[b2][:], cache_kv, idx[:, pg_:pg_ + 1], reads=[t_idx], writes=[t_pgb[b2]])
                k.op("pe", lambda h: h.transpose(pbank[4 + b2][:, 0:128], pgb[b2][:, (8 + kvh) * 128:(9 + kvh) * 128], ident[:]), reads=[t_pgb[b2], t_ident], writes=[t_pb[4 + b2]])
                k.op("act", lambda h: h.copy(kTp[b2][:], pbank[4 + b2][:, 0:128]), reads=[t_pb[4 + b2]], writes=[t_kTp[b2]])
                k.op("pe", lambda h: h.matmul(pbank[2 + b2][0:R32, 0:128], q_l, kTp[b2][:], start=True, stop=True), reads=[t_Qs, t_kTp[b2]], writes=[t_pb[2 + b2]])
                k.op("dve", lambda h: h.scalar_tensor_tensor(sm_s[:, pg_ * 128:(pg_ + 1) * 128].rearrange("p (j c) -> p j c", c=64), pbank[2 + b2][0:R32, 0:128].rearrange("p (j c) -> p j c", c=64), SCALE,
                                                               selb_s[:, 2 * pg_:2 * pg_ + 2].unsqueeze(2).to_broadcast([R32, 2, 64]), ALU.mult, ALU.add), reads=[t_pb[2 + b2], t_selbs], writes=[t_sms])
            k.op("pe", lambda h: h.transpose(pbank[4][:, 0:NS], nr_t[:, 0, :], ident[0:NS, 0:NS]), reads=[t_nr, t_ident], writes=[t_pb[4]])
            k.op("act", lambda h: h.copy(kTp[0][:, 0:NS], pbank[4][:, 0:NS]), reads=[t_pb[4]], writes=[t_kTp[0]])
            k.op("pe", lambda h: h.matmul(pbank[2][0:R32, 0:NS], q_l, kTp[0][:, 0:NS], start=True, stop=True), reads=[t_Qs, t_kTp[0]], writes=[t_pb[2]])
            k.op("dve", lambda h: h.scalar_tensor_tensor(sm_s[:, NKP:NKs], pbank[2][0:R32, 0:NS], SCALE, tri_s[:], ALU.mult, ALU.add), reads=[t_pb[2], t_ms], writes=[t_sms])
            k.op("dve", lambda h: h.tensor_scalar(sm_s[:, NKP:NKs], sm_s[:, NKP:NKs], selb_s[:, NBs - 1:NBs], None, ALU.add), reads=[t_sms, t_selbs], writes=[t_sms])
            softmax_gated(sm_s[:, 0:NKs], sm_s[:, 0:NKs], pg_s[:, 0:NKs], gate_s[:, kvh, 0, 1:2], R32, t_sms, t_sms, t_pgs)
            for pg_ in range(128):
                chunk_T(PT_s[:, pg_, :], pg_s[:, pg_ * 128:(pg_ + 1) * 128], R32, 128, t_pgs, t_PTs2)
            chunk_T(PT_s[0:NS, 128, :], pg_s[:, NKP:NKs], R32, NS, t_pgs, t_PTs2)
            k.dma("sp", cwk[:], cwin[:, :, kvh * 128:(kvh + 1) * 128], writes=[t_cw2])
            k.dma("sp", cwv[:], cwin[:, :, 512 + kvh * 128:512 + (kvh + 1) * 128], writes=[t_cw2])
            for a_ in range(4):
                k.op("pe", lambda h: h.transpose(pbank[4][:, 0:128], cwk[:, a_, :], ident[:]), reads=[t_cw2, t_ident], writes=[t_pb[4]])
                k.op("act", lambda h: h.copy(KwT_s[:, a_ * 128:(a_ + 1) * 128], pbank[4][:, 0:128]), reads=[t_pb[4]], writes=[t_kws])
            k.op("dve", lambda h: h.tensor_copy(Vw_s[:, 0:4, :], cwv[:]), reads=[t_cw2], writes=[t_kws])
            k.op("pe", lambda h: h.transpose(pbank[4][:, 0:NS], nr_t[:, 2, :], ident[0:NS, 0:NS]), reads=[t_nr, t_ident], writes=[t_pb[4]])
            k.op("act", lambda h: h.copy(KwT_s[:, 512:512 + NS], pbank[4][:, 0:NS]), reads=[t_pb[4]], writes=[t_kws])
            k.op("dve", lambda h: h.tensor_copy(Vw_s[0:NS, 4, :], nr_t[:, 3, :]), reads=[t_nr], writes=[t_kws])
            for (ch, w) in ((0, 512), (512, NS)):
                k.op("pe", lambda h: h.matmul(pbank[3][0:R32, 0:w], q_l, KwT_s[:, ch:ch + w], start=True, stop=True), reads=[t_Qs, t_kws], writes=[t_pb[3]])
                k.op("dve", lambda h: h.scalar_tensor_tensor(smw_s[:, ch:ch + w], pbank[3][0:R32, 0:w], SCALE, wmask_s[:, ch:ch + w], ALU.mult, ALU.add), reads=[t_pb[3], t_ms], writes=[t_smws])
            softmax_gated(smw_s[:, :], smw_s[:, :], pgw_s[:, :], gate_s[:, kvh, 0, 2:3], R32, t_smws, t_smws, t_pgws)
            for a_ in range(4):
                chunk_T(PTw_s[:, a_, :], pgw_s[:, a_ * 128:(a_ + 1) * 128], R32, 128, t_pgws, t_PTws)
            chunk_T(PTw_s[0:NS, 4, :], pgw_s[:, 512:512 + NS], R32, NS, t_pgws, t_PTws)
            oT = pbank[6][:, 0:R32]
            for gi in range(8):
                nblk = 127 if gi == 0 else 128
                k.op("pe", lambda h: h.matmul(oT, cmpV_s[0:nblk, gi, kvh, :], PcT_s[0:nblk, gi, :], start=(gi == 0), stop=False), reads=[t_cKs, t_PcTs], writes=[t_pb[6]], inc=False)
            for a_ in range(4):
                k.op("pe", lambda h: h.matmul(oT, Vw_s[:, a_, :], PTw_s[:, a_, :], start=False, stop=False), reads=[t_kws, t_PTws], writes=[t_pb[6]], inc=False)
            k.op("pe", lambda h: h.matmul(oT, Vw_s[0:NS, 4, :], PTw_s[0:NS, 4, :], start=False, stop=False), reads=[t_kws, t_PTws], writes=[t_pb[6]], inc=False)
            k.op("dve", lambda h: h.tensor_copy(vbp[0][0:NS, :], nr_t[:, 1, :]), reads=[t_nr], writes=[t_vbp[0]])
            k.op("pe", lambda h: h.matmul(oT, vbp[0][0:NS, :], PT_s[0:NS, 128, :], start=False, stop=False), reads=[t_vbp[0], t_PTs2], writes=[t_pb[6]])
            for pg_ in range(128):
                b2 = pg_ % 2
                k.idma(pgb[b2][:], cache_kv, idx[:, pg_:pg_ + 1], reads=[t_idx], writes=[t_pgb[b2]])
                k.op("dve", lambda h: h.tensor_copy(vbp[b2][:], pgb[b2][:, (12 + kvh) * 128:(13 + kvh) * 128]), reads=[t_pgb[b2]], writes=[t_vbp[b2]])
                k.op("pe", lambda h: h.matmul(oT, vbp[b2][:], PT_s[:, pg_, :], start=False, stop=(pg_ == 127)), reads=[t_vbp[b2], t_PTs2], writes=[t_pb[6]])
            k.op("act", lambda h: h.copy(OTs[:, kvh * 8:(kvh + 1) * 8, :].rearrange("p g i -> p i g"), pbank[6][:, 0:R32].rearrange("p (i g) -> p i g", g=8)), reads=[t_pb[6]], writes=[t_OTs])
        k.dma("sp", OTv[:, :, TP:TP + NS], OTs[:], reads=[t_OTs], writes=[t_OT])
        k.barrier()
        esB.close()
    k.barrier()

    with contextlib.ExitStack() as es:
        hbuf = k.sb(es, [128, KC, NT], BF16); t_h = T()
        for c8 in range(0, KC, 8):
            k.dma("sp", hbuf[:, c8:c8 + 8, :], OTv[:, c8:c8 + 8, :], reads=[t_OT], writes=[t_h])
        xc_ = [k.sb(es, [128, 512], F32) for _ in range(2)]; t_xc_ = [T(), T()]
        cnt = [0]
        def o_epi(idx, ti, pbs):
            c0, w, _s = tiles[ti]; b = cnt[0] % 2; cnt[0] += 1
            k.dma("sp", xc_[b][:, 0:w], xTv[:, idx, c0:c0 + w], reads=[t_xT], writes=[t_xc_[b]])
            k.op("dve", lambda h: h.tensor_tensor(xc_[b][:, 0:w], xc_[b][:, 0:w], pbank[pbs[0]][:, 0:w], ALU.add), reads=[t_pb[pbs[0]], t_xc_[b]], writes=[t_xc_[b]])
            k.dma("sp", xTv[:, idx, c0:c0 + w], xc_[b][:, 0:w], reads=[t_xc_[b]], writes=[t_xT])
        dense_ws(es, w_o, [[m] for m in range(KC)], hbuf, t_h, o_epi)
    k.barrier()

    ffn_layer(1)

    with contextlib.ExitStack() as es:
        xt = [k.sb(es, [128, KC, 128], F32) for _ in range(2)]; t_xt = [T(), T()]
        sq = k.sb(es, [128, 128], F32); t_sq = T()
        rstd = k.sb(es, [128, 128], F32); t_rstd = T()
        yc = [k.sb(es, [128, 4, 128], F32) for _ in range(2)]; t_yc = [T(), T()]
        yo = [k.sb(es, [128, D], F32) for _ in range(2)]; t_yo = [T(), T()]
        blocks = [(i * 128, min(128, TP - i * 128), 0) for i in range(NQB)] + [(TP, NS, 1)]
        for bi, (cc, hw, sq_) in enumerate(blocks):
            b = bi % 2
            k.dma("sp", xt[b][:, :, 0:hw], xTv[:, :, cc:cc + hw], reads=[t_xT], writes=[t_xt[b]])
            pb = 2 + b
            for c in range(KC):
                k.op("act", lambda h: h.activation(sq[:, 0:hw], xt[b][:, c, 0:hw], AF.Square), reads=[t_xt[b]], writes=[t_sq])
                k.op("pe", lambda h: h.matmul(pbank[pb][:, 0:hw], ones[:], sq[:, 0:hw], start=(c == 0), stop=(c == KC - 1)), reads=[t_sq, t_ones], writes=[t_pb[pb]])
            k.op("act", lambda h: h.activation(rstd[:, 0:hw], pbank[pb][:, 0:hw], AF.Sqrt, scale=1.0 / D, bias=epsc[:, 0:1]), reads=[t_pb[pb], t_iot], writes=[t_rstd])
            k.op("dve", lambda h: h.reciprocal(rstd[:, 0:hw], rstd[:, 0:hw]), reads=[t_rstd], writes=[t_rstd])
            for c4 in range(0, KC, 4):
                yi = (c4 // 4) % 2; n4 = min(4, KC - c4); pbt = 4 + yi
                for c in range(n4):
                    k.op("dve", lambda h: h.scalar_tensor_tensor(yc[yi][:, c, 0:hw], xt[b][:, c4 + c, 0:hw], gains[:, 5, c4 + c:c4 + c + 1], rstd[:, 0:hw], ALU.mult, ALU.mult),
                         reads=[t_xt[b], t_rstd, t_gains], writes=[t_yc[yi]])
                for c in range(n4):
                    k.op("pe", lambda h: h.transpose(pbank[pbt][0:hw, c * 128:(c + 1) * 128], yc[yi][:, c, 0:hw], ident[:]), reads=[t_yc[yi], t_ident], writes=[t_pb[pbt]], inc=(c == n4 - 1))
                k.op("act", lambda h: h.copy(yo[b][0:hw, c4 * 128:(c4 + n4) * 128], pbank[pbt][0:hw, 0:n4 * 128]), reads=[t_pb[pbt]], writes=[t_yo[b]])
            od = o_y_p[cc:cc + hw, :] if sq_ == 0 else o_y_s[0:hw, :]
            k.dma("sp", od, yo[b][0:hw, :], reads=[t_yo[b]], writes=[])

    glob.close()
    k.finish()
    return nc


def core_inputs(cfg, c, I):
    B = I["x_prompt"].shape[0]
    Q = cfg.Q
    m = {
        "xp": I["x_prompt"][c % B], "xs": I["x_sample"][c],
        "st_re": I["state_ssm_re"][0, c].reshape(Q, 128), "st_im": I["state_ssm_im"][0, c].reshape(Q, 128),
        "cache_kv": I["cache_kv"].reshape(-1, 2048), "page_tab": I["page_table"][c].reshape(1, 128),
        "st_conv": I["state_ffn_conv"][0, c], "st_conv1": I["state_ffn_conv"][1, c], "cache_win": I["cache_win"][c].reshape(512, 1024),
        "attn_g1": I["attn_norm"][1], "ffn_g1": I["ffn_norm"][1], "fin_g": I["final_norm"],
        "w_in1": I["ffn_w_in"][1], "w_down1": I["ffn_w_down"][1], "conv_w1": I["ffn_conv_w"][1], "conv_b1": I["ffn_conv_b"][1],
        "w_qg": I["w_qg"][0], "w_o": I["w_o"][0], "cmp_w1": I["cmp_w1"], "cmp_b1": I["cmp_b1"], "cmp_w2": I["cmp_w2"],
        "cmp_b2": I["cmp_b2"], "cmp_pe": I["cmp_pe"],
        "attn_g": I["attn_norm"][0], "ffn_g": I["ffn_norm"][0], "kv_g": I["kv_norm"],
        "lam_re": I["ssm_lam_re"][0].reshape(Q, 128), "lam_im": I["ssm_lam_im"][0].reshape(Q, 128),
        "log_step": I["ssm_log_step"][0].reshape(Q, 2),
        "b_re": I["ssm_b_re"][0], "b_im": I["ssm_b_im"][0],
        "c_re": I["ssm_c_re"][0].reshape(cfg.G * 16, 64), "c_im": I["ssm_c_im"][0].reshape(cfg.G * 16, 64),
        "ssm_d": I["ssm_d"][0], "w_glu": I["ssm_w_glu"][0], "w_in": I["ffn_w_in"][0], "w_down": I["ffn_w_down"][0],
        "conv_w": I["ffn_conv_w"][0], "conv_b": I["ffn_conv_b"][0], "w_kv": I["w_kv"],
    }
    return {n: np.ascontiguousarray(np.asarray(v, dtype=(np.int32 if n == "page_tab" else np.float32))) for n, v in m.items()}


def kernel(**I):
    I = {n: np.asarray(v) for n, v in I.items()}
    B, TP, D = I["x_prompt"].shape
    DB, NS, _ = I["x_sample"].shape
    DFF = I["ffn_conv_b"].shape[1]
    cfg = Cfg(D=D, TP=TP, NS=NS, DFF=DFF, NKV=I["w_kv"].shape[1], NPOOL=I["cache_kv"].shape[0])
    nc = build(cfg)
    ncores = 8
    in_maps = [core_inputs(cfg, c, I) for c in range(ncores)]
    res = run_bass_kernel_spmd(nc, in_maps, core_ids=list(range(ncores)))
    R = res.results
    G, Q = cfg.G, cfg.Q
    depth = I["ffn_w_in"].shape[0]
    f32 = np.float32
    y_p = np.stack([R[b]["o_y_p"] for b in range(B)]); y_s = np.stack([R[c]["o_y_s"] for c in range(DB)])
    ssm_re_p = np.stack([R[b]["o_ssm_re_p"].reshape(G, 64) for b in range(B)])[None]
    ssm_im_p = np.stack([R[b]["o_ssm_im_p"].reshape(G, 64) for b in range(B)])[None]
    ssm_re_s = np.stack([R[c]["o_ssm_re_s"].reshape(G, 64) for c in range(DB)])[None]
    ssm_im_s = np.stack([R[c]["o_ssm_im_s"].reshape(G, 64) for c in range(DB)])[None]
    conv_p = np.zeros((depth, B, 2, DFF), f32); conv_s = np.zeros((depth, DB, 2, DFF), f32)
    conv_p[0] = np.stack([R[b]["o_conv_p"] for b in range(B)])
    conv_s[0] = np.stack([R[c]["o_conv_s"] for c in range(DB)])
    conv_p[1] = np.stack([R[b]["o_conv_p1"] for b in range(B)])
    conv_s[1] = np.stack([R[c]["o_conv_s1"] for c in range(DB)])
    kv_p = np.stack([R[b]["o_kv_p"].reshape(TP, 4, 4, 128) for b in range(B)])
    kv_s = np.stack([R[c]["o_kv_s"].reshape(NS, 4, 4, 128) for c in range(DB)])
    win_p = np.stack([R[b]["o_win_p"].reshape(512, 2, 4, 128) for b in range(B)])
    win_s = np.stack([R[c]["o_win_s"].reshape(512, 2, 4, 128) for c in range(DB)])
    return (y_p, y_s, ssm_re_p.astype(f32), ssm_im_p.astype(f32), ssm_re_s.astype(f32), ssm_im_s.astype(f32),
            conv_p, conv_s, kv_p.astype(f32), kv_s.astype(f32), win_p.astype(f32), win_s.astype(f32))
```

```python
import math, contextlib
import numpy as np
import concourse.bass as bass
import concourse.mybir as mybir
from concourse.bass_utils import run_bass_kernel_spmd

F32 = mybir.dt.float32; BF16 = mybir.dt.bfloat16; I32 = mybir.dt.int32
ALU = mybir.AluOpType
AF = mybir.ActivationFunctionType
TWO_PI = 2.0 * math.pi


class T:
    __slots__ = ("w", "r")
    def __init__(self):
        self.w = []; self.r = []


class Eng:
    def __init__(self, h, sem, name):
        self.h = h; self.sem = sem; self.count = 0; self.seen = {}; self.name = name


class K:
    def __init__(self, n_dma_sems=16):
        self.nc = bass.Bass("TRN2", target_bir_lowering=False)
        nc = self.nc
        self.es = contextlib.ExitStack()
        self.E = {}
        for nm, h in (("pe", nc.tensor), ("act", nc.scalar), ("dve", nc.vector), ("pool", nc.gpsimd), ("sp", nc.sync)):
            s = self.es.enter_context(nc.semaphore("sem_" + nm))
            self.E[nm] = Eng(h, s, nm)
        self.dsem = {}
        for q in ("sp", "pool"):
            self.dsem[q] = [[self.es.enter_context(nc.semaphore(f"d_{q}_{i}")), 0] for i in range(n_dma_sems)]
        self.dnext = {"sp": 0, "pool": 0}
        self.uid = 0

    def sb(self, es, shape, dt, name=None):
        self.uid += 1
        return es.enter_context(self.nc.sbuf_tensor(name or f"sb{self.uid}", list(shape), dt))

    def ps(self, es, shape, dt=F32, name=None):
        self.uid += 1
        return es.enter_context(self.nc.psum_tensor(name or f"ps{self.uid}", list(shape), dt))

    def _waits(self, E, reads, writes):
        waits = {}
        for t in reads:
            for (s, v) in t.w:
                if waits.get(id(s), (None, 0))[1] < v: waits[id(s)] = (s, v)
        for t in writes:
            for (s, v) in t.w + t.r:
                if waits.get(id(s), (None, 0))[1] < v: waits[id(s)] = (s, v)
        for s, v in waits.values():
            if s is E.sem and v > E.count:
                continue
            if E.seen.get(id(s), 0) >= v: continue
            E.h.wait_ge(s, v); E.seen[id(s)] = v

    def op(self, eng, fn, reads=(), writes=(), inc=True):
        E = self.E[eng]
        self._waits(E, reads, writes)
        ins = fn(E.h)
        if inc:
            E.count += 1
            ins.then_inc(E.sem, 1)
            ev = (E.sem, E.count)
        else:
            ev = (E.sem, E.count + 1)
        for t in reads:
            t.r = [e for e in t.r if e[0] is not ev[0]] + [ev]
        for t in writes:
            t.w = [ev]; t.r = []
        return ins

    def dma(self, q, out, in_, reads=(), writes=(), **kw):
        E = self.E[q]
        self._waits(E, reads, writes)
        lst = self.dsem[q]; i = self.dnext[q]; self.dnext[q] = (i + 1) % len(lst)
        s, v = lst[i]
        if v > 0 and E.seen.get(id(s), 0) < v:
            E.h.wait_ge(s, v); E.seen[id(s)] = v
        lst[i][1] = v + 16
        E.h.dma_start(out=out, in_=in_, **kw).then_inc(s, 16)
        ev = (s, v + 16)
        for t in reads:
            t.r = [e for e in t.r if e[0] is not ev[0]] + [ev]
        for t in writes:
            t.w = [ev]; t.r = []
        return ev

    def idma(self, out, in_, idx_ap, reads=(), writes=()):
        q = "pool"; E = self.E[q]
        self._waits(E, reads, writes)
        lst = self.dsem[q]; i = self.dnext[q]; self.dnext[q] = (i + 1) % len(lst)
        s, v = lst[i]
        if v > 0 and E.seen.get(id(s), 0) < v:
            E.h.wait_ge(s, v); E.seen[id(s)] = v
        lst[i][1] = v + 16
        E.h.indirect_dma_start(out=out, out_offset=None, in_=in_, in_offset=bass.IndirectOffsetOnAxis(ap=idx_ap, axis=0)).then_inc(s, 16)
        ev = (s, v + 16)
        for t in reads:
            t.r = [e for e in t.r if e[0] is not ev[0]] + [ev]
        for t in writes:
            t.w = [ev]; t.r = []
        return ev

    def barrier(self):
        for nm, E in self.E.items():
            for q in self.dsem:
                for s, v in self.dsem[q]:
                    if v > 0 and E.seen.get(id(s), 0) < v:
                        E.h.wait_ge(s, v); E.seen[id(s)] = v
            for nm2, E2 in self.E.items():
                if E2 is E or E2.count == 0: continue
                if E.seen.get(id(E2.sem), 0) < E2.count:
                    E.h.wait_ge(E2.sem, E2.count); E.seen[id(E2.sem)] = E2.count

    def finish(self):
        self.barrier()
        self.es.close()


class Cfg:
    def __init__(self, D=4096, TP=2048, NS=4, DFF=11008, NKV=3072, NPOOL=1280):
        self.D = D; self.TP = TP; self.NS = NS; self.DFF = DFF; self.NKV = NKV
        self.KC = D // 128; self.G = D // 16; self.Q = self.G // 2; self.FC = DFF // 128
        self.NT = TP + NS
        self.NPOOL = NPOOL
        self.tiles = [(i * 512, min(512, TP - i * 512), 0) for i in range((TP + 511) // 512)] + [(TP, NS, 1)]


def build(cfg):
    k = K(); nc = k.nc
    D, TP, NS, DFF, NKV, KC, G, Q, FC, NT = cfg.D, cfg.TP, cfg.NS, cfg.DFF, cfg.NKV, cfg.KC, cfg.G, cfg.Q, cfg.FC, cfg.NT
    tiles = cfg.tiles
    WT = 513

    def din(name, shape, dt=F32):
        return nc.dram_tensor(name, list(shape), dt, kind="ExternalInput").ap()
    def dout(name, shape, dt=F32):
        return nc.dram_tensor(name, list(shape), dt, kind="ExternalOutput").ap()
    def dscr(name, shape, dt=F32):
        return nc.dram_tensor(name, list(shape), dt, kind="Internal").ap()

    xp = din("xp", [TP, D]); xs = din("xs", [NS, D])
    st_re = din("st_re", [Q, 128]); st_im = din("st_im", [Q, 128])
    st_conv = din("st_conv", [2, DFF]); st_conv1 = din("st_conv1", [2, DFF])
    cache_win = din("cache_win", [512, 1024])
    NPOOL = cfg.NPOOL
    cache_kv = din("cache_kv", [NPOOL * 128, 2048]); page_tab = din("page_tab", [1, 128], I32)
    attn_g = din("attn_g", [D]); ffn_g = din("ffn_g", [D]); kv_g = din("kv_g", [D])
    attn_g1 = din("attn_g1", [D]); ffn_g1 = din("ffn_g1", [D]); fin_g = din("fin_g", [D])
    w_in1 = din("w_in1", [D, 2 * DFF]); w_down1 = din("w_down1", [DFF, D])
    conv_w1 = din("conv_w1", [3, DFF]); conv_b1 = din("conv_b1", [DFF])
    w_qg = din("w_qg", [D, D + 96]); w_o = din("w_o", [D, D])
    cmp_w1 = din("cmp_w1", [2, 4096, 256]); cmp_b1 = din("cmp_b1", [2, 256]); cmp_w2 = din("cmp_w2", [2, 256, 128])
    cmp_b2 = din("cmp_b2", [2, 128]); cmp_pe = din("cmp_pe", [2, 32, 128])
    lam_re = din("lam_re", [Q, 128]); lam_im = din("lam_im", [Q, 128]); log_step = din("log_step", [Q, 2])
    b_re = din("b_re", [G, 64, 16]); b_im = din("b_im", [G, 64, 16])
    c_re = din("c_re", [G * 16, 64]); c_im = din("c_im", [G * 16, 64])
    ssm_d = din("ssm_d", [D])
    w_glu = din("w_glu", [D, 2 * D]); w_in = din("w_in", [D, 2 * DFF]); w_down = din("w_down", [DFF, D])
    conv_w = din("conv_w", [3, DFF]); conv_b = din("conv_b", [DFF])
    w_kv = din("w_kv", [D, NKV])

    o_ssm_re_p = dout("o_ssm_re_p", [Q, 128]); o_ssm_im_p = dout("o_ssm_im_p", [Q, 128])
    o_ssm_re_s = dout("o_ssm_re_s", [Q, 128]); o_ssm_im_s = dout("o_ssm_im_s", [Q, 128])
    o_conv_p = dout("o_conv_p", [2, DFF]); o_conv_s = dout("o_conv_s", [2, DFF])
    o_conv_p1 = dout("o_conv_p1", [2, DFF]); o_conv_s1 = dout("o_conv_s1", [2, DFF])
    o_y_p = dout("o_y_p", [TP, D]); o_y_s = dout("o_y_s", [NS, D])
    o_kv_p = dout("o_kv_p", [TP, 2048]); o_kv_s = dout("o_kv_s", [NS, 2048])
    o_win_p = dout("o_win_p", [512, 1024]); o_win_s = dout("o_win_s", [512, 1024])

    xT = dscr("xT", [D, NT])
    zT = dscr("zT", [D, NT], BF16)
    actT = dscr("actT", [DFF, NT], BF16)
    t_xT = T(); t_zT = T(); t_act = T()
    xTv = xT.rearrange("(c p) t -> p c t", p=128)
    zTv = zT.rearrange("(c p) t -> p c t", p=128)
    actTv = actT.rearrange("(c p) t -> p c t", p=128)

    glob = contextlib.ExitStack()
    ident = k.sb(glob, [128, 128], F32); t_ident = T()
    ones = k.sb(glob, [128, 128], F32); t_ones = T()
    iot = k.sb(glob, [128, WT], F32); t_iot = T()
    iot_i = k.sb(glob, [128, WT], I32)
    halfpi = k.sb(glob, [128, 1], F32)
    k.op("pool", lambda h: h.memset(ident[:], 0.0), writes=[t_ident])
    k.op("pool", lambda h: h.affine_select(out=ident[:], in_=ident[:], pattern=[[-1, 128]], compare_op=ALU.not_equal,
                                           fill=1.0, base=0, channel_multiplier=1), reads=[t_ident], writes=[t_ident])
    k.op("pool", lambda h: h.memset(ones[:], 1.0), writes=[t_ones])
    k.op("pool", lambda h: h.iota(iot_i[:], pattern=[[1, WT]], base=0, channel_multiplier=0), writes=[t_iot])
    k.op("dve", lambda h: h.tensor_copy(iot[:], iot_i[:]), reads=[t_iot], writes=[t_iot])
    k.op("dve", lambda h: h.memset(halfpi[:], math.pi / 2), writes=[t_iot])
    epsc = k.sb(glob, [128, 1], F32)
    k.op("dve", lambda h: h.memset(epsc[:], 1e-6), writes=[t_iot])
    msel = k.sb(glob, [128, 4, 128], F32); t_msel = T()
    k.op("dve", lambda h: h.memset(msel[:], 0.0), writes=[t_msel])
    for j in range(4):
        k.op("dve", lambda h: h.memset(msel[0:64, j, 32 * j:32 * j + 16], 1.0), reads=[t_msel], writes=[t_msel])
        k.op("dve", lambda h: h.memset(msel[64:128, j, 32 * j + 16:32 * j + 32], 1.0), reads=[t_msel], writes=[t_msel])
    gains = k.sb(glob, [128, 6, KC], F32); t_gains = T()
    dskip = k.sb(glob, [128, KC], F32)
    cwt = k.sb(glob, [128, 2, 4, FC], F32); t_cw = T()
    natv = k.sb(glob, [128, 128], F32); t_natv = T()
    pbank = [k.ps(glob, [128, 512]) for _ in range(8)]
    t_pb = [T() for _ in range(8)]
    def load_cols(dst, t_dst, src, n):
        k.dma("sp", natv[0:n, :], src.rearrange("(c p) -> c p", p=128), writes=[t_natv])
        k.op("pe", lambda h: h.transpose(pbank[0][:, 0:n], natv[0:n, :], ident[0:n, 0:n]), reads=[t_natv, t_ident], writes=[t_pb[0]])
        k.op("dve", lambda h: h.tensor_copy(dst, pbank[0][:, 0:n]), reads=[t_pb[0]], writes=[t_dst])
    for i, g in enumerate((attn_g, ffn_g, kv_g, attn_g1, ffn_g1, fin_g)):
        load_cols(gains[:, i, :], t_gains, g, KC)
    load_cols(dskip[:], t_gains, ssm_d, KC)
    stc = k.sb(glob, [128, 2, FC, 2], F32); t_stc = T()
    convo = k.sb(glob, [128, 2, 2, FC], F32); t_convo = T()
    for L_, (cw_, cb_, sc_) in enumerate(((conv_w, conv_b, st_conv), (conv_w1, conv_b1, st_conv1))):
        for i in range(3):
            load_cols(cwt[:, L_, i, :], t_cw, cw_[i], FC)
        load_cols(cwt[:, L_, 3, :], t_cw, cb_, FC)
        for t_ in range(2):
            load_cols(stc[:, L_, :, t_], t_stc, sc_[t_], FC)

    with contextlib.ExitStack() as es:
        xin = [k.sb(es, [128, D], F32) for _ in range(2)]; t_xin = [T(), T()]
        xo = [k.sb(es, [128, KC, 128], F32) for _ in range(2)]; t_xo = [T(), T()]
        blocks = [(xp, i * 128, min(128, TP - i * 128), i * 128) for i in range((TP + 127) // 128)] + [(xs, 0, NS, TP)]
        for bi, (src, r0, nr, c0) in enumerate(blocks):
            b = bi % 2
            k.dma("sp", xin[b][0:nr, :], src[r0:r0 + nr, :], writes=[t_xin[b]])
            for c4 in range(0, KC, 4):
                pb = (c4 // 4) % 2
                n4 = min(4, KC - c4)
                for c in range(n4):
                    k.op("pe", lambda h: h.transpose(pbank[pb][:, c * 128:c * 128 + nr], xin[b][0:nr, (c4 + c) * 128:(c4 + c + 1) * 128], ident[0:nr, 0:nr]),
                         reads=[t_xin[b], t_ident], writes=[t_pb[pb]], inc=(c == n4 - 1))
                k.op("act", lambda h: h.copy(xo[b][:, c4:c4 + n4, 0:nr], pbank[pb][:, 0:n4 * 128].rearrange("p (c t) -> p c t", t=128)[:, :, 0:nr]),
                     reads=[t_pb[pb]], writes=[t_xo[b]])
            k.dma("sp", xTv[:, :, c0:c0 + nr], xo[b][:, :, 0:nr], reads=[t_xo[b]], writes=[t_xT])
    k.barrier()

    def rmsnorm(hbuf, t_h, gi, es):
        xt = [k.sb(es, [128, KC, 128], F32) for _ in range(2)]; t_xt = [T(), T()]
        sq = k.sb(es, [128, 256], F32); t_sq = T()
        rstd = k.sb(es, [128, 256], F32); t_rstd = T()
        ntile = 0
        for (c0, w, _s) in tiles:
            for h0 in range(0, w, 128):
                hw = min(128, w - h0); cc = c0 + h0
                b = ntile % 2; ntile += 1
                k.dma("sp", xt[b][:, :, 0:hw], xTv[:, :, cc:cc + hw], reads=[t_xT], writes=[t_xt[b]])
                pb = 2 + b
                for c in range(KC):
                    k.op("act", lambda h: h.activation(sq[:, 0:hw], xt[b][:, c, 0:hw], AF.Square), reads=[t_xt[b]], writes=[t_sq])
                    k.op("pe", lambda h: h.matmul(pbank[pb][:, 0:hw], ones[:], sq[:, 0:hw], start=(c == 0), stop=(c == KC - 1)),
                         reads=[t_sq, t_ones], writes=[t_pb[pb]])
                k.op("act", lambda h: h.activation(rstd[:, 0:hw], pbank[pb][:, 0:hw], AF.Sqrt, scale=1.0 / D, bias=epsc[:, 0:1]),
                     reads=[t_pb[pb], t_iot], writes=[t_rstd])
                k.op("dve", lambda h: h.reciprocal(rstd[:, 0:hw], rstd[:, 0:hw]),
                     reads=[t_rstd], writes=[t_rstd])
                for c in range(KC):
                    k.op("dve", lambda h: h.scalar_tensor_tensor(hbuf[:, c, cc:cc + hw], xt[b][:, c, 0:hw], gains[:, gi, c:c + 1], rstd[:, 0:hw], ALU.mult, ALU.mult),
                         reads=[t_xt[b], t_rstd, t_gains], writes=[t_h])

    def dense_ws(es, W, col_lists, hbuf, t_h, epilogue):
        nw = len(col_lists[0])
        wb = [[k.sb(es, [128, KC, 128], BF16) for _ in range(nw)] for _ in range(2)]
        t_wb = [[T() for _ in range(nw)] for _ in range(2)]
        Wv = W.rearrange("(c p) m -> p c m", p=128)
        cnt = 0
        for idx, cols in enumerate(col_lists):
            b = idx % 2
            for wi, col in enumerate(cols):
                k.dma("pool", wb[b][wi][:], Wv[:, :, col * 128:(col + 1) * 128], writes=[t_wb[b][wi]])
            for ti, (c0, w, _s) in enumerate(tiles):
                pbs = []
                for wi in range(nw):
                    pb = 4 + (cnt % 4); cnt += 1
                    for c in range(KC):
                        k.op("pe", lambda h: h.matmul(pbank[pb][:, 0:w], wb[b][wi][:, c, :], hbuf[:, c, c0:c0 + w], start=(c == 0), stop=(c == KC - 1)),
                             reads=[t_wb[b][wi], t_h], writes=[t_pb[pb]], inc=(c == KC - 1))
                    pbs.append(pb)
                epilogue(idx, ti, pbs)

    with contextlib.ExitStack() as es:
        hbuf = k.sb(es, [128, KC, NT], BF16); t_h = T()
        with contextlib.ExitStack() as es2:
            rmsnorm(hbuf, t_h, 0, es2)
        k.barrier()
        with contextlib.ExitStack() as es2:
            nat = {n: k.sb(es2, [128, 128], F32) for n in ("lr", "li", "dt", "a", "th", "r", "fr", "sn", "cs", "lbr", "lbi", "den", "kr", "ki", "t1", "t2")}
            nat_i = k.sb(es2, [128, 128], I32)
            t_nat = T()
            k.dma("sp", nat["lr"][0:Q, :], lam_re[:, :], writes=[t_nat])
            k.dma("sp", nat["li"][0:Q, :], lam_im[:, :], writes=[t_nat])
            k.dma("sp", nat["t2"][0:Q, 0:2], log_step[:, :], writes=[t_nat])
            for hf in range(2):
                k.op("dve", lambda h: h.tensor_copy(nat["dt"][0:Q, hf * 64:(hf + 1) * 64], nat["t2"][0:Q, hf:hf + 1].to_broadcast([Q, 64])), reads=[t_nat], writes=[t_nat])
            R = [t_nat]
            def dv(fn): k.op("dve", fn, reads=R, writes=R)
            def ac(fn): k.op("act", fn, reads=R, writes=R)
            N = lambda n: nat[n][0:Q, :]
            ac(lambda h: h.activation(N("dt"), N("dt"), AF.Exp))
            dv(lambda h: h.tensor_tensor(N("a"), N("lr"), N("dt"), ALU.mult))
            dv(lambda h: h.tensor_tensor(N("th"), N("li"), N("dt"), ALU.mult))
            ac(lambda h: h.activation(N("r"), N("a"), AF.Exp))
            dv(lambda h: h.tensor_scalar(N("th"), N("th"), 1.0 / TWO_PI, None, ALU.mult))
            dv(lambda h: h.tensor_copy(nat_i[0:Q, :], N("th")))
            dv(lambda h: h.tensor_copy(N("t1"), nat_i[0:Q, :]))
            dv(lambda h: h.tensor_tensor(N("fr"), N("th"), N("t1"), ALU.subtract))
            ac(lambda h: h.activation(N("sn"), N("fr"), AF.Sin, scale=TWO_PI))
            ac(lambda h: h.activation(N("t1"), N("fr"), AF.Abs))
            ac(lambda h: h.activation(N("cs"), N("t1"), AF.Sin, scale=-TWO_PI, bias=halfpi[0:Q, :]))
            dv(lambda h: h.tensor_tensor(N("lbr"), N("r"), N("cs"), ALU.mult))
            dv(lambda h: h.tensor_scalar(N("lbr"), N("lbr"), -1.0, None, ALU.add))
            dv(lambda h: h.tensor_tensor(N("lbi"), N("r"), N("sn"), ALU.mult))
            dv(lambda h: h.tensor_tensor(N("den"), N("lr"), N("lr"), ALU.mult))
            dv(lambda h: h.tensor_tensor(N("t1"), N("li"), N("li"), ALU.mult))
            dv(lambda h: h.tensor_tensor(N("den"), N("den"), N("t1"), ALU.add))
            dv(lambda h: h.reciprocal(N("den"), N("den")))
            dv(lambda h: h.tensor_tensor(N("t1"), N("lbr"), N("lr"), ALU.mult))
            dv(lambda h: h.tensor_tensor(N("t2"), N("lbi"), N("li"), ALU.mult))
            dv(lambda h: h.tensor_tensor(N("kr"), N("t1"), N("t2"), ALU.add))
            dv(lambda h: h.tensor_tensor(N("kr"), N("kr"), N("den"), ALU.mult))
            dv(lambda h: h.tensor_tensor(N("t1"), N("lbi"), N("lr"), ALU.mult))
            dv(lambda h: h.tensor_tensor(N("t2"), N("lbr"), N("li"), ALU.mult))
            dv(lambda h: h.tensor_tensor(N("ki"), N("t1"), N("t2"), ALU.subtract))
            dv(lambda h: h.tensor_tensor(N("ki"), N("ki"), N("den"), ALU.mult))
            PL = {n: k.sb(es2, [128, 128], F32) for n in ("r", "fr", "kr", "ki", "sre", "sim")}
            t_PL = T()
            k.dma("sp", nat["t1"][0:Q, :], st_re[:, :], reads=R, writes=R)
            k.dma("sp", nat["t2"][0:Q, :], st_im[:, :], reads=R, writes=R)
            for n, srcn in (("r", "r"), ("fr", "fr"), ("kr", "kr"), ("ki", "ki"), ("sre", "t1"), ("sim", "t2")):
                k.op("pe", lambda h: h.transpose(pbank[0][:, 0:Q], nat[srcn][0:Q, :], ident[0:Q, 0:Q]), reads=R + [t_ident], writes=[t_pb[0]])
                k.op("dve", lambda h: h.tensor_copy(PL[n][:, 0:Q], pbank[0][:, 0:Q]), reads=[t_pb[0]], writes=[t_PL])
            carry = k.sb(es2, [128, 2, 2, 128], F32); t_carry = T()
            fin = k.sb(es2, [128, 2, 2, 128], F32); t_fin = T()
            k.op("dve", lambda h: h.memset(carry[:], 0.0), writes=[t_carry])
            k.op("dve", lambda h: h.memset(fin[:], 0.0), writes=[t_fin])
            tmpc = k.sb(es2, [128, 4], F32); t_tmpc = T()
            cosT = k.sb(es2, [128, WT], F32); sinT = k.sb(es2, [128, WT], F32); prT = k.sb(es2, [128, WT], F32); piT = k.sb(es2, [128, WT], F32)
            mT = k.sb(es2, [128, WT], F32); mI = k.sb(es2, [128, WT], I32); t_tab = T(); t_tabw = T()
            nb = [k.sb(es2, [128, 128], F32) for _ in range(2)]; t_nb = T()
            ncn = [k.sb(es2, [128, 128], F32) for _ in range(2)]; t_ncn = T()
            xc = [k.sb(es2, [128, 128], F32) for _ in range(2)]; t_xc = T()
            mb = k.sb(es2, [128, 128], F32); t_mb = T()
            lB = [k.sb(es2, [128, 128], BF16) for _ in range(2)]; t_lB = T()
            lC = [k.sb(es2, [128, 128], BF16) for _ in range(3)]; t_lC = T()
            wk = {n: k.sb(es2, [128, 512], F32) for n in ("a", "b", "c", "d", "tre", "tim", "gre", "gim")}
            t_wk = {n: T() for n in wk}
            gk = {n: k.sb(es2, [128, 512], BF16) for n in ("g1", "g2", "g3", "g4")}; t_gk = {n: T() for n in gk}
            yb = k.sb(es2, [128, 512], F32); t_yb = T()
            zb = [k.sb(es2, [128, 512], BF16) for _ in range(2)]; t_zb = [T(), T()]
            ntl = len(tiles)
            assert ntl <= 5
            for kc in range(KC):
                for ri, bsrc in enumerate((b_re, b_im)):
                    for hf in range(2):
                        k.dma("sp", nb[ri][hf * 64:(hf + 1) * 64, :].rearrange("p (g n) -> p g n", n=16),
                              bsrc[kc * 8:(kc + 1) * 8].rearrange("g p n -> p g n"), writes=[t_nb])
                for ri, csrc in enumerate((c_re, c_im)):
                    for hf in range(2):
                        k.dma("sp", ncn[ri][:, hf * 64:(hf + 1) * 64], csrc[kc * 128:(kc + 1) * 128, :], writes=[t_ncn])
                for ri in range(2):
                    k.op("pe", lambda h: h.transpose(pbank[0][:, 0:128], ncn[ri][:], ident[:]), reads=[t_ncn, t_ident], writes=[t_pb[0]])
                    k.op("dve", lambda h: h.tensor_copy(xc[ri][:], pbank[0][:, 0:128]), reads=[t_pb[0]], writes=[t_xc])
                for j in range(4):
                    q = kc * 4 + j
                    k.op("dve", lambda h: h.tensor_scalar(mT[:], iot[:], PL["fr"][:, q:q + 1], None, ALU.mult), reads=[t_iot, t_PL], writes=[t_tabw])
                    k.op("dve", lambda h: h.tensor_copy(mI[:], mT[:]), reads=[t_tabw], writes=[t_tabw])
                    k.op("dve", lambda h: h.tensor_copy(prT[:], mI[:]), reads=[t_tabw, t_tab], writes=[t_tab])
                    k.op("dve", lambda h: h.tensor_tensor(mT[:], mT[:], prT[:], ALU.subtract), reads=[t_tabw, t_tab], writes=[t_tabw])
                    k.op("act", lambda h: h.activation(sinT[:], mT[:], AF.Sin, scale=TWO_PI), reads=[t_tabw, t_tab], writes=[t_tab])
                    k.op("act", lambda h: h.activation(mT[:], mT[:], AF.Abs), reads=[t_tabw, t_tab], writes=[t_tabw])
                    k.op("act", lambda h: h.activation(cosT[:], mT[:], AF.Sin, scale=-TWO_PI, bias=halfpi[:]), reads=[t_tabw, t_tab], writes=[t_tab])
                    k.op("dve", lambda h: h.tensor_scalar(prT[:], sinT[:], PL["ki"][:, q:q + 1], None, ALU.mult), reads=[t_tab, t_PL], writes=[t_tab])
                    k.op("dve", lambda h: h.scalar_tensor_tensor(prT[:], cosT[:], PL["kr"][:, q:q + 1], prT[:], ALU.mult, ALU.add), reads=[t_tab, t_PL], writes=[t_tab])
                    k.op("dve", lambda h: h.tensor_scalar(piT[:], sinT[:], PL["kr"][:, q:q + 1], None, ALU.mult), reads=[t_tab, t_PL], writes=[t_tab])
                    k.op("dve", lambda h: h.scalar_tensor_tensor(piT[:], cosT[:], PL["ki"][:, q:q + 1], piT[:], ALU.mult, ALU.subtract), reads=[t_tab, t_PL], writes=[t_tab])
                    for ri in range(2):
                        k.op("dve", lambda h: h.tensor_tensor(mb[:], nb[ri][:], msel[:, j, :], ALU.mult), reads=[t_nb, t_msel], writes=[t_mb])
                        k.op("pe", lambda h: h.transpose(pbank[0][:, 0:128], mb[:], ident[:]), reads=[t_mb, t_ident], writes=[t_pb[0]])
                        k.op("dve", lambda h: h.tensor_copy(lB[ri][:], pbank[0][:, 0:128]), reads=[t_pb[0]], writes=[t_lB])
                    k.op("dve", lambda h: h.tensor_tensor(lC[0][:], xc[0][:], msel[:, j, :], ALU.mult), reads=[t_xc, t_msel], writes=[t_lC])
                    k.op("dve", lambda h: h.scalar_tensor_tensor(lC[1][:], xc[0][:], -1.0, msel[:, j, :], ALU.mult, ALU.mult), reads=[t_xc, t_msel], writes=[t_lC])
                    k.op("dve", lambda h: h.scalar_tensor_tensor(lC[2][:], xc[1][:], -1.0, msel[:, j, :], ALU.mult, ALU.mult), reads=[t_xc, t_msel], writes=[t_lC])
                    for ti, (c0, w, sq_) in enumerate(tiles):
                        first = (ti == 0) or (tiles[ti - 1][2] != sq_)
                        last = (ti == ntl - 1) or (tiles[ti + 1][2] != sq_)
                        if first and sq_ == 1:
                            k.op("dve", lambda h: h.tensor_tensor(tmpc[:, 0:1], PL["sim"][:, q:q + 1], sinT[:, 1:2], ALU.mult), reads=[t_PL, t_tab], writes=[t_tmpc])
                            k.op("dve", lambda h: h.scalar_tensor_tensor(carry[:, 1, 0, q:q + 1], PL["sre"][:, q:q + 1], cosT[:, 1:2], tmpc[:, 0:1], ALU.mult, ALU.subtract), reads=[t_PL, t_tab, t_tmpc], writes=[t_carry])
                            k.op("dve", lambda h: h.tensor_tensor(tmpc[:, 1:2], PL["sre"][:, q:q + 1], sinT[:, 1:2], ALU.mult), reads=[t_PL, t_tab], writes=[t_tmpc])
                            k.op("dve", lambda h: h.scalar_tensor_tensor(carry[:, 1, 1, q:q + 1], PL["sim"][:, q:q + 1], cosT[:, 1:2], tmpc[:, 1:2], ALU.mult, ALU.add), reads=[t_PL, t_tab, t_tmpc], writes=[t_carry])
                        for ri in range(2):
                            k.op("pe", lambda h: h.matmul(pbank[1 + ri][:, 0:w], lB[ri][:], hbuf[:, kc, c0:c0 + w], start=True, stop=True),
                                 reads=[t_lB, t_h], writes=[t_pb[1 + ri]])
                        W_ = slice(0, w)
                        k.op("dve", lambda h: h.tensor_tensor(wk["a"][:, W_], pbank[1][:, W_], prT[:, W_], ALU.mult), reads=[t_pb[1], t_tab], writes=[t_wk["a"]])
                        k.op("dve", lambda h: h.tensor_tensor(wk["b"][:, W_], pbank[2][:, W_], piT[:, W_], ALU.mult), reads=[t_pb[2], t_tab], writes=[t_wk["b"]])
                        k.op("dve", lambda h: h.tensor_tensor(wk["c"][:, W_], pbank[2][:, W_], prT[:, W_], ALU.mult), reads=[t_pb[2], t_tab], writes=[t_wk["c"]])
                        k.op("dve", lambda h: h.tensor_tensor(wk["d"][:, W_], pbank[1][:, W_], piT[:, W_], ALU.mult), reads=[t_pb[1], t_tab], writes=[t_wk["d"]])
                        k.op("pool", lambda h: h.tensor_tensor(wk["tre"][:, W_], wk["a"][:, W_], wk["b"][:, W_], ALU.subtract), reads=[t_wk["a"], t_wk["b"]], writes=[t_wk["tre"]])
                        k.op("pool", lambda h: h.tensor_tensor(wk["tim"][:, W_], wk["c"][:, W_], wk["d"][:, W_], ALU.add), reads=[t_wk["c"], t_wk["d"]], writes=[t_wk["tim"]])
                        rb = PL["r"][:, q:q + 1].to_broadcast([128, w])
                        k.op("dve", lambda h: h.tensor_tensor_scan(wk["gre"][:, W_], rb, wk["tre"][:, W_], carry[:, sq_, 0, q:q + 1], ALU.mult, ALU.add),
                             reads=[t_wk["tre"], t_PL, t_carry], writes=[t_wk["gre"]])
                        k.op("dve", lambda h: h.tensor_tensor_scan(wk["gim"][:, W_], rb, wk["tim"][:, W_], carry[:, sq_, 1, q:q + 1], ALU.mult, ALU.add),
                             reads=[t_wk["tim"], t_PL, t_carry], writes=[t_wk["gim"]])
                        gl_re = wk["gre"][:, w - 1:w]; gl_im = wk["gim"][:, w - 1:w]
                        RG = [t_wk["gre"], t_wk["gim"], t_tab]
                        if not last:
                            k.op("dve", lambda h: h.tensor_tensor(tmpc[:, 0:1], gl_im, sinT[:, w:w + 1], ALU.mult), reads=RG, writes=[t_tmpc])
                            k.op("dve", lambda h: h.scalar_tensor_tensor(carry[:, sq_, 0, q:q + 1], gl_re, cosT[:, w:w + 1], tmpc[:, 0:1], ALU.mult, ALU.subtract), reads=RG + [t_tmpc], writes=[t_carry])
                            k.op("dve", lambda h: h.tensor_tensor(tmpc[:, 1:2], gl_re, sinT[:, w:w + 1], ALU.mult), reads=RG, writes=[t_tmpc])
                            k.op("dve", lambda h: h.scalar_tensor_tensor(carry[:, sq_, 1, q:q + 1], gl_im, cosT[:, w:w + 1], tmpc[:, 1:2], ALU.mult, ALU.add), reads=RG + [t_tmpc], writes=[t_carry])
                        else:
                            k.op("dve", lambda h: h.tensor_tensor(tmpc[:, 2:3], gl_im, sinT[:, w - 1:w], ALU.mult), reads=RG, writes=[t_tmpc])
                            k.op("dve", lambda h: h.scalar_tensor_tensor(fin[:, sq_, 0, q:q + 1], gl_re, cosT[:, w - 1:w], tmpc[:, 2:3], ALU.mult, ALU.subtract), reads=RG + [t_tmpc], writes=[t_fin])
                            k.op("dve", lambda h: h.tensor_tensor(tmpc[:, 3:4], gl_re, sinT[:, w - 1:w], ALU.mult), reads=RG, writes=[t_tmpc])
                            k.op("dve", lambda h: h.scalar_tensor_tensor(fin[:, sq_, 1, q:q + 1], gl_im, cosT[:, w - 1:w], tmpc[:, 3:4], ALU.mult, ALU.add), reads=RG + [t_tmpc], writes=[t_fin])
                        k.op("pool", lambda h: h.tensor_tensor(gk["g1"][:, W_], wk["gre"][:, W_], cosT[:, W_], ALU.mult), reads=[t_wk["gre"], t_tab], writes=[t_gk["g1"]])
                        k.op("pool", lambda h: h.tensor_tensor(gk["g2"][:, W_], wk["gim"][:, W_], sinT[:, W_], ALU.mult), reads=[t_wk["gim"], t_tab], writes=[t_gk["g2"]])
                        k.op("pool", lambda h: h.tensor_tensor(gk["g3"][:, W_], wk["gim"][:, W_], cosT[:, W_], ALU.mult), reads=[t_wk["gim"], t_tab], writes=[t_gk["g3"]])
                        k.op("pool", lambda h: h.tensor_tensor(gk["g4"][:, W_], wk["gre"][:, W_], sinT[:, W_], ALU.mult), reads=[t_wk["gre"], t_tab], writes=[t_gk["g4"]])
                        yb_ = 3 + ti
                        for mi, (lc, gn) in enumerate(((0, "g1"), (1, "g2"), (2, "g3"), (2, "g4"))):
                            k.op("pe", lambda h: h.matmul(pbank[yb_][:, W_], lC[lc][:], gk[gn][:, W_], start=(j == 0 and mi == 0), stop=(j == 3 and mi == 3)),
                                 reads=[t_lC, t_gk[gn]], writes=[t_pb[yb_]])
                for ti, (c0, w, sq_) in enumerate(tiles):
                    W_ = slice(0, w); yb_ = 3 + ti; zi = ti % 2
                    k.op("dve", lambda h: h.scalar_tensor_tensor(yb[:, W_], hbuf[:, kc, c0:c0 + w], dskip[:, kc:kc + 1], pbank[yb_][:, W_], ALU.mult, ALU.add),
                         reads=[t_h, t_gains, t_pb[yb_]], writes=[t_yb])
                    k.op("pool", lambda h: h.tensor_tensor(wk["a"][:, W_], yb[:, W_], yb[:, W_], ALU.mult), reads=[t_yb], writes=[t_wk["a"]])
                    k.op("pool", lambda h: h.tensor_scalar(wk["a"][:, W_], wk["a"][:, W_], 0.044715, 1.0, ALU.mult, ALU.add), reads=[t_wk["a"]], writes=[t_wk["a"]])
                    k.op("pool", lambda h: h.tensor_tensor(wk["a"][:, W_], wk["a"][:, W_], yb[:, W_], ALU.mult), reads=[t_wk["a"], t_yb], writes=[t_wk["a"]])
                    k.op("act", lambda h: h.activation(wk["b"][:, W_], wk["a"][:, W_], AF.Sigmoid, scale=2.0 * math.sqrt(2.0 / math.pi)), reads=[t_wk["a"]], writes=[t_wk["b"]])
                    k.op("pool", lambda h: h.tensor_tensor(zb[zi][:, W_], wk["b"][:, W_], yb[:, W_], ALU.mult), reads=[t_wk["b"], t_yb], writes=[t_zb[zi]])
                    k.dma("sp", zTv[:, kc, c0:c0 + w], zb[zi][:, W_], reads=[t_zb[zi]], writes=[t_zT])
            for sq_, (ore, oim) in enumerate(((o_ssm_re_p, o_ssm_im_p), (o_ssm_re_s, o_ssm_im_s))):
                for ri, od in enumerate((ore, oim)):
                    k.op("pe", lambda h: h.transpose(pbank[0][0:Q, 0:128], fin[:, sq_, ri, 0:Q], ident[:]), reads=[t_fin, t_ident], writes=[t_pb[0]])
                    k.op("dve", lambda h: h.tensor_copy(nat["t1"][0:Q, :], pbank[0][0:Q, 0:128]), reads=[t_pb[0]] + R, writes=R)
                    k.dma("sp", od[:, :], nat["t1"][0:Q, :], reads=R, writes=[])
        k.barrier()

        k.dma("sp", hbuf[:], zTv[:, :, :], reads=[t_zT], writes=[t_h])
        with contextlib.ExitStack() as es2:
            xc_ = [k.sb(es2, [128, 512], F32) for _ in range(2)]; t_xc_ = [T(), T()]
            sg = k.sb(es2, [128, 512], F32); t_sg = T()
            cnt = [0]
            def glu_epi(idx, ti, pbs):
                c0, w, _s = tiles[ti]; b = cnt[0] % 2; cnt[0] += 1
                k.dma("sp", xc_[b][:, 0:w], xTv[:, idx, c0:c0 + w], reads=[t_xT], writes=[t_xc_[b]])
                k.op("act", lambda h: h.activation(sg[:, 0:w], pbank[pbs[1]][:, 0:w], AF.Sigmoid), reads=[t_pb[pbs[1]]], writes=[t_sg])
                k.op("dve", lambda h: h.tensor_tensor(sg[:, 0:w], sg[:, 0:w], pbank[pbs[0]][:, 0:w], ALU.mult), reads=[t_sg, t_pb[pbs[0]]], writes=[t_sg])
                k.op("dve", lambda h: h.tensor_tensor(xc_[b][:, 0:w], xc_[b][:, 0:w], sg[:, 0:w], ALU.add), reads=[t_sg, t_xc_[b]], writes=[t_xc_[b]])
                k.dma("sp", xTv[:, idx, c0:c0 + w], xc_[b][:, 0:w], reads=[t_xc_[b]], writes=[t_xT])
            dense_ws(es2, w_glu, [[m, KC + m] for m in range(KC)], hbuf, t_h, glu_epi)
        k.barrier()

    wdS = dscr("wdS", [KC, 128, FC, 128], BF16); t_wdS = [T() for _ in range(KC)]

    def ffn_layer(L):
        gi = (1, 4)[L]
        w_in_L, w_down_L = (w_in, w_in1)[L], (w_down, w_down1)[L]
        o_cp, o_cs = ((o_conv_p, o_conv_s), (o_conv_p1, o_conv_s1))[L]
        with contextlib.ExitStack() as es:
            hbuf = k.sb(es, [128, KC, NT], BF16); t_h = T()
            with contextlib.ExitStack() as es2:
                rmsnorm(hbuf, t_h, gi, es2)
            k.barrier()
            with contextlib.ExitStack() as es2:
                gb = k.sb(es2, [128, 514], F32); t_gb = T()
                gc = k.sb(es2, [128, 512], F32); t_gc = T()
                ab_ = [k.sb(es2, [128, 512], BF16) for _ in range(2)]; t_ab = [T(), T()]
                halo = k.sb(es2, [128, 2, 2], F32); t_halo = T()
                cnt = [0]
                def up_epi(f, ti, pbs):
                    c0, w, sq_ = tiles[ti]; b = cnt[0] % 2; cnt[0] += 1
                    first = (ti == 0) or (tiles[ti - 1][2] != sq_)
                    last = (ti == len(tiles) - 1) or (tiles[ti + 1][2] != sq_)
                    if first:
                        if sq_ == 0:
                            k.op("dve", lambda h: h.memset(gb[:, 0:2], 0.0), writes=[t_gb])
                        else:
                            k.op("dve", lambda h: h.tensor_copy(gb[:, 0:2], stc[:, L, f, :]), reads=[t_stc], writes=[t_gb])
                    else:
                        k.op("dve", lambda h: h.tensor_copy(gb[:, 0:2], halo[:, sq_, :]), reads=[t_halo], writes=[t_gb])
                    k.op("act", lambda h: h.copy(gb[:, 2:2 + w], pbank[pbs[1]][:, 0:w]), reads=[t_pb[pbs[1]]], writes=[t_gb])
                    k.op("dve", lambda h: h.tensor_copy(halo[:, sq_, :], gb[:, w:w + 2]), reads=[t_gb], writes=[t_halo])
                    if last:
                        k.op("dve", lambda h: h.tensor_copy(convo[:, sq_, :, f], gb[:, w:w + 2]), reads=[t_gb], writes=[t_convo])
                    k.op("dve", lambda h: h.tensor_scalar(gc[:, 0:w], gb[:, 2:2 + w], cwt[:, L, 2, f:f + 1], cwt[:, L, 3, f:f + 1], ALU.mult, ALU.add), reads=[t_gb, t_cw], writes=[t_gc])
                    k.op("dve", lambda h: h.scalar_tensor_tensor(gc[:, 0:w], gb[:, 1:1 + w], cwt[:, L, 1, f:f + 1], gc[:, 0:w], ALU.mult, ALU.add), reads=[t_gb, t_cw, t_gc], writes=[t_gc])
                    k.op("dve", lambda h: h.scalar_tensor_tensor(gc[:, 0:w], gb[:, 0:w], cwt[:, L, 0, f:f + 1], gc[:, 0:w], ALU.mult, ALU.add), reads=[t_gb, t_cw, t_gc], writes=[t_gc])
                    k.op("act", lambda h: h.activation(gc[:, 0:w], gc[:, 0:w], AF.Silu), reads=[t_gc], writes=[t_gc])
                    k.op("dve", lambda h: h.tensor_tensor(ab_[b][:, 0:w], gc[:, 0:w], pbank[pbs[0]][:, 0:w], ALU.mult), reads=[t_gc, t_pb[pbs[0]]], writes=[t_ab[b]])
                    k.dma("sp", actTv[:, f, c0:c0 + w], ab_[b][:, 0:w], reads=[t_ab[b]], writes=[t_act])
                dense_ws(es2, w_in_L, [[f, FC + f] for f in range(FC)], hbuf, t_h, up_epi)
                for sq_, od in enumerate((o_cp, o_cs)):
                    for t_ in range(2):
                        k.op("pe", lambda h: h.transpose(pbank[0][0:FC, 0:128], convo[:, sq_, t_, :], ident[:]), reads=[t_convo, t_ident], writes=[t_pb[0]])
                        k.op("dve", lambda h: h.tensor_copy(natv[0:FC, :], pbank[0][0:FC, 0:128]), reads=[t_pb[0]], writes=[t_natv])
                        k.dma("sp", od[t_].rearrange("(c p) -> c p", p=128), natv[0:FC, :], reads=[t_natv], writes=[])
        k.barrier()
        with contextlib.ExitStack() as es:
            at = k.sb(es, [128, FC, 512], BF16); t_at = T()
            wd = [k.sb(es, [128, FC, 128], BF16) for _ in range(2)]; t_wd = [T(), T()]
            xc_ = [k.sb(es, [128, 512], F32) for _ in range(2)]; t_xc_ = [T(), T()]
            Wd = w_down_L.rearrange("(c p) m -> p c m", p=128)
            cnt = 0
            for ti, (c0, w, sq_) in enumerate(tiles):
                k.dma("sp", at[:, :, 0:w], actTv[:, :, c0:c0 + w], reads=[t_act], writes=[t_at])
                for m in range(KC):
                    b = cnt % 2; cnt += 1
                    if ti == 0:
                        k.dma("pool", wd[b][:], Wd[:, :, m * 128:(m + 1) * 128], reads=[t_wdS[m]], writes=[t_wd[b]])
                        k.dma("sp", wdS[m], wd[b][:], reads=[t_wd[b]], writes=[t_wdS[m]])
                    else:
                        k.dma("sp", wd[b][:], wdS[m], reads=[t_wdS[m]], writes=[t_wd[b]])
                    k.dma("sp", xc_[b][:, 0:w], xTv[:, m, c0:c0 + w], reads=[t_xT], writes=[t_xc_[b]])
                    pb = 4 + b
                    for f in range(FC):
                        k.op("pe", lambda h: h.matmul(pbank[pb][:, 0:w], wd[b][:, f, :], at[:, f, 0:w], start=(f == 0), stop=(f == FC - 1)),
                             reads=[t_wd[b], t_at], writes=[t_pb[pb]], inc=(f == FC - 1))
                    k.op("dve", lambda h: h.tensor_tensor(xc_[b][:, 0:w], xc_[b][:, 0:w], pbank[pb][:, 0:w], ALU.add), reads=[t_pb[pb], t_xc_[b]], writes=[t_xc_[b]])
                    k.dma("sp", xTv[:, m, c0:c0 + w], xc_[b][:, 0:w], reads=[t_xc_[b]], writes=[t_xT])
        k.barrier()

    ffn_layer(0)

    KT = dscr("KT", [4, 4, 128, TP], BF16); t_KT = T()
    Vtok = dscr("Vtok", [2, TP, 512], BF16); t_Vtok = T()
    kvS = dscr("kvS", [NS, NKV], F32); t_kvS = T()
    ktmap = {0: 0, 1: 1, 2: 2, 4: 3}; vmap = {3: 0, 5: 1}
    with contextlib.ExitStack() as es:
        hbuf = k.sb(es, [128, KC, NT], BF16); t_h = T()
        with contextlib.ExitStack() as es2:
            rmsnorm(hbuf, t_h, 2, es2)
        k.barrier()
        wkb = [k.sb(es, [128, KC, 256], BF16) for _ in range(2)]; t_wkb = [T(), T()]
        ob = [k.sb(es, [128, 256], F32) for _ in range(2)]; t_ob = [T(), T()]
        ktb = [k.sb(es, [128, 2, 128], BF16) for _ in range(2)]; t_ktb = [T(), T()]
        vb = [k.sb(es, [128, 256], BF16) for _ in range(2)]; t_vb = [T(), T()]
        Wk = w_kv.rearrange("(c p) m -> p c m", p=128)
        blocks = [(i * 128, min(128, TP - i * 128), 0) for i in range((TP + 127) // 128)] + [(TP, NS, 1)]
        cnt = 0
        k.dma("sp", o_win_s[0:512 - NS, :], cache_win[NS:512, :])
        for nb_ in range(NKV // 256):
            b = nb_ % 2; br = nb_ // 2; kp = nb_ % 2
            k.dma("pool", wkb[b][:], Wk[:, :, nb_ * 256:(nb_ + 1) * 256], writes=[t_wkb[b]])
            for (c0, nr, sq_) in blocks:
                ob_i = cnt % 2; pb = 4 + ob_i; cnt += 1
                for c in range(KC):
                    k.op("pe", lambda h: h.matmul(pbank[pb][0:nr, 0:256], hbuf[:, c, c0:c0 + nr], wkb[b][:, c, :], start=(c == 0), stop=(c == KC - 1)),
                         reads=[t_wkb[b], t_h], writes=[t_pb[pb]], inc=(c == KC - 1))
                k.op("act", lambda h: h.copy(ob[ob_i][0:nr, :], pbank[pb][0:nr, 0:256]), reads=[t_pb[pb]], writes=[t_ob[ob_i]])
                if sq_ == 1:
                    k.dma("sp", kvS[0:nr, nb_ * 256:(nb_ + 1) * 256], ob[ob_i][0:nr, :], reads=[t_ob[ob_i]], writes=[t_kvS])
                elif br in ktmap:
                    for i2 in range(2):
                        k.op("pe", lambda h: h.transpose(pbank[2 + ob_i][:, i2 * 128:i2 * 128 + nr], ob[ob_i][0:nr, i2 * 128:(i2 + 1) * 128], ident[0:nr, 0:nr]),
                             reads=[t_ob[ob_i], t_ident], writes=[t_pb[2 + ob_i]], inc=(i2 == 1))
                    k.op("dve", lambda h: h.tensor_copy(ktb[ob_i][:, :, 0:nr], pbank[2 + ob_i][:, 0:256].rearrange("p (a t) -> p a t", t=128)[:, :, 0:nr]),
                         reads=[t_pb[2 + ob_i]], writes=[t_ktb[ob_i]])
                    k.dma("sp", KT[ktmap[br], 2 * kp:2 * kp + 2, :, c0:c0 + nr].rearrange("a p t -> p a t"), ktb[ob_i][:, :, 0:nr], reads=[t_ktb[ob_i]], writes=[t_KT])
                else:
                    k.op("dve", lambda h: h.tensor_copy(vb[ob_i][0:nr, :], ob[ob_i][0:nr, :]), reads=[t_ob[ob_i]], writes=[t_vb[ob_i]])
                    k.dma("sp", Vtok[vmap[br], c0:c0 + nr, kp * 256:(kp + 1) * 256], vb[ob_i][0:nr, :], reads=[t_vb[ob_i]], writes=[t_Vtok])
                if nb_ < 8:
                    od = o_kv_p[c0:c0 + nr, nb_ * 256:(nb_ + 1) * 256] if sq_ == 0 else o_kv_s[0:nr, nb_ * 256:(nb_ + 1) * 256]
                    k.dma("sp", od, ob[ob_i][0:nr, :], reads=[t_ob[ob_i]])
                else:
                    wc = (nb_ - 8) * 256
                    if sq_ == 0:
                        lo = max(c0, TP - 512)
                        if c0 + nr > lo:
                            k.dma("sp", o_win_p[lo - (TP - 512):c0 + nr - (TP - 512), wc:wc + 256], ob[ob_i][lo - c0:nr, :], reads=[t_ob[ob_i]])
                    else:
                        k.dma("sp", o_win_s[512 - NS:512, wc:wc + 256], ob[ob_i][0:nr, :], reads=[t_ob[ob_i]])
    k.barrier()

    SCALE = 128 ** -0.5
    NEG = 30000.0
    BIG = 1.0e30
    NQB = TP // 128; NC_ = TP // 16 - 1; NB = TP // 64
    AXX = mybir.AxisListType.X
    QT = dscr("QT", [32, 128, NT], BF16); t_QT = T()
    OT = dscr("OT", [D, NT], BF16); t_OT = T()
    OTv = OT.rearrange("(c p) t -> p c t", p=128)
    gat = k.sb(glob, [128, NQB + 1, 96], F32); t_gat = T()
    ident_bf = k.sb(glob, [128, 128], BF16)
    k.op("dve", lambda h: h.tensor_copy(ident_bf[:], ident[:]), reads=[t_ident], writes=[t_ident])
    pT7 = pbank[7][:, :].bitcast(BF16)

    with contextlib.ExitStack() as es:
        hbuf = k.sb(es, [128, KC, NT], BF16); t_h = T()
        with contextlib.ExitStack() as es2:
            rmsnorm(hbuf, t_h, 3, es2)
        k.barrier()
        wg = k.sb(es, [128, KC, 96], BF16); t_wg = T()
        k.dma("pool", wg[:], w_qg.rearrange("(c p) m -> p c m", p=128)[:, :, D:D + 96], writes=[t_wg])
        blocks = [(i * 128, min(128, TP - i * 128)) for i in range(NQB)] + [(TP, NS)]
        for bi, (c0, nr) in enumerate(blocks):
            pb = 2 + bi % 2
            for c in range(KC):
                k.op("pe", lambda h: h.matmul(pbank[pb][0:nr, 0:96], hbuf[:, c, c0:c0 + nr], wg[:, c, :], start=(c == 0), stop=(c == KC - 1)),
                     reads=[t_wg, t_h], writes=[t_pb[pb]], inc=(c == KC - 1))
            k.op("act", lambda h: h.activation(gat[0:nr, bi, :], pbank[pb][0:nr, 0:96], AF.Sigmoid), reads=[t_pb[pb]], writes=[t_gat])
        qb_ = [k.sb(es, [128, 512], BF16) for _ in range(2)]; t_qb = [T(), T()]
        cnt = [0]
        def q_epi(idx, ti, pbs):
            c0, w, _s = tiles[ti]; b = cnt[0] % 2; cnt[0] += 1
            k.op("act", lambda h: h.copy(qb_[b][:, 0:w], pbank[pbs[0]][:, 0:w]), reads=[t_pb[pbs[0]]], writes=[t_qb[b]])
            k.dma("sp", QT[idx, :, c0:c0 + w], qb_[b][:, 0:w], reads=[t_qb[b]], writes=[t_QT])
        dense_ws(es, w_qg, [[m] for m in range(D // 128)], hbuf, t_h, q_epi)
    k.barrier()

    def gelu_to(dst, src, tmp_a, tmp_b, t_src, t_tmp, t_dst):
        k.op("pool", lambda h: h.tensor_tensor(tmp_a, src, src, ALU.mult), reads=[t_src], writes=[t_tmp])
        k.op("pool", lambda h: h.tensor_scalar(tmp_a, tmp_a, 0.044715, 1.0, ALU.mult, ALU.add), reads=[t_tmp], writes=[t_tmp])
        k.op("pool", lambda h: h.tensor_tensor(tmp_a, tmp_a, src, ALU.mult), reads=[t_tmp, t_src], writes=[t_tmp])
        k.op("act", lambda h: h.activation(tmp_b, tmp_a, AF.Sigmoid, scale=2.0 * math.sqrt(2.0 / math.pi)), reads=[t_tmp], writes=[t_tmp])
        k.op("pool", lambda h: h.tensor_tensor(dst, tmp_b, src, ALU.mult), reads=[t_tmp, t_src], writes=[t_dst])

    with contextlib.ExitStack() as es:
        cmpKT = k.sb(es, [128, 4, 128], BF16); t_cK = T()
        cmpV = k.sb(es, [128, 4, 128], BF16); t_cV = T()
        cmpKT_s = k.sb(es, [128, 4, 1024], BF16); cmpV_s = k.sb(es, [128, 8, 4, 128], BF16); t_cKs = T()
        idx = k.sb(es, [128, 128], I32); t_idx = T()
        idx_c = k.sb(es, [128, 128], I32); idx_k = [k.sb(es, [128, 128], I32) for _ in range(4)]; idx_v = [k.sb(es, [128, 128], I32) for _ in range(4)]
        cache_h = cache_kv.rearrange("r (s c) -> (r s) c", c=1024)
        cache_s = cache_kv.rearrange("r (s c) -> (r s) c", c=128)
        st = k.sb(es, [128, 8], F32); t_st = T()
        with contextlib.ExitStack() as es2:
            w1b = [k.sb(es2, [128, 32, 256], BF16) for _ in range(2)]; t_w1 = T()
            w2b = [k.sb(es2, [128, 2, 128], BF16) for _ in range(2)]; t_w2 = T()
            b1c = k.sb(es2, [128, 2, 2], F32); b2c = k.sb(es2, [128, 2], F32); t_bc = T()
            b2r = k.sb(es2, [1, 2, 128], F32)
            peT = k.sb(es2, [128, 2, 32], BF16); t_pe = T()
            cb = k.sb(es2, [128, 2, 2], F32); t_cb = T()
            for b in range(2):
                k.dma("pool", w1b[b][:], cmp_w1[b].rearrange("(t p) m -> p t m", p=128), writes=[t_w1])
                k.dma("pool", w2b[b][:], cmp_w2[b].rearrange("(c p) m -> p c m", p=128), writes=[t_w2])
                load_cols(b1c[:, b, :], t_bc, cmp_b1[b], 2)
                load_cols(b2c[:, b:b + 1], t_bc, cmp_b2[b], 1)
                k.dma("sp", b2r[0:1, b, :], cmp_b2[b:b + 1, :], writes=[t_bc])
                k.dma("sp", natv[0:32, :], cmp_pe[b], writes=[t_natv])
                k.op("pe", lambda h: h.transpose(pbank[0][:, 0:32], natv[0:32, :], ident[0:32, 0:32]), reads=[t_natv, t_ident], writes=[t_pb[0]])
                k.op("dve", lambda h: h.tensor_copy(peT[:, b, :], pbank[0][:, 0:32]), reads=[t_pb[0]], writes=[t_pe])
            for b in range(2):
                for hc in range(2):
                    for t_ in range(32):
                        k.op("pe", lambda h: h.matmul(pbank[0][:, 0:1], w1b[b][:, t_, hc * 128:(hc + 1) * 128], peT[:, b, t_:t_ + 1], start=(t_ == 0), stop=(t_ == 31)),
                             reads=[t_w1, t_pe], writes=[t_pb[0]], inc=(t_ == 31))
                    k.op("dve", lambda h: h.tensor_tensor(cb[:, b, hc:hc + 1], pbank[0][:, 0:1], b1c[:, b, hc:hc + 1], ALU.add), reads=[t_pb[0], t_bc], writes=[t_cb])
            xTt = k.sb(es2, [128, TP], BF16); t_xTt = T()
            pre = k.sb(es2, [128, 128], F32); t_pre = T()
            ta = k.sb(es2, [128, 128], F32); tb = k.sb(es2, [128, 128], F32); t_tt = T()
            hid = k.sb(es2, [128, 2, 128], BF16); t_hid = T()

            def compress(b, rhs_fn, t_src, nblk, dstK, dstV, t_dst):
                for hc in range(2):
                    for t_ in range(32):
                        k.op("pe", lambda h: h.matmul(pbank[1 + hc][:, 0:nblk], w1b[b][:, t_, hc * 128:(hc + 1) * 128], rhs_fn(t_), start=(t_ == 0), stop=(t_ == 31)),
                             reads=[t_w1, t_src], writes=[t_pb[1 + hc]], inc=(t_ == 31))
                    k.op("dve", lambda h: h.tensor_scalar(pre[:, 0:nblk], pbank[1 + hc][:, 0:nblk], cb[:, b, hc:hc + 1], None, ALU.add), reads=[t_pb[1 + hc], t_cb], writes=[t_pre])
                    gelu_to(hid[:, hc, 0:nblk], pre[:, 0:nblk], ta[:, 0:nblk], tb[:, 0:nblk], t_pre, t_tt, t_hid)
                if b == 0:
                    for hc in range(2):
                        k.op("pe", lambda h: h.matmul(pbank[3][:, 0:nblk], w2b[0][:, hc, :], hid[:, hc, 0:nblk], start=(hc == 0), stop=(hc == 1)),
                             reads=[t_w2, t_hid], writes=[t_pb[3]], inc=(hc == 1))
                    k.op("dve", lambda h: h.tensor_scalar(dstK, pbank[3][:, 0:nblk], b2c[:, 0:1], None, ALU.add), reads=[t_pb[3], t_bc], writes=[t_dst])
                else:
                    for hc in range(2):
                        k.op("pe", lambda h: h.matmul(pbank[3][0:nblk, 0:128], hid[:, hc, 0:nblk], w2b[1][:, hc, :], start=(hc == 0), stop=False),
                             reads=[t_w2, t_hid], writes=[t_pb[3]], inc=False)
                    k.op("pe", lambda h: h.matmul(pbank[3][0:nblk, 0:128], ones[0:1, 0:nblk], b2r[0:1, 1, :], start=False, stop=True),
                         reads=[t_ones, t_bc], writes=[t_pb[3]])
                    k.op("dve", lambda h: h.tensor_copy(dstV, pbank[3][0:nblk, 0:128]), reads=[t_pb[3]], writes=[t_dst])

            for b in range(2):
                for kvh in range(4):
                    k.dma("sp", xTt[:], KT[b, kvh], reads=[t_KT], writes=[t_xTt])
                    compress(b, lambda t_: xTt[:, t_:t_ + 16 * (NC_ - 1) + 1:16], t_xTt, NC_,
                             cmpKT[:, kvh, 0:NC_], cmpV[0:NC_, kvh, :], t_cK if b == 0 else t_cV)

            pt_i = k.sb(es2, [128, 128], I32); ptf = k.sb(es2, [128, 128], F32); pidxf = k.sb(es2, [128, 1], F32); pidxi = k.sb(es2, [128, 1], I32)
            k.dma("sp", pt_i[:], page_tab[0:1, :].to_broadcast([128, 128]), writes=[t_idx])
            k.op("pool", lambda h: h.iota(pidxi[:], pattern=[[0, 1]], base=0, channel_multiplier=1), writes=[t_idx])
            k.op("dve", lambda h: h.tensor_copy(pidxf[:], pidxi[:]), reads=[t_idx], writes=[t_idx])
            k.op("dve", lambda h: h.tensor_copy(ptf[:], pt_i[:]), reads=[t_idx], writes=[t_idx])
            k.op("dve", lambda h: h.tensor_scalar(ptf[:], ptf[:], 128.0, pidxf[:, 0:1], ALU.mult, ALU.add), reads=[t_idx], writes=[t_idx])
            k.op("dve", lambda h: h.tensor_copy(idx[:], ptf[:]), reads=[t_idx], writes=[t_idx])
            ptf2 = k.sb(es2, [128, 128], F32)
            k.op("dve", lambda h: h.tensor_scalar(ptf2[:], ptf[:], 2.0, None, ALU.mult), reads=[t_idx], writes=[t_idx])
            k.op("dve", lambda h: h.tensor_copy(idx_c[:], ptf2[:]), reads=[t_idx], writes=[t_idx])
            for kvh in range(4):
                k.op("dve", lambda h: h.tensor_scalar(ptf2[:], ptf[:], 16.0, float(8 + kvh), ALU.mult, ALU.add), reads=[t_idx], writes=[t_idx])
                k.op("dve", lambda h: h.tensor_copy(idx_k[kvh][:], ptf2[:]), reads=[t_idx], writes=[t_idx])
                k.op("dve", lambda h: h.tensor_scalar(ptf2[:], ptf[:], 16.0, float(12 + kvh), ALU.mult, ALU.add), reads=[t_idx], writes=[t_idx])
                k.op("dve", lambda h: h.tensor_copy(idx_v[kvh][:], ptf2[:]), reads=[t_idx], writes=[t_idx])
            pgb = [k.sb(es2, [128, 1024], F32) for _ in range(4)]; t_pgb = [T() for _ in range(4)]
            XT = k.sb(es2, [128, 8, 2064], BF16); t_XT = T()
            k.op("dve", lambda h: h.memset(XT[:, :, 0:16], 0.0), writes=[t_XT])
            for gi in range(8):
                for pl in range(16):
                    pg_ = gi * 16 + pl; b2 = pg_ % 4
                    k.idma(pgb[b2][:], cache_h, idx_c[:, pg_:pg_ + 1], reads=[t_idx], writes=[t_pgb[b2]])
                    for b4 in range(2):
                        pbx = 4 + b4
                        for c in range(4):
                            bk = b4 * 4 + c
                            k.op("pe", lambda h: h.transpose(pbank[pbx][:, c * 128:(c + 1) * 128], pgb[b2][:, bk * 128:(bk + 1) * 128], ident[:]),
                                 reads=[t_pgb[b2], t_ident], writes=[t_pb[pbx]], inc=(c == 3))
                        k.op("act" if b4 == 0 else "dve", lambda h: (h.copy if b4 == 0 else h.tensor_copy)(XT[:, b4 * 4:b4 * 4 + 4, 16 + pl * 128:16 + (pl + 1) * 128], pbank[pbx][:, 0:512].rearrange("p (a t) -> p a t", t=128)),
                             reads=[t_pb[pbx]], writes=[t_XT])
                nblk = 127 if gi == 0 else 128; j0 = 1 if gi == 0 else 0; cbase = 0 if gi == 0 else 128 * gi - 1
                for b in range(2):
                    for kvh in range(4):
                        bk = b * 4 + kvh
                        compress(b, lambda t_: XT[:, bk, t_ + 16 * j0:t_ + 16 * j0 + 16 * (nblk - 1) + 1:16], t_XT, nblk,
                                 cmpKT_s[:, kvh, cbase:cbase + nblk], cmpV_s[0:nblk, gi, kvh, :], t_cKs)
                k.op("dve", lambda h: h.tensor_copy(XT[:, :, 0:16], XT[:, :, 2048:2064]), reads=[t_XT], writes=[t_XT])
        k.barrier()

        esA = contextlib.ExitStack()
        cm01 = k.sb(esA, [128, NQB, NC_], F32); cmadd = k.sb(esA, [128, NQB, NC_], F32); t_cm = T()
        keep = k.sb(esA, [128, NQB, NB], F32); over = k.sb(esA, [128, NQB, NB], F32); t_ko = T()
        tri = k.sb(esA, [128, 128], F32); W640 = k.sb(esA, [128, 640], F32); t_tri = T()
        ovl = k.sb(esA, [128, NB], F32); t_ovl = T()
        esM = contextlib.ExitStack()
        mi = k.sb(esM, [128, NQB * 128], I32); t_mi = T()
        dT = k.sb(esM, [128, NQB, NB], F32); fT = k.sb(esM, [128, NQB, NB], F32); pidx = k.sb(esM, [128, 1], F32)
        wtmp = k.sb(esM, [128, 640], F32); ovt = k.sb(esM, [128, NB], F32)
        k.op("pool", lambda h: h.iota(mi[:, 0:NQB * NC_].rearrange("p (a c) -> p a c", c=NC_), pattern=[[128, NQB], [-16, NC_]], base=-31, channel_multiplier=1), writes=[t_mi])
        k.op("dve", lambda h: h.tensor_copy(cm01[:], mi[:, 0:NQB * NC_].rearrange("p (a c) -> p a c", c=NC_)), reads=[t_mi], writes=[t_cm])
        k.op("dve", lambda h: h.tensor_single_scalar(cm01[:], cm01[:], 0.0, ALU.is_ge), reads=[t_cm], writes=[t_cm])
        k.op("dve", lambda h: h.tensor_scalar(cmadd[:], cm01[:], NEG, -NEG, ALU.mult, ALU.add), reads=[t_cm], writes=[t_cm])
        V3 = lambda: mi[:, 0:NQB * NB].rearrange("p (a c) -> p a c", c=NB)
        k.op("pool", lambda h: h.iota(mi[:, 0:1], pattern=[[0, 1]], base=0, channel_multiplier=1), reads=[t_cm], writes=[t_mi])
        k.op("dve", lambda h: h.tensor_copy(pidx[:], mi[:, 0:1]), reads=[t_mi], writes=[t_ko])
        k.op("dve", lambda h: h.tensor_single_scalar(pidx[:], pidx[:], 64.0, ALU.is_ge), reads=[t_ko], writes=[t_ko])
        k.op("pool", lambda h: h.iota(V3(), pattern=[[-2, NQB], [1, NB]], base=0, channel_multiplier=0), reads=[t_ko], writes=[t_mi])
        k.op("dve", lambda h: h.tensor_copy(dT[:], V3()), reads=[t_mi], writes=[t_ko])
        k.op("dve", lambda h: h.tensor_scalar(dT[:], dT[:], pidx[:, 0:1], None, ALU.subtract), reads=[t_ko], writes=[t_ko])
        k.op("dve", lambda h: h.tensor_single_scalar(over[:], dT[:], 0.0, ALU.is_gt), reads=[t_ko], writes=[t_ko])
        k.op("dve", lambda h: h.tensor_single_scalar(keep[:], dT[:], 0.0, ALU.is_equal), reads=[t_ko], writes=[t_ko])
        k.op("dve", lambda h: h.tensor_single_scalar(fT[:], dT[:], -1.0, ALU.is_equal), reads=[t_ko], writes=[t_ko])
        k.op("dve", lambda h: h.tensor_tensor(keep[:], keep[:], fT[:], ALU.max), reads=[t_ko], writes=[t_ko])
        k.op("pool", lambda h: h.iota(V3(), pattern=[[0, NQB], [1, NB]], base=0, channel_multiplier=0), reads=[t_ko], writes=[t_mi])
        k.op("dve", lambda h: h.tensor_copy(fT[:], V3()), reads=[t_mi], writes=[t_ko])
        k.op("dve", lambda h: h.tensor_single_scalar(fT[:], fT[:], 0.0, ALU.is_equal), reads=[t_ko], writes=[t_ko])
        k.op("dve", lambda h: h.tensor_tensor(keep[:], keep[:], fT[:], ALU.max), reads=[t_ko], writes=[t_ko])
        k.op("dve", lambda h: h.tensor_tensor(fT[:], keep[:], over[:], ALU.subtract), reads=[t_ko], writes=[t_ko])
        k.op("dve", lambda h: h.tensor_tensor(keep[:], keep[:], over[:], ALU.add), reads=[t_ko], writes=[t_ko])
        k.op("dve", lambda h: h.tensor_scalar(keep[:], keep[:], -1.0, 1.0, ALU.mult, ALU.add), reads=[t_ko], writes=[t_ko])
        k.op("dve", lambda h: h.tensor_scalar(over[:], fT[:], BIG, None, ALU.mult), reads=[t_ko], writes=[t_ko])
        k.op("pool", lambda h: h.iota(mi[:, 0:128], pattern=[[-1, 128]], base=0, channel_multiplier=1), reads=[t_ko], writes=[t_mi])
        k.op("dve", lambda h: h.tensor_copy(tri[:], mi[:, 0:128]), reads=[t_mi], writes=[t_tri])
        k.op("dve", lambda h: h.tensor_single_scalar(tri[:], tri[:], 0.0, ALU.is_ge), reads=[t_tri], writes=[t_tri])
        k.op("dve", lambda h: h.tensor_scalar(tri[:], tri[:], NEG, -NEG, ALU.mult, ALU.add), reads=[t_tri], writes=[t_tri])
        k.op("pool", lambda h: h.iota(mi[:, 0:640], pattern=[[-1, 640]], base=512, channel_multiplier=1), reads=[t_tri], writes=[t_mi])
        k.op("dve", lambda h: h.tensor_copy(W640[:], mi[:, 0:640]), reads=[t_mi], writes=[t_tri])
        k.op("dve", lambda h: h.tensor_single_scalar(W640[:], W640[:], 0.0, ALU.is_ge), reads=[t_tri], writes=[t_tri])
        k.op("pool", lambda h: h.iota(mi[:, 0:640], pattern=[[1, 640]], base=-1, channel_multiplier=-1), reads=[t_tri], writes=[t_mi])
        k.op("dve", lambda h: h.tensor_copy(wtmp[:], mi[:, 0:640]), reads=[t_mi], writes=[t_tri])
        k.op("dve", lambda h: h.tensor_single_scalar(wtmp[:], wtmp[:], 0.0, ALU.is_ge), reads=[t_tri], writes=[t_tri])
        k.op("dve", lambda h: h.tensor_tensor(W640[:], W640[:], wtmp[:], ALU.mult), reads=[t_tri], writes=[t_tri])
        k.op("dve", lambda h: h.tensor_scalar(W640[:], W640[:], NEG, -NEG, ALU.mult, ALU.add), reads=[t_tri], writes=[t_tri])
        k.op("pool", lambda h: h.iota(mi[:, 0:NB], pattern=[[64, NB]], base=64, channel_multiplier=-16), reads=[t_tri], writes=[t_mi])
        k.op("dve", lambda h: h.tensor_copy(ovl[:], mi[:, 0:NB]), reads=[t_mi], writes=[t_ovl])
        k.op("dve", lambda h: h.tensor_single_scalar(ovl[:], ovl[:], 0.0, ALU.is_gt), reads=[t_ovl], writes=[t_ovl])
        k.op("pool", lambda h: h.iota(mi[:, 0:NB], pattern=[[-64, NB]], base=32, channel_multiplier=16), reads=[t_ovl], writes=[t_mi])
        k.op("dve", lambda h: h.tensor_copy(ovt[:], mi[:, 0:NB]), reads=[t_mi], writes=[t_ovl])
        k.op("dve", lambda h: h.tensor_single_scalar(ovt[:], ovt[:], 0.0, ALU.is_gt), reads=[t_ovl], writes=[t_ovl])
        k.op("dve", lambda h: h.tensor_tensor(ovl[:], ovl[:], ovt[:], ALU.mult), reads=[t_ovl], writes=[t_ovl])

        k.barrier()
        esM.close()
        KsT = k.sb(esA, [128, 4, TP], BF16); KwT = k.sb(esA, [128, 4, TP], BF16); t_Kx = T()
        Vs = k.sb(esA, [128, NQB, 512], BF16); Vw = k.sb(esA, [128, NQB, 512], BF16); t_Vx = T()
        for kvh in range(4):
            k.dma("sp", KsT[:, kvh, :], KT[2, kvh], reads=[t_KT], writes=[t_Kx])
            k.dma("sp", KwT[:, kvh, :], KT[3, kvh], reads=[t_KT], writes=[t_Kx])
        k.dma("sp", Vs[:], Vtok[0].rearrange("(b p) m -> p b m", p=128), reads=[t_Vtok], writes=[t_Vx])
        k.dma("sp", Vw[:], Vtok[1].rearrange("(b p) m -> p b m", p=128), reads=[t_Vtok], writes=[t_Vx])
        _q = k.sb(esA, [128, 32, 128], BF16); _tq = T(); QTb = [_q, _q]; t_QTb = [_tq, _tq]
        _o = k.sb(esA, [128, 32, 128], BF16); _to = T(); OTb = [_o, _o]; t_OTb = [_to, _to]
        sm2 = [k.sb(esA, [128, TP], F32) for _ in range(2)]; t_sm2 = [T(), T()]
        pg2 = [k.sb(esA, [128, TP], BF16) for _ in range(2)]; t_pg2 = [T(), T()]
        smw2 = [k.sb(esA, [128, 640], F32) for _ in range(2)]; t_smw2 = [T(), T()]
        pgw2 = [k.sb(esA, [128, 640], BF16) for _ in range(2)]; t_pgw2 = [T(), T()]
        PTs2 = [k.sb(esA, [128, NQB, 128], BF16) for _ in range(2)]; t_PTs2b = [T(), T()]
        PTw2 = [k.sb(esA, [128, 5, 128], BF16) for _ in range(2)]; t_PTw2 = [T(), T()]
        pcT = k.sb(esA, [128, 8, 128], BF16); t_pcT = T()
        smc = k.sb(esA, [128, 128], F32); exc = k.sb(esA, [128, 128], F32); t_smc = T()
        pgc = k.sb(esA, [128, 128], BF16); t_pgc = T()
        psP = k.sb(esA, [128, 128], F32); t_psP = T()
        psT = k.sb(esA, [128, 128], F32); t_psT = T()
        impm = k.sb(esA, [128, NB], F32); impw = k.sb(esA, [128, NB], F32); m8 = k.sb(esA, [128, 16], F32); t_imp = T()
        selb = k.sb(esA, [128, NB], F32); t_selb = T()
        bfull = k.sb(esA, [128, TP], F32); t_bf = T()

        def softmax_gated(sm_ap, ex_ap, pg_ap, gate_ap, nq, t_s, t_e, t_p):
            k.op("dve", lambda h: h.reduce_max(st[0:nq, 0:1], sm_ap, AXX), reads=[t_s], writes=[t_st])
            k.op("dve", lambda h: h.tensor_scalar(st[0:nq, 1:2], st[0:nq, 0:1], -1.0, None, ALU.mult), reads=[t_st], writes=[t_st])
            k.op("act", lambda h: h.activation(ex_ap, sm_ap, AF.Exp, bias=st[0:nq, 1:2], scale=1.0, accum_out=st[0:nq, 2:3]), reads=[t_s, t_st], writes=[t_e, t_st])
            k.op("dve", lambda h: h.reciprocal(st[0:nq, 3:4], st[0:nq, 2:3]), reads=[t_st], writes=[t_st])
            k.op("dve", lambda h: h.tensor_tensor(st[0:nq, 4:5], st[0:nq, 3:4], gate_ap, ALU.mult), reads=[t_st, t_gat], writes=[t_st])
            k.op("pool", lambda h: h.tensor_scalar(pg_ap, ex_ap, st[0:nq, 4:5], None, ALU.mult), reads=[t_e, t_st], writes=[t_p])

        def transposes(dst, t_dst, src, t_src, nblk, nq):
            for j0 in range(0, nblk, 4):
                n4 = min(4, nblk - j0)
                for j in range(n4):
                    k.op("pe", lambda h: h.transpose(pT7[:, j * 128:j * 128 + nq], src[0:nq, (j0 + j) * 128:(j0 + j + 1) * 128], ident_bf[0:nq, 0:nq]),
                         reads=[t_src, t_ident], writes=[t_pb[7]], inc=(j == n4 - 1))
                k.op("act", lambda h: h.copy(dst[:, j0:j0 + n4, 0:nq], pT7[:, 0:n4 * 128].rearrange("p (a t) -> p a t", t=128)[:, :, 0:nq]),
                     reads=[t_pb[7]], writes=[t_dst])

        for qb in range(NQB):
            qi = qb % 2; nk = 128 * (qb + 1); nq = 128
            k.dma("sp", QTb[qi][:], QT[:, :, qb * 128:(qb + 1) * 128].rearrange("h p t -> p h t"), reads=[t_QT], writes=[t_QTb[qi]])
            for kvh in range(4):
                for g in range(8):
                    hh = kvh * 8 + g
                    k.op("pe", lambda h: h.matmul(pbank[0][:, 0:NC_], QTb[qi][:, hh, :], cmpKT[:, kvh, 0:NC_], start=True, stop=True), reads=[t_QTb[qi], t_cK], writes=[t_pb[0]])
                    k.op("dve", lambda h: h.scalar_tensor_tensor(smc[:, 0:NC_], pbank[0][:, 0:NC_], SCALE, cmadd[:, qb, :], ALU.mult, ALU.add), reads=[t_pb[0], t_cm], writes=[t_smc])
                    k.op("dve", lambda h: h.reduce_max(st[:, 0:1], smc[:, 0:NC_], AXX), reads=[t_smc], writes=[t_st])
                    k.op("dve", lambda h: h.tensor_scalar(st[:, 1:2], st[:, 0:1], -1.0, None, ALU.mult), reads=[t_st], writes=[t_st])
                    k.op("act", lambda h: h.activation(exc[:, 0:NC_], smc[:, 0:NC_], AF.Exp, bias=st[:, 1:2], scale=1.0), reads=[t_smc, t_st], writes=[t_smc])
                    k.op("dve", lambda h: h.tensor_tensor(exc[:, 0:NC_], exc[:, 0:NC_], cm01[:, qb, :], ALU.mult), reads=[t_smc, t_cm], writes=[t_smc])
                    k.op("dve", lambda h: h.reduce_sum(st[:, 2:3], exc[:, 0:NC_], AXX), reads=[t_smc], writes=[t_st])
                    k.op("dve", lambda h: h.tensor_scalar(st[:, 2:3], st[:, 2:3], 1e-30, None, ALU.max), reads=[t_st], writes=[t_st])
                    k.op("dve", lambda h: h.reciprocal(st[:, 3:4], st[:, 2:3]), reads=[t_st], writes=[t_st])
                    k.op("dve", lambda h: h.tensor_scalar(exc[:, 0:NC_], exc[:, 0:NC_], st[:, 3:4], None, ALU.mult), reads=[t_smc, t_st], writes=[t_smc])
                    if g == 0:
                        k.op("pool", lambda h: h.tensor_copy(psP[:, 0:NC_], exc[:, 0:NC_]), reads=[t_smc], writes=[t_psP])
                    else:
                        k.op("pool", lambda h: h.tensor_tensor(psP[:, 0:NC_], psP[:, 0:NC_], exc[:, 0:NC_], ALU.add), reads=[t_smc, t_psP], writes=[t_psP])
                    k.op("dve", lambda h: h.memset(pgc[:], 0.0), writes=[t_pgc])
                    k.op("dve", lambda h: h.tensor_scalar(pgc[:, 0:NC_], exc[:, 0:NC_], gat[:, qb, hh * 3:hh * 3 + 1], None, ALU.mult), reads=[t_smc, t_gat], writes=[t_pgc])
                    k.op("pe", lambda h: h.transpose(pT7[:, 0:128], pgc[:, :], ident_bf[:]), reads=[t_pgc, t_ident], writes=[t_pb[7]])
                    k.op("act", lambda h: h.copy(pcT[:, g, :], pT7[:, 0:128]), reads=[t_pb[7]], writes=[t_pcT])
                k.op("dve", lambda h: h.memset(psT[:], 0.0), writes=[t_psT])
                k.op("pe", lambda h: h.transpose(pbank[1][0:NC_, 0:128], psP[:, 0:NC_], ident[:]), reads=[t_psP, t_ident], writes=[t_pb[1]])
                k.op("dve", lambda h: h.tensor_copy(psT[0:NC_, :], pbank[1][0:NC_, 0:128]), reads=[t_pb[1]], writes=[t_psT])
                k.op("pe", lambda h: h.matmul(pbank[1][:, 256:256 + NB], psT[0:NC_, :], ovl[0:NC_, :], start=True, stop=True), reads=[t_psT, t_ovl], writes=[t_pb[1]])
                k.op("dve", lambda h: h.tensor_tensor(impm[:], pbank[1][:, 256:256 + NB], keep[:, qb, :], ALU.mult), reads=[t_pb[1], t_ko], writes=[t_imp])
                k.op("dve", lambda h: h.tensor_tensor(impm[:], impm[:], over[:, qb, :], ALU.add), reads=[t_imp, t_ko], writes=[t_imp])
                k.op("dve", lambda h: h.max(m8[:, 0:8], impm[:]), reads=[t_imp], writes=[t_imp])
                k.op("dve", lambda h: h.match_replace(impw[:], m8[:, 0:8], impm[:], -3.0e38), reads=[t_imp], writes=[t_imp])
                k.op("dve", lambda h: h.max(m8[:, 8:16], impw[:]), reads=[t_imp], writes=[t_imp])
                k.op("dve", lambda h: h.tensor_scalar(selb[:], impm[:], m8[:, 15:16], None, ALU.is_ge), reads=[t_imp], writes=[t_selb])
                k.op("dve", lambda h: h.tensor_scalar(selb[:], selb[:], NEG, -NEG, ALU.mult, ALU.add), reads=[t_selb], writes=[t_selb])
                nbk = 2 * (qb + 1)
                k.op("dve", lambda h: h.tensor_copy(bfull[:, 0:nk].rearrange("p (j c) -> p j c", c=64), selb[:, 0:nbk].unsqueeze(2).to_broadcast([128, nbk, 64])), reads=[t_selb], writes=[t_bf])
                k.op("dve", lambda h: h.tensor_tensor(bfull[:, nk - 128:nk], bfull[:, nk - 128:nk], tri[:], ALU.add), reads=[t_bf, t_tri], writes=[t_bf])
                kb0 = max(0, qb - 4); nkw = 128 * (qb + 1 - kb0); woff = 640 - nkw; nwb = qb + 1 - kb0
                for g in range(8):
                    hh = kvh * 8 + g
                    sm = sm2[g % 2]; t_sm = t_sm2[g % 2]; pg = pg2[g % 2]; t_pg = t_pg2[g % 2]
                    smw = smw2[g % 2]; t_smw = t_smw2[g % 2]; pgw = pgw2[g % 2]; t_pgw = t_pgw2[g % 2]
                    PTs = PTs2[g % 2]; t_PTs = t_PTs2b[g % 2]; PTw = PTw2[g % 2]; t_PTw = t_PTw2[g % 2]
                    for ci, ch in enumerate(range(0, nk, 512)):
                        w = min(512, nk - ch); pb = 2 + ci % 2
                        k.op("pe", lambda h: h.matmul(pbank[pb][:, 0:w], QTb[qi][:, hh, :], KsT[:, kvh, ch:ch + w], start=True, stop=True), reads=[t_QTb[qi], t_Kx], writes=[t_pb[pb]])
                        k.op("dve", lambda h: h.scalar_tensor_tensor(sm[:, ch:ch + w], pbank[pb][:, 0:w], SCALE, bfull[:, ch:ch + w], ALU.mult, ALU.add), reads=[t_pb[pb], t_bf], writes=[t_sm])
                    softmax_gated(sm[:, 0:nk], sm[:, 0:nk], pg[:, 0:nk], gat[:, qb, hh * 3 + 1:hh * 3 + 2], 128, t_sm, t_sm, t_pg)
                    for ci, ch in enumerate(range(0, nkw, 512)):
                        w = min(512, nkw - ch); pb = 4 + ci % 2
                        k.op("pe", lambda h: h.matmul(pbank[pb][:, 0:w], QTb[qi][:, hh, :], KwT[:, kvh, kb0 * 128 + ch:kb0 * 128 + ch + w], start=True, stop=True), reads=[t_QTb[qi], t_Kx], writes=[t_pb[pb]])
                        k.op("dve", lambda h: h.scalar_tensor_tensor(smw[:, ch:ch + w], pbank[pb][:, 0:w], SCALE, W640[:, woff + ch:woff + ch + w], ALU.mult, ALU.add), reads=[t_pb[pb], t_tri], writes=[t_smw])
                    softmax_gated(smw[:, 0:nkw], smw[:, 0:nkw], pgw[:, 0:nkw], gat[:, qb, hh * 3 + 2:hh * 3 + 3], 128, t_smw, t_smw, t_pgw)
                    transposes(PTs, t_PTs, pg, t_pg, qb + 1, 128)
                    transposes(PTw, t_PTw, pgw, t_pgw, nwb, 128)
                    k.op("pe", lambda h: h.matmul(pbank[6][:, 0:128], cmpV[0:NC_, kvh, :], pcT[0:NC_, g, :], start=True, stop=False), reads=[t_cV, t_pcT], writes=[t_pb[6]], inc=False)
                    for j in range(qb + 1):
                        k.op("pe", lambda h: h.matmul(pbank[6][:, 0:128], Vs[:, j, kvh * 128:(kvh + 1) * 128], PTs[:, j, :], start=False, stop=False), reads=[t_Vx, t_PTs], writes=[t_pb[6]], inc=False)
                    for j in range(nwb):
                        k.op("pe", lambda h: h.matmul(pbank[6][:, 0:128], Vw[:, kb0 + j, kvh * 128:(kvh + 1) * 128], PTw[:, j, :], start=False, stop=(j == nwb - 1)), reads=[t_Vx, t_PTw], writes=[t_pb[6]], inc=(j == nwb - 1))
                    k.op("act", lambda h: h.copy(OTb[qi][:, hh, :], pbank[6][:, 0:128]), reads=[t_pb[6]], writes=[t_OTb[qi]])
            k.dma("sp", OTv[:, :, qb * 128:(qb + 1) * 128], OTb[qi][:], reads=[t_OTb[qi]], writes=[t_OT])
        k.barrier()
        esA.close()
        esB = contextlib.ExitStack()
        NKP = 128 * 128; NKs = NKP + NS; NCs = 1023; NBs = 257
        gS = dscr("gS", [NS, 96]); t_gS = T()
        Qs = k.sb(esB, [128, 32, NS], BF16); QsT = k.sb(esB, [128, 4, 32], BF16); t_Qs = T()
        k.dma("sp", Qs[:], QT[:, :, TP:TP + NS].rearrange("h p t -> p h t"), reads=[t_QT], writes=[t_Qs])
        for kvh in range(4):
            k.op("dve", lambda h: h.tensor_copy(QsT[:, kvh, :].rearrange("p (i g) -> p i g", g=8), Qs[:, kvh * 8:(kvh + 1) * 8, :].rearrange("p g i -> p i g")), reads=[t_Qs], writes=[t_Qs])
        k.dma("sp", gS[:, :], gat[0:NS, NQB, :], reads=[t_gat], writes=[t_gS])
        gate_s = k.sb(esB, [32, 4, 8, 3], F32); t_gs = T()
        for i in range(NS):
            k.dma("sp", gate_s[8 * i:8 * i + 8, :, 0, :], gS[i].rearrange("(k g b) -> g k b", k=4, g=8), reads=[t_gS], writes=[t_gs])
        R32 = 8 * NS
        mis = k.sb(esB, [128, 520], I32); t_mis = T()
        keep_s = k.sb(esB, [32, NBs], F32); over_s = k.sb(esB, [32, NBs], F32); t_kos = T()
        k.op("dve", lambda h: h.memset(keep_s[:], 1.0), writes=[t_kos])
        k.op("dve", lambda h: h.memset(over_s[:], 0.0), writes=[t_kos])
        for (lo, hi) in ((0, 1), (NBs - 2, NBs)):
            k.op("dve", lambda h: h.memset(keep_s[:, lo:hi], 0.0), reads=[t_kos], writes=[t_kos])
            k.op("dve", lambda h: h.memset(over_s[:, lo:hi], BIG), reads=[t_kos], writes=[t_kos])
        tri_s = k.sb(esB, [32, NS], F32); wmask_s = k.sb(esB, [32, 512 + NS], F32); t_ms = T()
        k.op("pool", lambda h: h.iota(mis[0:32, 0:NS], pattern=[[-8, NS]], base=0, channel_multiplier=1), writes=[t_mis])
        k.op("dve", lambda h: h.tensor_copy(tri_s[:], mis[0:32, 0:NS]), reads=[t_mis], writes=[t_ms])
        k.op("dve", lambda h: h.tensor_single_scalar(tri_s[:], tri_s[:], 0.0, ALU.is_ge), reads=[t_ms], writes=[t_ms])
        k.op("dve", lambda h: h.tensor_scalar(tri_s[:], tri_s[:], NEG, -NEG, ALU.mult, ALU.add), reads=[t_ms], writes=[t_ms])
        k.op("pool", lambda h: h.iota(mis[0:32, 0:512], pattern=[[8, 512]], base=0, channel_multiplier=-1), reads=[t_ms], writes=[t_mis])
        k.op("dve", lambda h: h.tensor_copy(wmask_s[:, 0:512], mis[0:32, 0:512]), reads=[t_mis], writes=[t_ms])
        k.op("dve", lambda h: h.tensor_single_scalar(wmask_s[:, 0:512], wmask_s[:, 0:512], 0.0, ALU.is_gt), reads=[t_ms], writes=[t_ms])
        k.op("dve", lambda h: h.tensor_scalar(wmask_s[:, 0:512], wmask_s[:, 0:512], NEG, -NEG, ALU.mult, ALU.add), reads=[t_ms], writes=[t_ms])
        k.op("dve", lambda h: h.tensor_copy(wmask_s[:, 512:512 + NS], tri_s[:]), reads=[t_ms], writes=[t_ms])
        Gsum = k.sb(esB, [32, NS, 8], F32); gtmp = k.sb(esB, [32, NS, 8], F32)
        k.op("pool", lambda h: h.iota(mis[0:32, 0:32].rearrange("p (i g) -> p i g", g=8), pattern=[[-8, NS], [0, 8]], base=0, channel_multiplier=1), reads=[t_ms], writes=[t_mis])
        k.op("dve", lambda h: h.tensor_copy(Gsum[:], mis[0:32, 0:32].rearrange("p (i g) -> p i g", g=8)), reads=[t_mis], writes=[t_ms])
        k.op("dve", lambda h: h.tensor_single_scalar(gtmp[:], Gsum[:], 7.0, ALU.is_le), reads=[t_ms], writes=[t_ms])
        k.op("dve", lambda h: h.tensor_single_scalar(Gsum[:], Gsum[:], 0.0, ALU.is_ge), reads=[t_ms], writes=[t_ms])
        k.op("dve", lambda h: h.tensor_tensor(Gsum[:], Gsum[:], gtmp[:], ALU.mult), reads=[t_ms], writes=[t_ms])
        ovl_s = k.sb(esB, [128, 8, NBs], F32); ovt_s = k.sb(esB, [128, NBs], F32); t_ovs = T()
        for gi in range(8):
            cbase = 0 if gi == 0 else 128 * gi - 1
            k.op("pool", lambda h: h.iota(mis[:, 0:NBs], pattern=[[64, NBs]], base=64 - 16 * cbase, channel_multiplier=-16), reads=[t_ovs, t_ms], writes=[t_mis])
            k.op("dve", lambda h: h.tensor_copy(ovl_s[:, gi, :], mis[:, 0:NBs]), reads=[t_mis], writes=[t_ovs])
            k.op("dve", lambda h: h.tensor_single_scalar(ovl_s[:, gi, :], ovl_s[:, gi, :], 0.0, ALU.is_gt), reads=[t_ovs], writes=[t_ovs])
            k.op("pool", lambda h: h.iota(mis[:, 0:NBs], pattern=[[-64, NBs]], base=16 * cbase + 32, channel_multiplier=16), reads=[t_ovs], writes=[t_mis])
            k.op("dve", lambda h: h.tensor_copy(ovt_s[:], mis[:, 0:NBs]), reads=[t_mis], writes=[t_ovs])
            k.op("dve", lambda h: h.tensor_single_scalar(ovt_s[:], ovt_s[:], 0.0, ALU.is_gt), reads=[t_ovs], writes=[t_ovs])
            k.op("dve", lambda h: h.tensor_tensor(ovl_s[:, gi, :], ovl_s[:, gi, :], ovt_s[:], ALU.mult), reads=[t_ovs], writes=[t_ovs])

        pgb = [k.sb(esB, [128, 128], F32) for _ in range(4)]; t_pgb = [T() for _ in range(4)]
        kTp = [k.sb(esB, [128, 128], BF16) for _ in range(2)]; t_kTp = [T(), T()]
        vbp = [k.sb(esB, [128, 128], BF16) for _ in range(2)]; t_vbp = [T(), T()]
        sm_s = k.sb(esB, [32, NKs], F32); t_sms = T()
        pg_s = k.sb(esB, [32, NKs], BF16); t_pgs = T()
        PT_s = k.sb(esB, [128, 129, 32], BF16); t_PTs2 = T()
        smc_s = k.sb(esB, [32, 1024], F32); t_smcs = T()
        pgc_s = k.sb(esB, [32, 1024], BF16); t_pgcs = T()
        PcT_s = k.sb(esB, [128, 8, 32], BF16); t_PcTs = T()
        PTf = k.sb(esB, [128, 8, 32], F32); t_PTf = T()
        impr = k.sb(esB, [32, NBs], F32); impm_s = k.sb(esB, [32, NBs], F32); impw_s = k.sb(esB, [32, NBs], F32); m8s = k.sb(esB, [32, 16], F32); t_imps = T()
        selb_s = k.sb(esB, [32, NBs + 7], F32); t_selbs = T()
        smw_s = k.sb(esB, [32, 512 + NS], F32); t_smws = T()
        pgw_s = k.sb(esB, [32, 512 + NS], BF16); t_pgws = T()
        PTw_s = k.sb(esB, [128, 5, 32], BF16); t_PTws = T()
        KwT_s = k.sb(esB, [128, 512 + NS], BF16); Vw_s = k.sb(esB, [128, 5, 128], BF16); t_kws = T()
        cwk = k.sb(esB, [128, 4, 128], F32); cwv = k.sb(esB, [128, 4, 128], F32); t_cw2 = T()
        nr_t = k.sb(esB, [NS, 4, 128], F32); t_nr = T()
        OTs = k.sb(esB, [128, 32, NS], BF16); t_OTs = T()
        cwin = cache_win.rearrange("(a p) m -> p a m", p=128)

        def chunk_T(dst_ap, src_ap, nq, ncol, t_src, t_dst):
            k.op("pe", lambda h: h.transpose(pT7[0:ncol, 0:nq], src_ap, ident_bf[0:nq, 0:nq]), reads=[t_src, t_ident], writes=[t_pb[7]])
            k.op("act", lambda h: h.copy(dst_ap, pT7[0:ncol, 0:nq]), reads=[t_pb[7]], writes=[t_dst])

        for kvh in range(4):
            q_l = QsT[:, kvh, :]
            for j_, col in enumerate((1024, 1536, 2048, 2560)):
                k.dma("sp", nr_t[:, j_, :], kvS[:, col + kvh * 128:col + (kvh + 1) * 128], reads=[t_kvS], writes=[t_nr])
            for ci, ch in enumerate(range(0, NCs, 512)):
                w = min(512, NCs - ch); pb = 2 + ci % 2
                k.op("pe", lambda h: h.matmul(pbank[pb][0:R32, 0:w], q_l, cmpKT_s[:, kvh, ch:ch + w], start=True, stop=True), reads=[t_Qs, t_cKs], writes=[t_pb[pb]])
                k.op("dve", lambda h: h.tensor_scalar(smc_s[:, ch:ch + w], pbank[pb][0:R32, 0:w], SCALE, None, ALU.mult), reads=[t_pb[pb]], writes=[t_smcs])
            k.op("dve", lambda h: h.reduce_max(st[0:R32, 0:1], smc_s[:, 0:NCs], AXX), reads=[t_smcs], writes=[t_st])
            k.op("dve", lambda h: h.tensor_scalar(st[0:R32, 1:2], st[0:R32, 0:1], -1.0, None, ALU.mult), reads=[t_st], writes=[t_st])
            k.op("act", lambda h: h.activation(smc_s[:, 0:NCs], smc_s[:, 0:NCs], AF.Exp, bias=st[0:R32, 1:2], scale=1.0, accum_out=st[0:R32, 2:3]), reads=[t_smcs, t_st], writes=[t_smcs, t_st])
            k.op("dve", lambda h: h.reciprocal(st[0:R32, 3:4], st[0:R32, 2:3]), reads=[t_st], writes=[t_st])
            k.op("dve", lambda h: h.tensor_scalar(smc_s[:, 0:NCs], smc_s[:, 0:NCs], st[0:R32, 3:4], None, ALU.mult), reads=[t_smcs, t_st], writes=[t_smcs])
            k.op("dve", lambda h: h.tensor_scalar(pgc_s[:, 0:NCs], smc_s[:, 0:NCs], gate_s[:, kvh, 0, 0:1], None, ALU.mult), reads=[t_smcs, t_gs], writes=[t_pgcs])
            for gi in range(8):
                nblk = 127 if gi == 0 else 128; cbase = 0 if gi == 0 else 128 * gi - 1
                chunk_T(PcT_s[0:nblk, gi, :], pgc_s[:, cbase:cbase + nblk], R32, nblk, t_pgcs, t_PcTs)
                k.op("pe", lambda h: h.transpose(pbank[1][0:nblk, 0:R32], smc_s[:, cbase:cbase + nblk], ident[0:R32, 0:R32]), reads=[t_smcs, t_ident], writes=[t_pb[1]])
                k.op("dve", lambda h: h.tensor_copy(PTf[0:nblk, gi, :], pbank[1][0:nblk, 0:R32]), reads=[t_pb[1]], writes=[t_PTf])
            for gi in range(8):
                nblk = 127 if gi == 0 else 128
                k.op("pe", lambda h: h.matmul(pbank[0][0:R32, 0:NBs], PTf[0:nblk, gi, :], ovl_s[0:nblk, gi, :], start=(gi == 0), stop=(gi == 7)), reads=[t_PTf, t_ovs], writes=[t_pb[0]], inc=(gi == 7))
            k.op("dve", lambda h: h.tensor_copy(impr[:], pbank[0][0:R32, 0:NBs]), reads=[t_pb[0]], writes=[t_imps])
            k.op("pe", lambda h: h.matmul(pbank[0][0:R32, 0:NBs], Gsum[:].rearrange("p i g -> p (i g)"), impr[:], start=True, stop=True), reads=[t_imps, t_ms], writes=[t_pb[0]])
            k.op("dve", lambda h: h.tensor_tensor(impm_s[:], pbank[0][0:R32, 0:NBs], keep_s[:], ALU.mult), reads=[t_pb[0], t_kos], writes=[t_imps])
            k.op("dve", lambda h: h.tensor_tensor(impm_s[:], impm_s[:], over_s[:], ALU.add), reads=[t_imps, t_kos], writes=[t_imps])
            k.op("dve", lambda h: h.max(m8s[:, 0:8], impm_s[:]), reads=[t_imps], writes=[t_imps])
            k.op("dve", lambda h: h.match_replace(impw_s[:], m8s[:, 0:8], impm_s[:], -3.0e38), reads=[t_imps], writes=[t_imps])
            k.op("dve", lambda h: h.max(m8s[:, 8:16], impw_s[:]), reads=[t_imps], writes=[t_imps])
            k.op("dve", lambda h: h.memset(selb_s[:], 0.0), writes=[t_selbs])
            k.op("dve", lambda h: h.tensor_scalar(selb_s[:, 0:NBs], impm_s[:], m8s[:, 15:16], None, ALU.is_ge), reads=[t_imps], writes=[t_selbs])
            k.op("dve", lambda h: h.tensor_scalar(selb_s[:, 0:NBs], selb_s[:, 0:NBs], NEG, -NEG, ALU.mult, ALU.add), reads=[t_selbs], writes=[t_selbs])
            for pg_ in range(128):
                b2 = pg_ % 2; b4 = pg_ % 4
                k.idma(pgb[b4][:], cache_s, idx_k[kvh][:, pg_:pg_ + 1], reads=[t_idx], writes=[t_pgb[b4]])
                k.op("pe", lambda h: h.transpose(pbank[4 + b2][:, 0:128], pgb[b4][:], ident[:]), reads=[t_pgb[b4], t_ident], writes=[t_pb[4 + b2]])
                k.op("act", lambda h: h.copy(kTp[b2][:], pbank[4 + b2][:, 0:128]), reads=[t_pb[4 + b2]], writes=[t_kTp[b2]])
                k.op("pe", lambda h: h.matmul(pbank[2 + b2][0:R32, 0:128], q_l, kTp[b2][:], start=True, stop=True), reads=[t_Qs, t_kTp[b2]], writes=[t_pb[2 + b2]])
                k.op("dve", lambda h: h.scalar_tensor_tensor(sm_s[:, pg_ * 128:(pg_ + 1) * 128].rearrange("p (j c) -> p j c", c=64), pbank[2 + b2][0:R32, 0:128].rearrange("p (j c) -> p j c", c=64), SCALE,
                                                               selb_s[:, 2 * pg_:2 * pg_ + 2].unsqueeze(2).to_broadcast([R32, 2, 64]), ALU.mult, ALU.add), reads=[t_pb[2 + b2], t_selbs], writes=[t_sms])
            k.op("pe", lambda h: h.transpose(pbank[4][:, 0:NS], nr_t[:, 0, :], ident[0:NS, 0:NS]), reads=[t_nr, t_ident], writes=[t_pb[4]])
            k.op("act", lambda h: h.copy(kTp[0][:, 0:NS], pbank[4][:, 0:NS]), reads=[t_pb[4]], writes=[t_kTp[0]])
            k.op("pe", lambda h: h.matmul(pbank[2][0:R32, 0:NS], q_l, kTp[0][:, 0:NS], start=True, stop=True), reads=[t_Qs, t_kTp[0]], writes=[t_pb[2]])
            k.op("dve", lambda h: h.scalar_tensor_tensor(sm_s[:, NKP:NKs], pbank[2][0:R32, 0:NS], SCALE, tri_s[:], ALU.mult, ALU.add), reads=[t_pb[2], t_ms], writes=[t_sms])
            k.op("dve", lambda h: h.tensor_scalar(sm_s[:, NKP:NKs], sm_s[:, NKP:NKs], selb_s[:, NBs - 1:NBs], None, ALU.add), reads=[t_sms, t_selbs], writes=[t_sms])
            softmax_gated(sm_s[:, 0:NKs], sm_s[:, 0:NKs], pg_s[:, 0:NKs], gate_s[:, kvh, 0, 1:2], R32, t_sms, t_sms, t_pgs)
            for pg_ in range(128):
                chunk_T(PT_s[:, pg_, :], pg_s[:, pg_ * 128:(pg_ + 1) * 128], R32, 128, t_pgs, t_PTs2)
            chunk_T(PT_s[0:NS, 128, :], pg_s[:, NKP:NKs], R32, NS, t_pgs, t_PTs2)
            k.dma("sp", cwk[:], cwin[:, :, kvh * 128:(kvh + 1) * 128], writes=[t_cw2])
            k.dma("sp", cwv[:], cwin[:, :, 512 + kvh * 128:512 + (kvh + 1) * 128], writes=[t_cw2])
            for a_ in range(4):
                k.op("pe", lambda h: h.transpose(pbank[4][:, 0:128], cwk[:, a_, :], ident[:]), reads=[t_cw2, t_ident], writes=[t_pb[4]])
                k.op("act", lambda h: h.copy(KwT_s[:, a_ * 128:(a_ + 1) * 128], pbank[4][:, 0:128]), reads=[t_pb[4]], writes=[t_kws])
            k.op("dve", lambda h: h.tensor_copy(Vw_s[:, 0:4, :], cwv[:]), reads=[t_cw2], writes=[t_kws])
            k.op("pe", lambda h: h.transpose(pbank[4][:, 0:NS], nr_t[:, 2, :], ident[0:NS, 0:NS]), reads=[t_nr, t_ident], writes=[t_pb[4]])
            k.op("act", lambda h: h.copy(KwT_s[:, 512:512 + NS], pbank[4][:, 0:NS]), reads=[t_pb[4]], writes=[t_kws])
            k.op("dve", lambda h: h.tensor_copy(Vw_s[0:NS, 4, :], nr_t[:, 3, :]), reads=[t_nr], writes=[t_kws])
            for (ch, w) in ((0, 512), (512, NS)):
                k.op("pe", lambda h: h.matmul(pbank[3][0:R32, 0:w], q_l, KwT_s[:, ch:ch + w], start=True, stop=True), reads=[t_Qs, t_kws], writes=[t_pb[3]])
                k.op("dve", lambda h: h.scalar_tensor_tensor(smw_s[:, ch:ch + w], pbank[3][0:R32, 0:w], SCALE, wmask_s[:, ch:ch + w], ALU.mult, ALU.add), reads=[t_pb[3], t_ms], writes=[t_smws])
            softmax_gated(smw_s[:, :], smw_s[:, :], pgw_s[:, :], gate_s[:, kvh, 0, 2:3], R32, t_smws, t_smws, t_pgws)
            for a_ in range(4):
                chunk_T(PTw_s[:, a_, :], pgw_s[:, a_ * 128:(a_ + 1) * 128], R32, 128, t_pgws, t_PTws)
            chunk_T(PTw_s[0:NS, 4, :], pgw_s[:, 512:512 + NS], R32, NS, t_pgws, t_PTws)
            oT = pbank[6][:, 0:R32]
            for gi in range(8):
                nblk = 127 if gi == 0 else 128
                k.op("pe", lambda h: h.matmul(oT, cmpV_s[0:nblk, gi, kvh, :], PcT_s[0:nblk, gi, :], start=(gi == 0), stop=False), reads=[t_cKs, t_PcTs], writes=[t_pb[6]], inc=False)
            for a_ in range(4):
                k.op("pe", lambda h: h.matmul(oT, Vw_s[:, a_, :], PTw_s[:, a_, :], start=False, stop=False), reads=[t_kws, t_PTws], writes=[t_pb[6]], inc=False)
            k.op("pe", lambda h: h.matmul(oT, Vw_s[0:NS, 4, :], PTw_s[0:NS, 4, :], start=False, stop=False), reads=[t_kws, t_PTws], writes=[t_pb[6]], inc=False)
            k.op("dve", lambda h: h.tensor_copy(vbp[0][0:NS, :], nr_t[:, 1, :]), reads=[t_nr], writes=[t_vbp[0]])
            k.op("pe", lambda h: h.matmul(oT, vbp[0][0:NS, :], PT_s[0:NS, 128, :], start=False, stop=False), reads=[t_vbp[0], t_PTs2], writes=[t_pb[6]])
            for pg_ in range(128):
                b2 = pg_ % 2; b4 = pg_ % 4
                k.idma(pgb[b4][:], cache_s, idx_v[kvh][:, pg_:pg_ + 1], reads=[t_idx], writes=[t_pgb[b4]])
                k.op("dve", lambda h: h.tensor_copy(vbp[b2][:], pgb[b4][:]), reads=[t_pgb[b4]], writes=[t_vbp[b2]])
                k.op("pe", lambda h: h.matmul(oT, vbp[b2][:], PT_s[:, pg_, :], start=False, stop=(pg_ == 127)), reads=[t_vbp[b2], t_PTs2], writes=[t_pb[6]])
            k.op("act", lambda h: h.copy(OTs[:, kvh * 8:(kvh + 1) * 8, :].rearrange("p g i -> p i g"), pbank[6][:, 0:R32].rearrange("p (i g) -> p i g", g=8)), reads=[t_pb[6]], writes=[t_OTs])
        k.dma("sp", OTv[:, :, TP:TP + NS], OTs[:], reads=[t_OTs], writes=[t_OT])
        k.barrier()
        esB.close()
    k.barrier()

    with contextlib.ExitStack() as es:
        hbuf = k.sb(es, [128, KC, NT], BF16); t_h = T()
        for c8 in range(0, KC, 8):
            k.dma("sp", hbuf[:, c8:c8 + 8, :], OTv[:, c8:c8 + 8, :], reads=[t_OT], writes=[t_h])
        xc_ = [k.sb(es, [128, 512], F32) for _ in range(2)]; t_xc_ = [T(), T()]
        cnt = [0]
        def o_epi(idx, ti, pbs):
            c0, w, _s = tiles[ti]; b = cnt[0] % 2; cnt[0] += 1
            k.dma("sp", xc_[b][:, 0:w], xTv[:, idx, c0:c0 + w], reads=[t_xT], writes=[t_xc_[b]])
            k.op("dve", lambda h: h.tensor_tensor(xc_[b][:, 0:w], xc_[b][:, 0:w], pbank[pbs[0]][:, 0:w], ALU.add), reads=[t_pb[pbs[0]], t_xc_[b]], writes=[t_xc_[b]])
            k.dma("sp", xTv[:, idx, c0:c0 + w], xc_[b][:, 0:w], reads=[t_xc_[b]], writes=[t_xT])
        dense_ws(es, w_o, [[m] for m in range(KC)], hbuf, t_h, o_epi)
    k.barrier()

    ffn_layer(1)

    with contextlib.ExitStack() as es:
        xt = [k.sb(es, [128, KC, 128], F32) for _ in range(2)]; t_xt = [T(), T()]
        sq = k.sb(es, [128, 128], F32); t_sq = T()
        rstd = k.sb(es, [128, 128], F32); t_rstd = T()
        yc = [k.sb(es, [128, 4, 128], F32) for _ in range(2)]; t_yc = [T(), T()]
        yo = [k.sb(es, [128, D], F32) for _ in range(2)]; t_yo = [T(), T()]
        blocks = [(i * 128, min(128, TP - i * 128), 0) for i in range(NQB)] + [(TP, NS, 1)]
        for bi, (cc, hw, sq_) in enumerate(blocks):
            b = bi % 2
            k.dma("sp", xt[b][:, :, 0:hw], xTv[:, :, cc:cc + hw], reads=[t_xT], writes=[t_xt[b]])
            pb = 2 + b
            for c in range(KC):
                k.op("act", lambda h: h.activation(sq[:, 0:hw], xt[b][:, c, 0:hw], AF.Square), reads=[t_xt[b]], writes=[t_sq])
                k.op("pe", lambda h: h.matmul(pbank[pb][:, 0:hw], ones[:], sq[:, 0:hw], start=(c == 0), stop=(c == KC - 1)), reads=[t_sq, t_ones], writes=[t_pb[pb]])
            k.op("act", lambda h: h.activation(rstd[:, 0:hw], pbank[pb][:, 0:hw], AF.Sqrt, scale=1.0 / D, bias=epsc[:, 0:1]), reads=[t_pb[pb], t_iot], writes=[t_rstd])
            k.op("dve", lambda h: h.reciprocal(rstd[:, 0:hw], rstd[:, 0:hw]), reads=[t_rstd], writes=[t_rstd])
            for c4 in range(0, KC, 4):
                yi = (c4 // 4) % 2; n4 = min(4, KC - c4); pbt = 4 + yi
                for c in range(n4):
                    k.op("dve", lambda h: h.scalar_tensor_tensor(yc[yi][:, c, 0:hw], xt[b][:, c4 + c, 0:hw], gains[:, 5, c4 + c:c4 + c + 1], rstd[:, 0:hw], ALU.mult, ALU.mult),
                         reads=[t_xt[b], t_rstd, t_gains], writes=[t_yc[yi]])
                for c in range(n4):
                    k.op("pe", lambda h: h.transpose(pbank[pbt][0:hw, c * 128:(c + 1) * 128], yc[yi][:, c, 0:hw], ident[:]), reads=[t_yc[yi], t_ident], writes=[t_pb[pbt]], inc=(c == n4 - 1))
                k.op("act", lambda h: h.copy(yo[b][0:hw, c4 * 128:(c4 + n4) * 128], pbank[pbt][0:hw, 0:n4 * 128]), reads=[t_pb[pbt]], writes=[t_yo[b]])
            od = o_y_p[cc:cc + hw, :] if sq_ == 0 else o_y_s[0:hw, :]
            k.dma("sp", od, yo[b][0:hw, :], reads=[t_yo[b]], writes=[])

    glob.close()
    k.finish()
    return nc


def core_inputs(cfg, c, I):
    B = I["x_prompt"].shape[0]
    Q = cfg.Q
    m = {
        "xp": I["x_prompt"][c % B], "xs": I["x_sample"][c],
        "st_re": I["state_ssm_re"][0, c].reshape(Q, 128), "st_im": I["state_ssm_im"][0, c].reshape(Q, 128),
        "cache_kv": I["cache_kv"].reshape(-1, 2048), "page_tab": I["page_table"][c].reshape(1, 128),
        "st_conv": I["state_ffn_conv"][0, c], "st_conv1": I["state_ffn_conv"][1, c], "cache_win": I["cache_win"][c].reshape(512, 1024),
        "attn_g1": I["attn_norm"][1], "ffn_g1": I["ffn_norm"][1], "fin_g": I["final_norm"],
        "w_in1": I["ffn_w_in"][1], "w_down1": I["ffn_w_down"][1], "conv_w1": I["ffn_conv_w"][1], "conv_b1": I["ffn_conv_b"][1],
        "w_qg": I["w_qg"][0], "w_o": I["w_o"][0], "cmp_w1": I["cmp_w1"], "cmp_b1": I["cmp_b1"], "cmp_w2": I["cmp_w2"],
        "cmp_b2": I["cmp_b2"], "cmp_pe": I["cmp_pe"],
        "attn_g": I["attn_norm"][0], "ffn_g": I["ffn_norm"][0], "kv_g": I["kv_norm"],
        "lam_re": I["ssm_lam_re"][0].reshape(Q, 128), "lam_im": I["ssm_lam_im"][0].reshape(Q, 128),
        "log_step": I["ssm_log_step"][0].reshape(Q, 2),
        "b_re": I["ssm_b_re"][0], "b_im": I["ssm_b_im"][0],
        "c_re": I["ssm_c_re"][0].reshape(cfg.G * 16, 64), "c_im": I["ssm_c_im"][0].reshape(cfg.G * 16, 64),
        "ssm_d": I["ssm_d"][0], "w_glu": I["ssm_w_glu"][0], "w_in": I["ffn_w_in"][0], "w_down": I["ffn_w_down"][0],
        "conv_w": I["ffn_conv_w"][0], "conv_b": I["ffn_conv_b"][0], "w_kv": I["w_kv"],
    }
    return {n: np.ascontiguousarray(np.asarray(v, dtype=(np.int32 if n == "page_tab" else np.float32))) for n, v in m.items()}


def kernel(**I):
    I = {n: np.asarray(v) for n, v in I.items()}
    B, TP, D = I["x_prompt"].shape
    DB, NS, _ = I["x_sample"].shape
    DFF = I["ffn_conv_b"].shape[1]
    cfg = Cfg(D=D, TP=TP, NS=NS, DFF=DFF, NKV=I["w_kv"].shape[1], NPOOL=I["cache_kv"].shape[0])
    nc = build(cfg)
    ncores = 8
    in_maps = [core_inputs(cfg, c, I) for c in range(ncores)]
    res = run_bass_kernel_spmd(nc, in_maps, core_ids=list(range(ncores)))
    R = res.results
    G, Q = cfg.G, cfg.Q
    depth = I["ffn_w_in"].shape[0]
    f32 = np.float32
    y_p = np.stack([R[b]["o_y_p"] for b in range(B)]); y_s = np.stack([R[c]["o_y_s"] for c in range(DB)])
    ssm_re_p = np.stack([R[b]["o_ssm_re_p"].reshape(G, 64) for b in range(B)])[None]
    ssm_im_p = np.stack([R[b]["o_ssm_im_p"].reshape(G, 64) for b in range(B)])[None]
    ssm_re_s = np.stack([R[c]["o_ssm_re_s"].reshape(G, 64) for c in range(DB)])[None]
    ssm_im_s = np.stack([R[c]["o_ssm_im_s"].reshape(G, 64) for c in range(DB)])[None]
    conv_p = np.zeros((depth, B, 2, DFF), f32); conv_s = np.zeros((depth, DB, 2, DFF), f32)
    conv_p[0] = np.stack([R[b]["o_conv_p"] for b in range(B)])
    conv_s[0] = np.stack([R[c]["o_conv_s"] for c in range(DB)])
    conv_p[1] = np.stack([R[b]["o_conv_p1"] for b in range(B)])
    conv_s[1] = np.stack([R[c]["o_conv_s1"] for c in range(DB)])
    kv_p = np.stack([R[b]["o_kv_p"].reshape(TP, 4, 4, 128) for b in range(B)])
    kv_s = np.stack([R[c]["o_kv_s"].reshape(NS, 4, 4, 128) for c in range(DB)])
    win_p = np.stack([R[b]["o_win_p"].reshape(512, 2, 4, 128) for b in range(B)])
    win_s = np.stack([R[c]["o_win_s"].reshape(512, 2, 4, 128) for c in range(DB)])
    return (y_p, y_s, ssm_re_p.astype(f32), ssm_im_p.astype(f32), ssm_re_s.astype(f32), ssm_im_s.astype(f32),
            conv_p, conv_s, kv_p.astype(f32), kv_s.astype(f32), win_p.astype(f32), win_s.astype(f32))
```

```python
import math, contextlib
import numpy as np
import concourse.bass as bass
import concourse.mybir as mybir
from concourse.bass_utils import run_bass_kernel_spmd

F32 = mybir.dt.float32; BF16 = mybir.dt.bfloat16; I32 = mybir.dt.int32
ALU = mybir.AluOpType
AF = mybir.ActivationFunctionType
TWO_PI = 2.0 * math.pi


class T:
    __slots__ = ("w", "r")
    def __init__(self):
        self.w = []; self.r = []


class Eng:
    def __init__(self, h, sem, name):
        self.h = h; self.sem = sem; self.count = 0; self.seen = {}; self.name = name


class K:
    def __init__(self, n_dma_sems=16):
        self.nc = bass.Bass("TRN2", target_bir_lowering=False)
        nc = self.nc
        self.es = contextlib.ExitStack()
        self.E = {}
        for nm, h in (("pe", nc.tensor), ("act", nc.scalar), ("dve", nc.vector), ("pool", nc.gpsimd), ("sp", nc.sync)):
            s = self.es.enter_context(nc.semaphore("sem_" + nm))
            self.E[nm] = Eng(h, s, nm)
        self.dsem = {}
        for q in ("sp", "pool"):
            self.dsem[q] = [[self.es.enter_context(nc.semaphore(f"d_{q}_{i}")), 0] for i in range(n_dma_sems)]
        self.dnext = {"sp": 0, "pool": 0}
        self.uid = 0

    def sb(self, es, shape, dt, name=None):
        self.uid += 1
        return es.enter_context(self.nc.sbuf_tensor(name or f"sb{self.uid}", list(shape), dt))

    def ps(self, es, shape, dt=F32, name=None):
        self.uid += 1
        return es.enter_context(self.nc.psum_tensor(name or f"ps{self.uid}", list(shape), dt))

    def _waits(self, E, reads, writes):
        waits = {}
        for t in reads:
            for (s, v) in t.w:
                if waits.get(id(s), (None, 0))[1] < v: waits[id(s)] = (s, v)
        for t in writes:
            for (s, v) in t.w + t.r:
                if waits.get(id(s), (None, 0))[1] < v: waits[id(s)] = (s, v)
        for s, v in waits.values():
            if s is E.sem and v > E.count:
                continue
            if E.seen.get(id(s), 0) >= v: continue
            E.h.wait_ge(s, v); E.seen[id(s)] = v

    def op(self, eng, fn, reads=(), writes=(), inc=True):
        E = self.E[eng]
        self._waits(E, reads, writes)
        ins = fn(E.h)
        if inc:
            E.count += 1
            ins.then_inc(E.sem, 1)
            ev = (E.sem, E.count)
        else:
            ev = (E.sem, E.count + 1)
        for t in reads:
            t.r = [e for e in t.r if e[0] is not ev[0]] + [ev]
        for t in writes:
            t.w = [ev]; t.r = []
        return ins

    def dma(self, q, out, in_, reads=(), writes=(), **kw):
        E = self.E[q]
        self._waits(E, reads, writes)
        lst = self.dsem[q]; i = self.dnext[q]; self.dnext[q] = (i + 1) % len(lst)
        s, v = lst[i]
        if v > 0 and E.seen.get(id(s), 0) < v:
            E.h.wait_ge(s, v); E.seen[id(s)] = v
        lst[i][1] = v + 16
        E.h.dma_start(out=out, in_=in_, **kw).then_inc(s, 16)
        ev = (s, v + 16)
        for t in reads:
            t.r = [e for e in t.r if e[0] is not ev[0]] + [ev]
        for t in writes:
            t.w = [ev]; t.r = []
        return ev

    def idma(self, out, in_, idx_ap, reads=(), writes=()):
        q = "pool"; E = self.E[q]
        self._waits(E, reads, writes)
        lst = self.dsem[q]; i = self.dnext[q]; self.dnext[q] = (i + 1) % len(lst)
        s, v = lst[i]
        if v > 0 and E.seen.get(id(s), 0) < v:
            E.h.wait_ge(s, v); E.seen[id(s)] = v
        lst[i][1] = v + 16
        E.h.indirect_dma_start(out=out, out_offset=None, in_=in_, in_offset=bass.IndirectOffsetOnAxis(ap=idx_ap, axis=0)).then_inc(s, 16)
        ev = (s, v + 16)
        for t in reads:
            t.r = [e for e in t.r if e[0] is not ev[0]] + [ev]
        for t in writes:
            t.w = [ev]; t.r = []
        return ev

    def barrier(self):
        for nm, E in self.E.items():
            for q in self.dsem:
                for s, v in self.dsem[q]:
                    if v > 0 and E.seen.get(id(s), 0) < v:
                        E.h.wait_ge(s, v); E.seen[id(s)] = v
            for nm2, E2 in self.E.items():
                if E2 is E or E2.count == 0: continue
                if E.seen.get(id(E2.sem), 0) < E2.count:
                    E.h.wait_ge(E2.sem, E2.count); E.seen[id(E2.sem)] = E2.count

    def finish(self):
        self.barrier()
        self.es.close()


class Cfg:
    def __init__(self, D=4096, TP=2048, NS=4, DFF=11008, NKV=3072, NPOOL=1280):
        self.D = D; self.TP = TP; self.NS = NS; self.DFF = DFF; self.NKV = NKV
        self.KC = D // 128; self.G = D // 16; self.Q = self.G // 2; self.FC = DFF // 128
        self.NT = TP + NS
        self.NPOOL = NPOOL
        self.tiles = [(i * 512, min(512, TP - i * 512), 0) for i in range((TP + 511) // 512)] + [(TP, NS, 1)]


def build(cfg):
    k = K(); nc = k.nc
    D, TP, NS, DFF, NKV, KC, G, Q, FC, NT = cfg.D, cfg.TP, cfg.NS, cfg.DFF, cfg.NKV, cfg.KC, cfg.G, cfg.Q, cfg.FC, cfg.NT
    tiles = cfg.tiles
    WT = 513

    def din(name, shape, dt=F32):
        return nc.dram_tensor(name, list(shape), dt, kind="ExternalInput").ap()
    def dout(name, shape, dt=F32):
        return nc.dram_tensor(name, list(shape), dt, kind="ExternalOutput").ap()
    def dscr(name, shape, dt=F32):
        return nc.dram_tensor(name, list(shape), dt, kind="Internal").ap()

    xp = din("xp", [TP, D]); xs = din("xs", [NS, D])
    st_re = din("st_re", [Q, 128]); st_im = din("st_im", [Q, 128])
    st_conv = din("st_conv", [2, DFF]); st_conv1 = din("st_conv1", [2, DFF])
    cache_win = din("cache_win", [512, 1024])
    NPOOL = cfg.NPOOL
    cache_kv = din("cache_kv", [NPOOL * 128, 2048]); page_tab = din("page_tab", [1, 128], I32)
    attn_g = din("attn_g", [D]); ffn_g = din("ffn_g", [D]); kv_g = din("kv_g", [D])
    attn_g1 = din("attn_g1", [D]); ffn_g1 = din("ffn_g1", [D]); fin_g = din("fin_g", [D])
    w_in1 = din("w_in1", [D, 2 * DFF]); w_down1 = din("w_down1", [DFF, D])
    conv_w1 = din("conv_w1", [3, DFF]); conv_b1 = din("conv_b1", [DFF])
    w_qg = din("w_qg", [D, D + 96]); w_o = din("w_o", [D, D])
    cmp_w1 = din("cmp_w1", [2, 4096, 256]); cmp_b1 = din("cmp_b1", [2, 256]); cmp_w2 = din("cmp_w2", [2, 256, 128])
    cmp_b2 = din("cmp_b2", [2, 128]); cmp_pe = din("cmp_pe", [2, 32, 128])
    lam_re = din("lam_re", [Q, 128]); lam_im = din("lam_im", [Q, 128]); log_step = din("log_step", [Q, 2])
    b_re = din("b_re", [G, 64, 16]); b_im = din("b_im", [G, 64, 16])
    c_re = din("c_re", [G * 16, 64]); c_im = din("c_im", [G * 16, 64])
    ssm_d = din("ssm_d", [D])
    w_glu = din("w_glu", [D, 2 * D]); w_in = din("w_in", [D, 2 * DFF]); w_down = din("w_down", [DFF, D])
    conv_w = din("conv_w", [3, DFF]); conv_b = din("conv_b", [DFF])
    w_kv = din("w_kv", [D, NKV])

    o_ssm_re_p = dout("o_ssm_re_p", [Q, 128]); o_ssm_im_p = dout("o_ssm_im_p", [Q, 128])
    o_ssm_re_s = dout("o_ssm_re_s", [Q, 128]); o_ssm_im_s = dout("o_ssm_im_s", [Q, 128])
    o_conv_p = dout("o_conv_p", [2, DFF]); o_conv_s = dout("o_conv_s", [2, DFF])
    o_conv_p1 = dout("o_conv_p1", [2, DFF]); o_conv_s1 = dout("o_conv_s1", [2, DFF])
    o_y_p = dout("o_y_p", [TP, D]); o_y_s = dout("o_y_s", [NS, D])
    o_kv_p = dout("o_kv_p", [TP, 2048]); o_kv_s = dout("o_kv_s", [NS, 2048])
    o_win_p = dout("o_win_p", [512, 1024]); o_win_s = dout("o_win_s", [512, 1024])

    xT = dscr("xT", [D, NT])
    zT = dscr("zT", [D, NT], BF16)
    actT = dscr("actT", [DFF, NT], BF16)
    t_xT = T(); t_zT = T(); t_act = T()
    xTv = xT.rearrange("(c p) t -> p c t", p=128)
    zTv = zT.rearrange("(c p) t -> p c t", p=128)
    actTv = actT.rearrange("(c p) t -> p c t", p=128)

    glob = contextlib.ExitStack()
    ident = k.sb(glob, [128, 128], F32); t_ident = T()
    ones = k.sb(glob, [128, 128], F32); t_ones = T()
    iot = k.sb(glob, [128, WT], F32); t_iot = T()
    iot_i = k.sb(glob, [128, WT], I32)
    halfpi = k.sb(glob, [128, 1], F32)
    k.op("pool", lambda h: h.memset(ident[:], 0.0), writes=[t_ident])
    k.op("pool", lambda h: h.affine_select(out=ident[:], in_=ident[:], pattern=[[-1, 128]], compare_op=ALU.not_equal,
                                           fill=1.0, base=0, channel_multiplier=1), reads=[t_ident], writes=[t_ident])
    k.op("pool", lambda h: h.memset(ones[:], 1.0), writes=[t_ones])
    k.op("pool", lambda h: h.iota(iot_i[:], pattern=[[1, WT]], base=0, channel_multiplier=0), writes=[t_iot])
    k.op("dve", lambda h: h.tensor_copy(iot[:], iot_i[:]), reads=[t_iot], writes=[t_iot])
    k.op("dve", lambda h: h.memset(halfpi[:], math.pi / 2), writes=[t_iot])
    epsc = k.sb(glob, [128, 1], F32)
    k.op("dve", lambda h: h.memset(epsc[:], 1e-6), writes=[t_iot])
    msel = k.sb(glob, [128, 4, 128], F32); t_msel = T()
    k.op("dve", lambda h: h.memset(msel[:], 0.0), writes=[t_msel])
    for j in range(4):
        k.op("dve", lambda h: h.memset(msel[0:64, j, 32 * j:32 * j + 16], 1.0), reads=[t_msel], writes=[t_msel])
        k.op("dve", lambda h: h.memset(msel[64:128, j, 32 * j + 16:32 * j + 32], 1.0), reads=[t_msel], writes=[t_msel])
    gains = k.sb(glob, [128, 6, KC], F32); t_gains = T()
    dskip = k.sb(glob, [128, KC], F32)
    cwt = k.sb(glob, [128, 2, 4, FC], F32); t_cw = T()
    natv = k.sb(glob, [128, 128], F32); t_natv = T()
    pbank = [k.ps(glob, [128, 512]) for _ in range(8)]
    t_pb = [T() for _ in range(8)]
    def load_cols(dst, t_dst, src, n):
        k.dma("sp", natv[0:n, :], src.rearrange("(c p) -> c p", p=128), writes=[t_natv])
        k.op("pe", lambda h: h.transpose(pbank[0][:, 0:n], natv[0:n, :], ident[0:n, 0:n]), reads=[t_natv, t_ident], writes=[t_pb[0]])
        k.op("dve", lambda h: h.tensor_copy(dst, pbank[0][:, 0:n]), reads=[t_pb[0]], writes=[t_dst])
    for i, g in enumerate((attn_g, ffn_g, kv_g, attn_g1, ffn_g1, fin_g)):
        load_cols(gains[:, i, :], t_gains, g, KC)
    load_cols(dskip[:], t_gains, ssm_d, KC)
    stc = k.sb(glob, [128, 2, FC, 2], F32); t_stc = T()
    convo = k.sb(glob, [128, 2, 2, FC], F32); t_convo = T()
    for L_, (cw_, cb_, sc_) in enumerate(((conv_w, conv_b, st_conv), (conv_w1, conv_b1, st_conv1))):
        for i in range(3):
            load_cols(cwt[:, L_, i, :], t_cw, cw_[i], FC)
        load_cols(cwt[:, L_, 3, :], t_cw, cb_, FC)
        for t_ in range(2):
            load_cols(stc[:, L_, :, t_], t_stc, sc_[t_], FC)

    with contextlib.ExitStack() as es:
        xin = [k.sb(es, [128, D], F32) for _ in range(2)]; t_xin = [T(), T()]
        xo = [k.sb(es, [128, KC, 128], F32) for _ in range(2)]; t_xo = [T(), T()]
        blocks = [(xp, i * 128, min(128, TP - i * 128), i * 128) for i in range((TP + 127) // 128)] + [(xs, 0, NS, TP)]
        for bi, (src, r0, nr, c0) in enumerate(blocks):
            b = bi % 2
            k.dma("sp", xin[b][0:nr, :], src[r0:r0 + nr, :], writes=[t_xin[b]])
            for c4 in range(0, KC, 4):
                pb = (c4 // 4) % 2
                n4 = min(4, KC - c4)
                for c in range(n4):
                    k.op("pe", lambda h: h.transpose(pbank[pb][:, c * 128:c * 128 + nr], xin[b][0:nr, (c4 + c) * 128:(c4 + c + 1) * 128], ident[0:nr, 0:nr]),
                         reads=[t_xin[b], t_ident], writes=[t_pb[pb]], inc=(c == n4 - 1))
                k.op("act", lambda h: h.copy(xo[b][:, c4:c4 + n4, 0:nr], pbank[pb][:, 0:n4 * 128].rearrange("p (c t) -> p c t", t=128)[:, :, 0:nr]),
                     reads=[t_pb[pb]], writes=[t_xo[b]])
            k.dma("sp", xTv[:, :, c0:c0 + nr], xo[b][:, :, 0:nr], reads=[t_xo[b]], writes=[t_xT])
    k.barrier()

    def rmsnorm(hbuf, t_h, gi, es):
        xt = [k.sb(es, [128, KC, 128], F32) for _ in range(2)]; t_xt = [T(), T()]
        sq2 = [k.sb(es, [128, 128], F32) for _ in range(4)]; t_sq2 = [T() for _ in range(4)]
        rstd = k.sb(es, [128, 256], F32); t_rstd = T()
        ntile = 0
        for (c0, w, _s) in tiles:
            for h0 in range(0, w, 128):
                hw = min(128, w - h0); cc = c0 + h0
                b = ntile % 2; ntile += 1
                k.dma("sp", xt[b][:, :, 0:hw], xTv[:, :, cc:cc + hw], reads=[t_xT], writes=[t_xt[b]])
                pb = 2 + b
                for c in range(KC):
                    sq = sq2[c % 4]; t_sq = t_sq2[c % 4]
                    k.op("act", lambda h: h.activation(sq[:, 0:hw], xt[b][:, c, 0:hw], AF.Square), reads=[t_xt[b]], writes=[t_sq])
                    k.op("pe", lambda h: h.matmul(pbank[pb][:, 0:hw], ones[:], sq[:, 0:hw], start=(c == 0), stop=(c == KC - 1)),
                         reads=[t_sq, t_ones], writes=[t_pb[pb]])
                k.op("act", lambda h: h.activation(rstd[:, 0:hw], pbank[pb][:, 0:hw], AF.Sqrt, scale=1.0 / D, bias=epsc[:, 0:1]),
                     reads=[t_pb[pb], t_iot], writes=[t_rstd])
                k.op("dve", lambda h: h.reciprocal(rstd[:, 0:hw], rstd[:, 0:hw]),
                     reads=[t_rstd], writes=[t_rstd])
                for c in range(KC):
                    k.op("dve", lambda h: h.scalar_tensor_tensor(hbuf[:, c, cc:cc + hw], xt[b][:, c, 0:hw], gains[:, gi, c:c + 1], rstd[:, 0:hw], ALU.mult, ALU.mult),
                         reads=[t_xt[b], t_rstd, t_gains], writes=[t_h])

    def dense_ws(es, W, col_lists, hbuf, t_h, epilogue):
        nw = len(col_lists[0])
        wb = [[k.sb(es, [128, KC, 128], BF16) for _ in range(nw)] for _ in range(2)]
        t_wb = [[T() for _ in range(nw)] for _ in range(2)]
        Wv = W.rearrange("(c p) m -> p c m", p=128)
        cnt = 0
        for idx, cols in enumerate(col_lists):
            b = idx % 2
            for wi, col in enumerate(cols):
                k.dma("pool", wb[b][wi][:], Wv[:, :, col * 128:(col + 1) * 128], writes=[t_wb[b][wi]])
            for ti, (c0, w, _s) in enumerate(tiles):
                pbs = []
                for wi in range(nw):
                    pb = 4 + (cnt % 4); cnt += 1
                    for c in range(KC):
                        k.op("pe", lambda h: h.matmul(pbank[pb][:, 0:w], wb[b][wi][:, c, :], hbuf[:, c, c0:c0 + w], start=(c == 0), stop=(c == KC - 1)),
                             reads=[t_wb[b][wi], t_h], writes=[t_pb[pb]], inc=(c == KC - 1))
                    pbs.append(pb)
                epilogue(idx, ti, pbs)

    with contextlib.ExitStack() as es:
        hbuf = k.sb(es, [128, KC, NT], BF16); t_h = T()
        with contextlib.ExitStack() as es2:
            rmsnorm(hbuf, t_h, 0, es2)
        k.barrier()
        with contextlib.ExitStack() as es2:
            nat = {n: k.sb(es2, [128, 128], F32) for n in ("lr", "li", "dt", "a", "th", "r", "fr", "sn", "cs", "lbr", "lbi", "den", "kr", "ki", "t1", "t2")}
            nat_i = k.sb(es2, [128, 128], I32)
            t_nat = T()
            k.dma("sp", nat["lr"][0:Q, :], lam_re[:, :], writes=[t_nat])
            k.dma("sp", nat["li"][0:Q, :], lam_im[:, :], writes=[t_nat])
            k.dma("sp", nat["t2"][0:Q, 0:2], log_step[:, :], writes=[t_nat])
            for hf in range(2):
                k.op("dve", lambda h: h.tensor_copy(nat["dt"][0:Q, hf * 64:(hf + 1) * 64], nat["t2"][0:Q, hf:hf + 1].to_broadcast([Q, 64])), reads=[t_nat], writes=[t_nat])
            R = [t_nat]
            def dv(fn): k.op("dve", fn, reads=R, writes=R)
            def ac(fn): k.op("act", fn, reads=R, writes=R)
            N = lambda n: nat[n][0:Q, :]
            ac(lambda h: h.activation(N("dt"), N("dt"), AF.Exp))
            dv(lambda h: h.tensor_tensor(N("a"), N("lr"), N("dt"), ALU.mult))
            dv(lambda h: h.tensor_tensor(N("th"), N("li"), N("dt"), ALU.mult))
            ac(lambda h: h.activation(N("r"), N("a"), AF.Exp))
            dv(lambda h: h.tensor_scalar(N("th"), N("th"), 1.0 / TWO_PI, None, ALU.mult))
            dv(lambda h: h.tensor_copy(nat_i[0:Q, :], N("th")))
            dv(lambda h: h.tensor_copy(N("t1"), nat_i[0:Q, :]))
            dv(lambda h: h.tensor_tensor(N("fr"), N("th"), N("t1"), ALU.subtract))
            ac(lambda h: h.activation(N("sn"), N("fr"), AF.Sin, scale=TWO_PI))
            ac(lambda h: h.activation(N("t1"), N("fr"), AF.Abs))
            ac(lambda h: h.activation(N("cs"), N("t1"), AF.Sin, scale=-TWO_PI, bias=halfpi[0:Q, :]))
            dv(lambda h: h.tensor_tensor(N("lbr"), N("r"), N("cs"), ALU.mult))
            dv(lambda h: h.tensor_scalar(N("lbr"), N("lbr"), -1.0, None, ALU.add))
            dv(lambda h: h.tensor_tensor(N("lbi"), N("r"), N("sn"), ALU.mult))
            dv(lambda h: h.tensor_tensor(N("den"), N("lr"), N("lr"), ALU.mult))
            dv(lambda h: h.tensor_tensor(N("t1"), N("li"), N("li"), ALU.mult))
            dv(lambda h: h.tensor_tensor(N("den"), N("den"), N("t1"), ALU.add))
            dv(lambda h: h.reciprocal(N("den"), N("den")))
            dv(lambda h: h.tensor_tensor(N("t1"), N("lbr"), N("lr"), ALU.mult))
            dv(lambda h: h.tensor_tensor(N("t2"), N("lbi"), N("li"), ALU.mult))
            dv(lambda h: h.tensor_tensor(N("kr"), N("t1"), N("t2"), ALU.add))
            dv(lambda h: h.tensor_tensor(N("kr"), N("kr"), N("den"), ALU.mult))
            dv(lambda h: h.tensor_tensor(N("t1"), N("lbi"), N("lr"), ALU.mult))
            dv(lambda h: h.tensor_tensor(N("t2"), N("lbr"), N("li"), ALU.mult))
            dv(lambda h: h.tensor_tensor(N("ki"), N("t1"), N("t2"), ALU.subtract))
            dv(lambda h: h.tensor_tensor(N("ki"), N("ki"), N("den"), ALU.mult))
            PL = {n: k.sb(es2, [128, 128], F32) for n in ("r", "fr", "kr", "ki", "sre", "sim")}
            t_PL = T()
            k.dma("sp", nat["t1"][0:Q, :], st_re[:, :], reads=R, writes=R)
            k.dma("sp", nat["t2"][0:Q, :], st_im[:, :], reads=R, writes=R)
            for n, srcn in (("r", "r"), ("fr", "fr"), ("kr", "kr"), ("ki", "ki"), ("sre", "t1"), ("sim", "t2")):
                k.op("pe", lambda h: h.transpose(pbank[0][:, 0:Q], nat[srcn][0:Q, :], ident[0:Q, 0:Q]), reads=R + [t_ident], writes=[t_pb[0]])
                k.op("dve", lambda h: h.tensor_copy(PL[n][:, 0:Q], pbank[0][:, 0:Q]), reads=[t_pb[0]], writes=[t_PL])
            carry = k.sb(es2, [128, 2, 2, 128], F32); t_carry = T()
            fin = k.sb(es2, [128, 2, 2, 128], F32); t_fin = T()
            k.op("dve", lambda h: h.memset(carry[:], 0.0), writes=[t_carry])
            k.op("dve", lambda h: h.memset(fin[:], 0.0), writes=[t_fin])
            tmpc = k.sb(es2, [128, 4], F32); t_tmpc = T()
            cosT = k.sb(es2, [128, WT], F32); sinT = k.sb(es2, [128, WT], F32); prT = k.sb(es2, [128, WT], F32); piT = k.sb(es2, [128, WT], F32)
            mT = k.sb(es2, [128, WT], F32); mI = k.sb(es2, [128, WT], I32); t_tab = T(); t_tabw = T()
            nb = [k.sb(es2, [128, 128], F32) for _ in range(2)]; t_nb = T()
            ncn = [k.sb(es2, [128, 128], F32) for _ in range(2)]; t_ncn = T()
            xc = [k.sb(es2, [128, 128], F32) for _ in range(2)]; t_xc = T()
            mb = k.sb(es2, [128, 128], F32); t_mb = T()
            lB = [k.sb(es2, [128, 128], BF16) for _ in range(2)]; t_lB = T()
            lC = [k.sb(es2, [128, 128], BF16) for _ in range(3)]; t_lC = T()
            wk = {n: k.sb(es2, [128, 512], F32) for n in ("a", "b", "c", "d", "tre", "tim", "gre", "gim")}
            t_wk = {n: T() for n in wk}
            gk = {n: k.sb(es2, [128, 512], BF16) for n in ("g1", "g2", "g3", "g4")}; t_gk = {n: T() for n in gk}
            yb = k.sb(es2, [128, 512], F32); t_yb = T()
            zb = [k.sb(es2, [128, 512], BF16) for _ in range(2)]; t_zb = [T(), T()]
            ntl = len(tiles)
            assert ntl <= 5
            for kc in range(KC):
                for ri, bsrc in enumerate((b_re, b_im)):
                    for hf in range(2):
                        k.dma("sp", nb[ri][hf * 64:(hf + 1) * 64, :].rearrange("p (g n) -> p g n", n=16),
                              bsrc[kc * 8:(kc + 1) * 8].rearrange("g p n -> p g n"), writes=[t_nb])
                for ri, csrc in enumerate((c_re, c_im)):
                    for hf in range(2):
                        k.dma("sp", ncn[ri][:, hf * 64:(hf + 1) * 64], csrc[kc * 128:(kc + 1) * 128, :], writes=[t_ncn])
                for ri in range(2):
                    k.op("pe", lambda h: h.transpose(pbank[0][:, 0:128], ncn[ri][:], ident[:]), reads=[t_ncn, t_ident], writes=[t_pb[0]])
                    k.op("dve", lambda h: h.tensor_copy(xc[ri][:], pbank[0][:, 0:128]), reads=[t_pb[0]], writes=[t_xc])
                for j in range(4):
                    q = kc * 4 + j
                    k.op("dve", lambda h: h.tensor_scalar(mT[:], iot[:], PL["fr"][:, q:q + 1], None, ALU.mult), reads=[t_iot, t_PL], writes=[t_tabw])
                    k.op("dve", lambda h: h.tensor_copy(mI[:], mT[:]), reads=[t_tabw], writes=[t_tabw])
                    k.op("dve", lambda h: h.tensor_copy(prT[:], mI[:]), reads=[t_tabw, t_tab], writes=[t_tab])
                    k.op("dve", lambda h: h.tensor_tensor(mT[:], mT[:], prT[:], ALU.subtract), reads=[t_tabw, t_tab], writes=[t_tabw])
                    k.op("act", lambda h: h.activation(sinT[:], mT[:], AF.Sin, scale=TWO_PI), reads=[t_tabw, t_tab], writes=[t_tab])
                    k.op("act", lambda h: h.activation(mT[:], mT[:], AF.Abs), reads=[t_tabw, t_tab], writes=[t_tabw])
                    k.op("act", lambda h: h.activation(cosT[:], mT[:], AF.Sin, scale=-TWO_PI, bias=halfpi[:]), reads=[t_tabw, t_tab], writes=[t_tab])
                    k.op("dve", lambda h: h.tensor_scalar(prT[:], sinT[:], PL["ki"][:, q:q + 1], None, ALU.mult), reads=[t_tab, t_PL], writes=[t_tab])
                    k.op("dve", lambda h: h.scalar_tensor_tensor(prT[:], cosT[:], PL["kr"][:, q:q + 1], prT[:], ALU.mult, ALU.add), reads=[t_tab, t_PL], writes=[t_tab])
                    k.op("dve", lambda h: h.tensor_scalar(piT[:], sinT[:], PL["kr"][:, q:q + 1], None, ALU.mult), reads=[t_tab, t_PL], writes=[t_tab])
                    k.op("dve", lambda h: h.scalar_tensor_tensor(piT[:], cosT[:], PL["ki"][:, q:q + 1], piT[:], ALU.mult, ALU.subtract), reads=[t_tab, t_PL], writes=[t_tab])
                    for ri in range(2):
                        k.op("dve", lambda h: h.tensor_tensor(mb[:], nb[ri][:], msel[:, j, :], ALU.mult), reads=[t_nb, t_msel], writes=[t_mb])
                        k.op("pe", lambda h: h.transpose(pbank[0][:, 0:128], mb[:], ident[:]), reads=[t_mb, t_ident], writes=[t_pb[0]])
                        k.op("dve", lambda h: h.tensor_copy(lB[ri][:], pbank[0][:, 0:128]), reads=[t_pb[0]], writes=[t_lB])
                    k.op("dve", lambda h: h.tensor_tensor(lC[0][:], xc[0][:], msel[:, j, :], ALU.mult), reads=[t_xc, t_msel], writes=[t_lC])
                    k.op("dve", lambda h: h.scalar_tensor_tensor(lC[1][:], xc[0][:], -1.0, msel[:, j, :], ALU.mult, ALU.mult), reads=[t_xc, t_msel], writes=[t_lC])
                    k.op("dve", lambda h: h.scalar_tensor_tensor(lC[2][:], xc[1][:], -1.0, msel[:, j, :], ALU.mult, ALU.mult), reads=[t_xc, t_msel], writes=[t_lC])
                    for ti, (c0, w, sq_) in enumerate(tiles):
                        first = (ti == 0) or (tiles[ti - 1][2] != sq_)
                        last = (ti == ntl - 1) or (tiles[ti + 1][2] != sq_)
                        if first and sq_ == 1:
                            k.op("dve", lambda h: h.tensor_tensor(tmpc[:, 0:1], PL["sim"][:, q:q + 1], sinT[:, 1:2], ALU.mult), reads=[t_PL, t_tab], writes=[t_tmpc])
                            k.op("dve", lambda h: h.scalar_tensor_tensor(carry[:, 1, 0, q:q + 1], PL["sre"][:, q:q + 1], cosT[:, 1:2], tmpc[:, 0:1], ALU.mult, ALU.subtract), reads=[t_PL, t_tab, t_tmpc], writes=[t_carry])
                            k.op("dve", lambda h: h.tensor_tensor(tmpc[:, 1:2], PL["sre"][:, q:q + 1], sinT[:, 1:2], ALU.mult), reads=[t_PL, t_tab], writes=[t_tmpc])
                            k.op("dve", lambda h: h.scalar_tensor_tensor(carry[:, 1, 1, q:q + 1], PL["sim"][:, q:q + 1], cosT[:, 1:2], tmpc[:, 1:2], ALU.mult, ALU.add), reads=[t_PL, t_tab, t_tmpc], writes=[t_carry])
                        for ri in range(2):
                            k.op("pe", lambda h: h.matmul(pbank[1 + ri][:, 0:w], lB[ri][:], hbuf[:, kc, c0:c0 + w], start=True, stop=True),
                                 reads=[t_lB, t_h], writes=[t_pb[1 + ri]])
                        W_ = slice(0, w)
                        k.op("dve", lambda h: h.tensor_tensor(wk["a"][:, W_], pbank[1][:, W_], prT[:, W_], ALU.mult), reads=[t_pb[1], t_tab], writes=[t_wk["a"]])
                        k.op("dve", lambda h: h.tensor_tensor(wk["b"][:, W_], pbank[2][:, W_], piT[:, W_], ALU.mult), reads=[t_pb[2], t_tab], writes=[t_wk["b"]])
                        k.op("dve", lambda h: h.tensor_tensor(wk["c"][:, W_], pbank[2][:, W_], prT[:, W_], ALU.mult), reads=[t_pb[2], t_tab], writes=[t_wk["c"]])
                        k.op("dve", lambda h: h.tensor_tensor(wk["d"][:, W_], pbank[1][:, W_], piT[:, W_], ALU.mult), reads=[t_pb[1], t_tab], writes=[t_wk["d"]])
                        k.op("pool", lambda h: h.tensor_tensor(wk["tre"][:, W_], wk["a"][:, W_], wk["b"][:, W_], ALU.subtract), reads=[t_wk["a"], t_wk["b"]], writes=[t_wk["tre"]])
                        k.op("pool", lambda h: h.tensor_tensor(wk["tim"][:, W_], wk["c"][:, W_], wk["d"][:, W_], ALU.add), reads=[t_wk["c"], t_wk["d"]], writes=[t_wk["tim"]])
                        rb = PL["r"][:, q:q + 1].to_broadcast([128, w])
                        k.op("dve", lambda h: h.tensor_tensor_scan(wk["gre"][:, W_], rb, wk["tre"][:, W_], carry[:, sq_, 0, q:q + 1], ALU.mult, ALU.add),
                             reads=[t_wk["tre"], t_PL, t_carry], writes=[t_wk["gre"]])
                        k.op("dve", lambda h: h.tensor_tensor_scan(wk["gim"][:, W_], rb, wk["tim"][:, W_], carry[:, sq_, 1, q:q + 1], ALU.mult, ALU.add),
                             reads=[t_wk["tim"], t_PL, t_carry], writes=[t_wk["gim"]])
                        gl_re = wk["gre"][:, w - 1:w]; gl_im = wk["gim"][:, w - 1:w]
                        RG = [t_wk["gre"], t_wk["gim"], t_tab]
                        if not last:
                            k.op("dve", lambda h: h.tensor_tensor(tmpc[:, 0:1], gl_im, sinT[:, w:w + 1], ALU.mult), reads=RG, writes=[t_tmpc])
                            k.op("dve", lambda h: h.scalar_tensor_tensor(carry[:, sq_, 0, q:q + 1], gl_re, cosT[:, w:w + 1], tmpc[:, 0:1], ALU.mult, ALU.subtract), reads=RG + [t_tmpc], writes=[t_carry])
                            k.op("dve", lambda h: h.tensor_tensor(tmpc[:, 1:2], gl_re, sinT[:, w:w + 1], ALU.mult), reads=RG, writes=[t_tmpc])
                            k.op("dve", lambda h: h.scalar_tensor_tensor(carry[:, sq_, 1, q:q + 1], gl_im, cosT[:, w:w + 1], tmpc[:, 1:2], ALU.mult, ALU.add), reads=RG + [t_tmpc], writes=[t_carry])
                        else:
                            k.op("dve", lambda h: h.tensor_tensor(tmpc[:, 2:3], gl_im, sinT[:, w - 1:w], ALU.mult), reads=RG, writes=[t_tmpc])
                            k.op("dve", lambda h: h.scalar_tensor_tensor(fin[:, sq_, 0, q:q + 1], gl_re, cosT[:, w - 1:w], tmpc[:, 2:3], ALU.mult, ALU.subtract), reads=RG + [t_tmpc], writes=[t_fin])
                            k.op("dve", lambda h: h.tensor_tensor(tmpc[:, 3:4], gl_re, sinT[:, w - 1:w], ALU.mult), reads=RG, writes=[t_tmpc])
                            k.op("dve", lambda h: h.scalar_tensor_tensor(fin[:, sq_, 1, q:q + 1], gl_im, cosT[:, w - 1:w], tmpc[:, 3:4], ALU.mult, ALU.add), reads=RG + [t_tmpc], writes=[t_fin])
                        k.op("pool", lambda h: h.tensor_tensor(gk["g1"][:, W_], wk["gre"][:, W_], cosT[:, W_], ALU.mult), reads=[t_wk["gre"], t_tab], writes=[t_gk["g1"]])
                        k.op("pool", lambda h: h.tensor_tensor(gk["g2"][:, W_], wk["gim"][:, W_], sinT[:, W_], ALU.mult), reads=[t_wk["gim"], t_tab], writes=[t_gk["g2"]])
                        k.op("dve", lambda h: h.tensor_tensor(gk["g3"][:, W_], wk["gim"][:, W_], cosT[:, W_], ALU.mult), reads=[t_wk["gim"], t_tab], writes=[t_gk["g3"]])
                        k.op("dve", lambda h: h.tensor_tensor(gk["g4"][:, W_], wk["gre"][:, W_], sinT[:, W_], ALU.mult), reads=[t_wk["gre"], t_tab], writes=[t_gk["g4"]])
                        yb_ = 3 + ti
                        for mi, (lc, gn) in enumerate(((0, "g1"), (1, "g2"), (2, "g3"), (2, "g4"))):
                            k.op("pe", lambda h: h.matmul(pbank[yb_][:, W_], lC[lc][:], gk[gn][:, W_], start=(j == 0 and mi == 0), stop=(j == 3 and mi == 3)),
                                 reads=[t_lC, t_gk[gn]], writes=[t_pb[yb_]])
                for ti, (c0, w, sq_) in enumerate(tiles):
                    W_ = slice(0, w); yb_ = 3 + ti; zi = ti % 2
                    k.op("dve", lambda h: h.scalar_tensor_tensor(yb[:, W_], hbuf[:, kc, c0:c0 + w], dskip[:, kc:kc + 1], pbank[yb_][:, W_], ALU.mult, ALU.add),
                         reads=[t_h, t_gains, t_pb[yb_]], writes=[t_yb])
                    k.op("pool", lambda h: h.tensor_tensor(wk["a"][:, W_], yb[:, W_], yb[:, W_], ALU.mult), reads=[t_yb], writes=[t_wk["a"]])
                    k.op("pool", lambda h: h.tensor_scalar(wk["a"][:, W_], wk["a"][:, W_], 0.044715, 1.0, ALU.mult, ALU.add), reads=[t_wk["a"]], writes=[t_wk["a"]])
                    k.op("pool", lambda h: h.tensor_tensor(wk["a"][:, W_], wk["a"][:, W_], yb[:, W_], ALU.mult), reads=[t_wk["a"], t_yb], writes=[t_wk["a"]])
                    k.op("act", lambda h: h.activation(wk["b"][:, W_], wk["a"][:, W_], AF.Sigmoid, scale=2.0 * math.sqrt(2.0 / math.pi)), reads=[t_wk["a"]], writes=[t_wk["b"]])
                    k.op("pool", lambda h: h.tensor_tensor(zb[zi][:, W_], wk["b"][:, W_], yb[:, W_], ALU.mult), reads=[t_wk["b"], t_yb], writes=[t_zb[zi]])
                    k.dma("sp", zTv[:, kc, c0:c0 + w], zb[zi][:, W_], reads=[t_zb[zi]], writes=[t_zT])
            for sq_, (ore, oim) in enumerate(((o_ssm_re_p, o_ssm_im_p), (o_ssm_re_s, o_ssm_im_s))):
                for ri, od in enumerate((ore, oim)):
                    k.op("pe", lambda h: h.transpose(pbank[0][0:Q, 0:128], fin[:, sq_, ri, 0:Q], ident[:]), reads=[t_fin, t_ident], writes=[t_pb[0]])
                    k.op("dve", lambda h: h.tensor_copy(nat["t1"][0:Q, :], pbank[0][0:Q, 0:128]), reads=[t_pb[0]] + R, writes=R)
                    k.dma("sp", od[:, :], nat["t1"][0:Q, :], reads=R, writes=[])
        k.barrier()

        k.dma("sp", hbuf[:], zTv[:, :, :], reads=[t_zT], writes=[t_h])
        with contextlib.ExitStack() as es2:
            xc_ = [k.sb(es2, [128, 512], F32) for _ in range(2)]; t_xc_ = [T(), T()]
            sg = k.sb(es2, [128, 512], F32); t_sg = T()
            cnt = [0]
            def glu_epi(idx, ti, pbs):
                c0, w, _s = tiles[ti]; b = cnt[0] % 2; cnt[0] += 1
                k.dma("sp", xc_[b][:, 0:w], xTv[:, idx, c0:c0 + w], reads=[t_xT], writes=[t_xc_[b]])
                k.op("act", lambda h: h.activation(sg[:, 0:w], pbank[pbs[1]][:, 0:w], AF.Sigmoid), reads=[t_pb[pbs[1]]], writes=[t_sg])
                k.op("dve", lambda h: h.tensor_tensor(sg[:, 0:w], sg[:, 0:w], pbank[pbs[0]][:, 0:w], ALU.mult), reads=[t_sg, t_pb[pbs[0]]], writes=[t_sg])
                k.op("dve", lambda h: h.tensor_tensor(xc_[b][:, 0:w], xc_[b][:, 0:w], sg[:, 0:w], ALU.add), reads=[t_sg, t_xc_[b]], writes=[t_xc_[b]])
                k.dma("sp", xTv[:, idx, c0:c0 + w], xc_[b][:, 0:w], reads=[t_xc_[b]], writes=[t_xT])
            dense_ws(es2, w_glu, [[m, KC + m] for m in range(KC)], hbuf, t_h, glu_epi)
        k.barrier()

    wdS = dscr("wdS", [KC, 128, FC, 128], BF16); t_wdS = [T() for _ in range(KC)]

    def ffn_layer(L):
        gi = (1, 4)[L]
        w_in_L, w_down_L = (w_in, w_in1)[L], (w_down, w_down1)[L]
        o_cp, o_cs = ((o_conv_p, o_conv_s), (o_conv_p1, o_conv_s1))[L]
        with contextlib.ExitStack() as es:
            hbuf = k.sb(es, [128, KC, NT], BF16); t_h = T()
            with contextlib.ExitStack() as es2:
                rmsnorm(hbuf, t_h, gi, es2)
            k.barrier()
            with contextlib.ExitStack() as es2:
                gb = k.sb(es2, [128, 514], F32); t_gb = T()
                gc = k.sb(es2, [128, 512], F32); t_gc = T()
                ab_ = [k.sb(es2, [128, 512], BF16) for _ in range(2)]; t_ab = [T(), T()]
                halo = k.sb(es2, [128, 2, 2], F32); t_halo = T()
                cnt = [0]
                def up_epi(f, ti, pbs):
                    c0, w, sq_ = tiles[ti]; b = cnt[0] % 2; cnt[0] += 1
                    first = (ti == 0) or (tiles[ti - 1][2] != sq_)
                    last = (ti == len(tiles) - 1) or (tiles[ti + 1][2] != sq_)
                    if first:
                        if sq_ == 0:
                            k.op("dve", lambda h: h.memset(gb[:, 0:2], 0.0), writes=[t_gb])
                        else:
                            k.op("dve", lambda h: h.tensor_copy(gb[:, 0:2], stc[:, L, f, :]), reads=[t_stc], writes=[t_gb])
                    else:
                        k.op("dve", lambda h: h.tensor_copy(gb[:, 0:2], halo[:, sq_, :]), reads=[t_halo], writes=[t_gb])
                    k.op("act", lambda h: h.copy(gb[:, 2:2 + w], pbank[pbs[1]][:, 0:w]), reads=[t_pb[pbs[1]]], writes=[t_gb])
                    k.op("dve", lambda h: h.tensor_copy(halo[:, sq_, :], gb[:, w:w + 2]), reads=[t_gb], writes=[t_halo])
                    if last:
                        k.op("dve", lambda h: h.tensor_copy(convo[:, sq_, :, f], gb[:, w:w + 2]), reads=[t_gb], writes=[t_convo])
                    k.op("dve", lambda h: h.tensor_scalar(gc[:, 0:w], gb[:, 2:2 + w], cwt[:, L, 2, f:f + 1], cwt[:, L, 3, f:f + 1], ALU.mult, ALU.add), reads=[t_gb, t_cw], writes=[t_gc])
                    k.op("dve", lambda h: h.scalar_tensor_tensor(gc[:, 0:w], gb[:, 1:1 + w], cwt[:, L, 1, f:f + 1], gc[:, 0:w], ALU.mult, ALU.add), reads=[t_gb, t_cw, t_gc], writes=[t_gc])
                    k.op("dve", lambda h: h.scalar_tensor_tensor(gc[:, 0:w], gb[:, 0:w], cwt[:, L, 0, f:f + 1], gc[:, 0:w], ALU.mult, ALU.add), reads=[t_gb, t_cw, t_gc], writes=[t_gc])
                    k.op("act", lambda h: h.activation(gc[:, 0:w], gc[:, 0:w], AF.Silu), reads=[t_gc], writes=[t_gc])
                    k.op("dve", lambda h: h.tensor_tensor(ab_[b][:, 0:w], gc[:, 0:w], pbank[pbs[0]][:, 0:w], ALU.mult), reads=[t_gc, t_pb[pbs[0]]], writes=[t_ab[b]])
                    k.dma("sp", actTv[:, f, c0:c0 + w], ab_[b][:, 0:w], reads=[t_ab[b]], writes=[t_act])
                dense_ws(es2, w_in_L, [[f, FC + f] for f in range(FC)], hbuf, t_h, up_epi)
                for sq_, od in enumerate((o_cp, o_cs)):
                    for t_ in range(2):
                        k.op("pe", lambda h: h.transpose(pbank[0][0:FC, 0:128], convo[:, sq_, t_, :], ident[:]), reads=[t_convo, t_ident], writes=[t_pb[0]])
                        k.op("dve", lambda h: h.tensor_copy(natv[0:FC, :], pbank[0][0:FC, 0:128]), reads=[t_pb[0]], writes=[t_natv])
                        k.dma("sp", od[t_].rearrange("(c p) -> c p", p=128), natv[0:FC, :], reads=[t_natv], writes=[])
        k.barrier()
        with contextlib.ExitStack() as es:
            at = k.sb(es, [128, FC, 512], BF16); t_at = T()
            wd = [k.sb(es, [128, FC, 128], BF16) for _ in range(2)]; t_wd = [T(), T()]
            xc_ = [k.sb(es, [128, 512], F32) for _ in range(2)]; t_xc_ = [T(), T()]
            Wd = w_down_L.rearrange("(c p) m -> p c m", p=128)
            cnt = 0
            for ti, (c0, w, sq_) in enumerate(tiles):
                k.dma("sp", at[:, :, 0:w], actTv[:, :, c0:c0 + w], reads=[t_act], writes=[t_at])
                for m in range(KC):
                    b = cnt % 2; cnt += 1
                    if ti == 0:
                        k.dma("pool", wd[b][:], Wd[:, :, m * 128:(m + 1) * 128], reads=[t_wdS[m]], writes=[t_wd[b]])
                        k.dma("sp", wdS[m], wd[b][:], reads=[t_wd[b]], writes=[t_wdS[m]])
                    else:
                        k.dma("sp", wd[b][:], wdS[m], reads=[t_wdS[m]], writes=[t_wd[b]])
                    k.dma("sp", xc_[b][:, 0:w], xTv[:, m, c0:c0 + w], reads=[t_xT], writes=[t_xc_[b]])
                    pb = 4 + b
                    for f in range(FC):
                        k.op("pe", lambda h: h.matmul(pbank[pb][:, 0:w], wd[b][:, f, :], at[:, f, 0:w], start=(f == 0), stop=(f == FC - 1)),
                             reads=[t_wd[b], t_at], writes=[t_pb[pb]], inc=(f == FC - 1))
                    k.op("dve", lambda h: h.tensor_tensor(xc_[b][:, 0:w], xc_[b][:, 0:w], pbank[pb][:, 0:w], ALU.add), reads=[t_pb[pb], t_xc_[b]], writes=[t_xc_[b]])
                    k.dma("sp", xTv[:, m, c0:c0 + w], xc_[b][:, 0:w], reads=[t_xc_[b]], writes=[t_xT])
        k.barrier()

    ffn_layer(0)

    KT = dscr("KT", [4, 4, 128, TP], BF16); t_KT = T()
    Vtok = dscr("Vtok", [2, TP, 512], BF16); t_Vtok = T()
    kvS = dscr("kvS", [NS, NKV], F32); t_kvS = T()
    ktmap = {0: 0, 1: 1, 2: 2, 4: 3}; vmap = {3: 0, 5: 1}
    with contextlib.ExitStack() as es:
        hbuf = k.sb(es, [128, KC, NT], BF16); t_h = T()
        with contextlib.ExitStack() as es2:
            rmsnorm(hbuf, t_h, 2, es2)
        k.barrier()
        wkb = [k.sb(es, [128, KC, 256], BF16) for _ in range(2)]; t_wkb = [T(), T()]
        ob = [k.sb(es, [128, 256], F32) for _ in range(2)]; t_ob = [T(), T()]
        ktb = [k.sb(es, [128, 2, 128], BF16) for _ in range(2)]; t_ktb = [T(), T()]
        vb = [k.sb(es, [128, 256], BF16) for _ in range(2)]; t_vb = [T(), T()]
        Wk = w_kv.rearrange("(c p) m -> p c m", p=128)
        blocks = [(i * 128, min(128, TP - i * 128), 0) for i in range((TP + 127) // 128)] + [(TP, NS, 1)]
        cnt = 0
        k.dma("sp", o_win_s[0:512 - NS, :], cache_win[NS:512, :])
        for nb_ in range(NKV // 256):
            b = nb_ % 2; br = nb_ // 2; kp = nb_ % 2
            k.dma("pool", wkb[b][:], Wk[:, :, nb_ * 256:(nb_ + 1) * 256], writes=[t_wkb[b]])
            for (c0, nr, sq_) in blocks:
                ob_i = cnt % 2; pb = 4 + ob_i; cnt += 1
                for c in range(KC):
                    k.op("pe", lambda h: h.matmul(pbank[pb][0:nr, 0:256], hbuf[:, c, c0:c0 + nr], wkb[b][:, c, :], start=(c == 0), stop=(c == KC - 1)),
                         reads=[t_wkb[b], t_h], writes=[t_pb[pb]], inc=(c == KC - 1))
                k.op("act", lambda h: h.copy(ob[ob_i][0:nr, :], pbank[pb][0:nr, 0:256]), reads=[t_pb[pb]], writes=[t_ob[ob_i]])
                if sq_ == 1:
                    k.dma("sp", kvS[0:nr, nb_ * 256:(nb_ + 1) * 256], ob[ob_i][0:nr, :], reads=[t_ob[ob_i]], writes=[t_kvS])
                elif br in ktmap:
                    for i2 in range(2):
                        k.op("pe", lambda h: h.transpose(pbank[2 + ob_i][:, i2 * 128:i2 * 128 + nr], ob[ob_i][0:nr, i2 * 128:(i2 + 1) * 128], ident[0:nr, 0:nr]),
                             reads=[t_ob[ob_i], t_ident], writes=[t_pb[2 + ob_i]], inc=(i2 == 1))
                    k.op("dve", lambda h: h.tensor_copy(ktb[ob_i][:, :, 0:nr], pbank[2 + ob_i][:, 0:256].rearrange("p (a t) -> p a t", t=128)[:, :, 0:nr]),
                         reads=[t_pb[2 + ob_i]], writes=[t_ktb[ob_i]])
                    k.dma("sp", KT[ktmap[br], 2 * kp:2 * kp + 2, :, c0:c0 + nr].rearrange("a p t -> p a t"), ktb[ob_i][:, :, 0:nr], reads=[t_ktb[ob_i]], writes=[t_KT])
                else:
                    k.op("dve", lambda h: h.tensor_copy(vb[ob_i][0:nr, :], ob[ob_i][0:nr, :]), reads=[t_ob[ob_i]], writes=[t_vb[ob_i]])
                    k.dma("sp", Vtok[vmap[br], c0:c0 + nr, kp * 256:(kp + 1) * 256], vb[ob_i][0:nr, :], reads=[t_vb[ob_i]], writes=[t_Vtok])
                if nb_ < 8:
                    od = o_kv_p[c0:c0 + nr, nb_ * 256:(nb_ + 1) * 256] if sq_ == 0 else o_kv_s[0:nr, nb_ * 256:(nb_ + 1) * 256]
                    k.dma("sp", od, ob[ob_i][0:nr, :], reads=[t_ob[ob_i]])
                else:
                    wc = (nb_ - 8) * 256
                    if sq_ == 0:
                        lo = max(c0, TP - 512)
                        if c0 + nr > lo:
                            k.dma("sp", o_win_p[lo - (TP - 512):c0 + nr - (TP - 512), wc:wc + 256], ob[ob_i][lo - c0:nr, :], reads=[t_ob[ob_i]])
                    else:
                        k.dma("sp", o_win_s[512 - NS:512, wc:wc + 256], ob[ob_i][0:nr, :], reads=[t_ob[ob_i]])
    k.barrier()

    SCALE = 128 ** -0.5
    NEG = 30000.0
    BIG = 1.0e30
    NQB = TP // 128; NC_ = TP // 16 - 1; NB = TP // 64
    AXX = mybir.AxisListType.X
    QT = dscr("QT", [32, 128, NT], BF16); t_QT = T()
    OT = dscr("OT", [D, NT], BF16); t_OT = T()
    OTv = OT.rearrange("(c p) t -> p c t", p=128)
    gat = k.sb(glob, [128, NQB + 1, 96], F32); t_gat = T()
    ident_bf = k.sb(glob, [128, 128], BF16)
    k.op("dve", lambda h: h.tensor_copy(ident_bf[:], ident[:]), reads=[t_ident], writes=[t_ident])
    pT7 = pbank[7][:, :].bitcast(BF16)

    with contextlib.ExitStack() as es:
        hbuf = k.sb(es, [128, KC, NT], BF16); t_h = T()
        with contextlib.ExitStack() as es2:
            rmsnorm(hbuf, t_h, 3, es2)
        k.barrier()
        wg = k.sb(es, [128, KC, 96], BF16); t_wg = T()
        k.dma("pool", wg[:], w_qg.rearrange("(c p) m -> p c m", p=128)[:, :, D:D + 96], writes=[t_wg])
        blocks = [(i * 128, min(128, TP - i * 128)) for i in range(NQB)] + [(TP, NS)]
        for bi, (c0, nr) in enumerate(blocks):
            pb = 2 + bi % 2
            for c in range(KC):
                k.op("pe", lambda h: h.matmul(pbank[pb][0:nr, 0:96], hbuf[:, c, c0:c0 + nr], wg[:, c, :], start=(c == 0), stop=(c == KC - 1)),
                     reads=[t_wg, t_h], writes=[t_pb[pb]], inc=(c == KC - 1))
            k.op("act", lambda h: h.activation(gat[0:nr, bi, :], pbank[pb][0:nr, 0:96], AF.Sigmoid), reads=[t_pb[pb]], writes=[t_gat])
        qb_ = [k.sb(es, [128, 512], BF16) for _ in range(2)]; t_qb = [T(), T()]
        cnt = [0]
        def q_epi(idx, ti, pbs):
            c0, w, _s = tiles[ti]; b = cnt[0] % 2; cnt[0] += 1
            k.op("act", lambda h: h.copy(qb_[b][:, 0:w], pbank[pbs[0]][:, 0:w]), reads=[t_pb[pbs[0]]], writes=[t_qb[b]])
            k.dma("sp", QT[idx, :, c0:c0 + w], qb_[b][:, 0:w], reads=[t_qb[b]], writes=[t_QT])
        dense_ws(es, w_qg, [[m] for m in range(D // 128)], hbuf, t_h, q_epi)
    k.barrier()

    def gelu_to(dst, src, tmp_a, tmp_b, t_src, t_tmp, t_dst):
        k.op("pool", lambda h: h.tensor_tensor(tmp_a, src, src, ALU.mult), reads=[t_src], writes=[t_tmp])
        k.op("pool", lambda h: h.tensor_scalar(tmp_a, tmp_a, 0.044715, 1.0, ALU.mult, ALU.add), reads=[t_tmp], writes=[t_tmp])
        k.op("pool", lambda h: h.tensor_tensor(tmp_a, tmp_a, src, ALU.mult), reads=[t_tmp, t_src], writes=[t_tmp])
        k.op("act", lambda h: h.activation(tmp_b, tmp_a, AF.Sigmoid, scale=2.0 * math.sqrt(2.0 / math.pi)), reads=[t_tmp], writes=[t_tmp])
        k.op("pool", lambda h: h.tensor_tensor(dst, tmp_b, src, ALU.mult), reads=[t_tmp, t_src], writes=[t_dst])

    with contextlib.ExitStack() as es:
        cmpKT = k.sb(es, [128, 4, 128], BF16); t_cK = T()
        cmpV = k.sb(es, [128, 4, 128], BF16); t_cV = T()
        cmpKT_s = k.sb(es, [128, 4, 1024], BF16); cmpV_s = k.sb(es, [128, 8, 4, 128], BF16); t_cKs = T()
        idx = k.sb(es, [128, 128], I32); t_idx = T()
        idx_c = k.sb(es, [128, 128], I32); idx_k = [k.sb(es, [128, 128], I32) for _ in range(4)]; idx_v = [k.sb(es, [128, 128], I32) for _ in range(4)]
        cache_h = cache_kv.rearrange("r (s c) -> (r s) c", c=1024)
        cache_s = cache_kv.rearrange("r (s c) -> (r s) c", c=128)
        st = k.sb(es, [128, 8], F32); t_st = T()
        with contextlib.ExitStack() as es2:
            w1b = [k.sb(es2, [128, 32, 256], BF16) for _ in range(2)]; t_w1 = T()
            w2b = [k.sb(es2, [128, 2, 128], BF16) for _ in range(2)]; t_w2 = T()
            b1c = k.sb(es2, [128, 2, 2], F32); b2c = k.sb(es2, [128, 2], F32); t_bc = T()
            b2r = k.sb(es2, [1, 2, 128], F32)
            peT = k.sb(es2, [128, 2, 32], BF16); t_pe = T()
            cb = k.sb(es2, [128, 2, 2], F32); t_cb = T()
            for b in range(2):
                k.dma("pool", w1b[b][:], cmp_w1[b].rearrange("(t p) m -> p t m", p=128), writes=[t_w1])
                k.dma("pool", w2b[b][:], cmp_w2[b].rearrange("(c p) m -> p c m", p=128), writes=[t_w2])
                load_cols(b1c[:, b, :], t_bc, cmp_b1[b], 2)
                load_cols(b2c[:, b:b + 1], t_bc, cmp_b2[b], 1)
                k.dma("sp", b2r[0:1, b, :], cmp_b2[b:b + 1, :], writes=[t_bc])
                k.dma("sp", natv[0:32, :], cmp_pe[b], writes=[t_natv])
                k.op("pe", lambda h: h.transpose(pbank[0][:, 0:32], natv[0:32, :], ident[0:32, 0:32]), reads=[t_natv, t_ident], writes=[t_pb[0]])
                k.op("dve", lambda h: h.tensor_copy(peT[:, b, :], pbank[0][:, 0:32]), reads=[t_pb[0]], writes=[t_pe])
            for b in range(2):
                for hc in range(2):
                    for t_ in range(32):
                        k.op("pe", lambda h: h.matmul(pbank[0][:, 0:1], w1b[b][:, t_, hc * 128:(hc + 1) * 128], peT[:, b, t_:t_ + 1], start=(t_ == 0), stop=(t_ == 31)),
                             reads=[t_w1, t_pe], writes=[t_pb[0]], inc=(t_ == 31))
                    k.op("dve", lambda h: h.tensor_tensor(cb[:, b, hc:hc + 1], pbank[0][:, 0:1], b1c[:, b, hc:hc + 1], ALU.add), reads=[t_pb[0], t_bc], writes=[t_cb])
            xTt = k.sb(es2, [128, TP], BF16); t_xTt = T()
            pre = k.sb(es2, [128, 128], F32); t_pre = T()
            ta = k.sb(es2, [128, 128], F32); tb = k.sb(es2, [128, 128], F32); t_tt = T()
            hid = k.sb(es2, [128, 2, 128], BF16); t_hid = T()

            def compress(b, rhs_fn, t_src, nblk, dstK, dstV, t_dst):
                for hc in range(2):
                    for t_ in range(32):
                        k.op("pe", lambda h: h.matmul(pbank[1 + hc][:, 0:nblk], w1b[b][:, t_, hc * 128:(hc + 1) * 128], rhs_fn(t_), start=(t_ == 0), stop=(t_ == 31)),
                             reads=[t_w1, t_src], writes=[t_pb[1 + hc]], inc=(t_ == 31))
                    k.op("dve", lambda h: h.tensor_scalar(pre[:, 0:nblk], pbank[1 + hc][:, 0:nblk], cb[:, b, hc:hc + 1], None, ALU.add), reads=[t_pb[1 + hc], t_cb], writes=[t_pre])
                    gelu_to(hid[:, hc, 0:nblk], pre[:, 0:nblk], ta[:, 0:nblk], tb[:, 0:nblk], t_pre, t_tt, t_hid)
                if b == 0:
                    for hc in range(2):
                        k.op("pe", lambda h: h.matmul(pbank[3][:, 0:nblk], w2b[0][:, hc, :], hid[:, hc, 0:nblk], start=(hc == 0), stop=(hc == 1)),
                             reads=[t_w2, t_hid], writes=[t_pb[3]], inc=(hc == 1))
                    k.op("dve", lambda h: h.tensor_scalar(dstK, pbank[3][:, 0:nblk], b2c[:, 0:1], None, ALU.add), reads=[t_pb[3], t_bc], writes=[t_dst])
                else:
                    for hc in range(2):
                        k.op("pe", lambda h: h.matmul(pbank[3][0:nblk, 0:128], hid[:, hc, 0:nblk], w2b[1][:, hc, :], start=(hc == 0), stop=False),
                             reads=[t_w2, t_hid], writes=[t_pb[3]], inc=False)
                    k.op("pe", lambda h: h.matmul(pbank[3][0:nblk, 0:128], ones[0:1, 0:nblk], b2r[0:1, 1, :], start=False, stop=True),
                         reads=[t_ones, t_bc], writes=[t_pb[3]])
                    k.op("dve", lambda h: h.tensor_copy(dstV, pbank[3][0:nblk, 0:128]), reads=[t_pb[3]], writes=[t_dst])

            for b in range(2):
                for kvh in range(4):
                    k.dma("sp", xTt[:], KT[b, kvh], reads=[t_KT], writes=[t_xTt])
                    compress(b, lambda t_: xTt[:, t_:t_ + 16 * (NC_ - 1) + 1:16], t_xTt, NC_,
                             cmpKT[:, kvh, 0:NC_], cmpV[0:NC_, kvh, :], t_cK if b == 0 else t_cV)

            pt_i = k.sb(es2, [128, 128], I32); ptf = k.sb(es2, [128, 128], F32); pidxf = k.sb(es2, [128, 1], F32); pidxi = k.sb(es2, [128, 1], I32)
            k.dma("sp", pt_i[:], page_tab[0:1, :].to_broadcast([128, 128]), writes=[t_idx])
            k.op("pool", lambda h: h.iota(pidxi[:], pattern=[[0, 1]], base=0, channel_multiplier=1), writes=[t_idx])
            k.op("dve", lambda h: h.tensor_copy(pidxf[:], pidxi[:]), reads=[t_idx], writes=[t_idx])
            k.op("dve", lambda h: h.tensor_copy(ptf[:], pt_i[:]), reads=[t_idx], writes=[t_idx])
            k.op("dve", lambda h: h.tensor_scalar(ptf[:], ptf[:], 128.0, pidxf[:, 0:1], ALU.mult, ALU.add), reads=[t_idx], writes=[t_idx])
            k.op("dve", lambda h: h.tensor_copy(idx[:], ptf[:]), reads=[t_idx], writes=[t_idx])
            ptf2 = k.sb(es2, [128, 128], F32)
            k.op("dve", lambda h: h.tensor_scalar(ptf2[:], ptf[:], 2.0, None, ALU.mult), reads=[t_idx], writes=[t_idx])
            k.op("dve", lambda h: h.tensor_copy(idx_c[:], ptf2[:]), reads=[t_idx], writes=[t_idx])
            for kvh in range(4):
                k.op("dve", lambda h: h.tensor_scalar(ptf2[:], ptf[:], 16.0, float(8 + kvh), ALU.mult, ALU.add), reads=[t_idx], writes=[t_idx])
                k.op("dve", lambda h: h.tensor_copy(idx_k[kvh][:], ptf2[:]), reads=[t_idx], writes=[t_idx])
                k.op("dve", lambda h: h.tensor_scalar(ptf2[:], ptf[:], 16.0, float(12 + kvh), ALU.mult, ALU.add), reads=[t_idx], writes=[t_idx])
                k.op("dve", lambda h: h.tensor_copy(idx_v[kvh][:], ptf2[:]), reads=[t_idx], writes=[t_idx])
            pgb = [k.sb(es2, [128, 1024], F32) for _ in range(4)]; t_pgb = [T() for _ in range(4)]
            XT = k.sb(es2, [128, 8, 2064], BF16); t_XT = T()
            k.op("dve", lambda h: h.memset(XT[:, :, 0:16], 0.0), writes=[t_XT])
            for gi in range(8):
                for pl in range(16):
                    pg_ = gi * 16 + pl; b2 = pg_ % 4
                    k.idma(pgb[b2][:], cache_h, idx_c[:, pg_:pg_ + 1], reads=[t_idx], writes=[t_pgb[b2]])
                    for b4 in range(2):
                        pbx = 4 + b4
                        for c in range(4):
                            bk = b4 * 4 + c
                            k.op("pe", lambda h: h.transpose(pbank[pbx][:, c * 128:(c + 1) * 128], pgb[b2][:, bk * 128:(bk + 1) * 128], ident[:]),
                                 reads=[t_pgb[b2], t_ident], writes=[t_pb[pbx]], inc=(c == 3))
                        k.op("act" if b4 == 0 else "dve", lambda h: (h.copy if b4 == 0 else h.tensor_copy)(XT[:, b4 * 4:b4 * 4 + 4, 16 + pl * 128:16 + (pl + 1) * 128], pbank[pbx][:, 0:512].rearrange("p (a t) -> p a t", t=128)),
                             reads=[t_pb[pbx]], writes=[t_XT])
                nblk = 127 if gi == 0 else 128; j0 = 1 if gi == 0 else 0; cbase = 0 if gi == 0 else 128 * gi - 1
                for b in range(2):
                    for kvh in range(4):
                        bk = b * 4 + kvh
                        compress(b, lambda t_: XT[:, bk, t_ + 16 * j0:t_ + 16 * j0 + 16 * (nblk - 1) + 1:16], t_XT, nblk,
                                 cmpKT_s[:, kvh, cbase:cbase + nblk], cmpV_s[0:nblk, gi, kvh, :], t_cKs)
                k.op("dve", lambda h: h.tensor_copy(XT[:, :, 0:16], XT[:, :, 2048:2064]), reads=[t_XT], writes=[t_XT])
        k.barrier()

        esA = contextlib.ExitStack()
        cm01 = k.sb(esA, [128, NQB, NC_], F32); cmadd = k.sb(esA, [128, NQB, NC_], F32); t_cm = T()
        keep = k.sb(esA, [128, NQB, NB], F32); over = k.sb(esA, [128, NQB, NB], F32); t_ko = T()
        tri = k.sb(esA, [128, 128], F32); W640 = k.sb(esA, [128, 640], F32); t_tri = T()
        ovl = k.sb(esA, [128, NB], F32); t_ovl = T()
        esM = contextlib.ExitStack()
        mi = k.sb(esM, [128, NQB * 128], I32); t_mi = T()
        dT = k.sb(esM, [128, NQB, NB], F32); fT = k.sb(esM, [128, NQB, NB], F32); pidx = k.sb(esM, [128, 1], F32)
        wtmp = k.sb(esM, [128, 640], F32); ovt = k.sb(esM, [128, NB], F32)
        k.op("pool", lambda h: h.iota(mi[:, 0:NQB * NC_].rearrange("p (a c) -> p a c", c=NC_), pattern=[[128, NQB], [-16, NC_]], base=-31, channel_multiplier=1), writes=[t_mi])
        k.op("dve", lambda h: h.tensor_copy(cm01[:], mi[:, 0:NQB * NC_].rearrange("p (a c) -> p a c", c=NC_)), reads=[t_mi], writes=[t_cm])
        k.op("dve", lambda h: h.tensor_single_scalar(cm01[:], cm01[:], 0.0, ALU.is_ge), reads=[t_cm], writes=[t_cm])
        k.op("dve", lambda h: h.tensor_scalar(cmadd[:], cm01[:], NEG, -NEG, ALU.mult, ALU.add), reads=[t_cm], writes=[t_cm])
        V3 = lambda: mi[:, 0:NQB * NB].rearrange("p (a c) -> p a c", c=NB)
        k.op("pool", lambda h: h.iota(mi[:, 0:1], pattern=[[0, 1]], base=0, channel_multiplier=1), reads=[t_cm], writes=[t_mi])
        k.op("dve", lambda h: h.tensor_copy(pidx[:], mi[:, 0:1]), reads=[t_mi], writes=[t_ko])
        k.op("dve", lambda h: h.tensor_single_scalar(pidx[:], pidx[:], 64.0, ALU.is_ge), reads=[t_ko], writes=[t_ko])
        k.op("pool", lambda h: h.iota(V3(), pattern=[[-2, NQB], [1, NB]], base=0, channel_multiplier=0), reads=[t_ko], writes=[t_mi])
        k.op("dve", lambda h: h.tensor_copy(dT[:], V3()), reads=[t_mi], writes=[t_ko])
        k.op("dve", lambda h: h.tensor_scalar(dT[:], dT[:], pidx[:, 0:1], None, ALU.subtract), reads=[t_ko], writes=[t_ko])
        k.op("dve", lambda h: h.tensor_single_scalar(over[:], dT[:], 0.0, ALU.is_gt), reads=[t_ko], writes=[t_ko])
        k.op("dve", lambda h: h.tensor_single_scalar(keep[:], dT[:], 0.0, ALU.is_equal), reads=[t_ko], writes=[t_ko])
        k.op("dve", lambda h: h.tensor_single_scalar(fT[:], dT[:], -1.0, ALU.is_equal), reads=[t_ko], writes=[t_ko])
        k.op("dve", lambda h: h.tensor_tensor(keep[:], keep[:], fT[:], ALU.max), reads=[t_ko], writes=[t_ko])
        k.op("pool", lambda h: h.iota(V3(), pattern=[[0, NQB], [1, NB]], base=0, channel_multiplier=0), reads=[t_ko], writes=[t_mi])
        k.op("dve", lambda h: h.tensor_copy(fT[:], V3()), reads=[t_mi], writes=[t_ko])
        k.op("dve", lambda h: h.tensor_single_scalar(fT[:], fT[:], 0.0, ALU.is_equal), reads=[t_ko], writes=[t_ko])
        k.op("dve", lambda h: h.tensor_tensor(keep[:], keep[:], fT[:], ALU.max), reads=[t_ko], writes=[t_ko])
        k.op("dve", lambda h: h.tensor_tensor(fT[:], keep[:], over[:], ALU.subtract), reads=[t_ko], writes=[t_ko])
        k.op("dve", lambda h: h.tensor_tensor(keep[:], keep[:], over[:], ALU.add), reads=[t_ko], writes=[t_ko])
        k.op("dve", lambda h: h.tensor_scalar(keep[:], keep[:], -1.0, 1.0, ALU.mult, ALU.add), reads=[t_ko], writes=[t_ko])
        k.op("dve", lambda h: h.tensor_scalar(over[:], fT[:], BIG, None, ALU.mult), reads=[t_ko], writes=[t_ko])
        k.op("pool", lambda h: h.iota(mi[:, 0:128], pattern=[[-1, 128]], base=0, channel_multiplier=1), reads=[t_ko], writes=[t_mi])
        k.op("dve", lambda h: h.tensor_copy(tri[:], mi[:, 0:128]), reads=[t_mi], writes=[t_tri])
        k.op("dve", lambda h: h.tensor_single_scalar(tri[:], tri[:], 0.0, ALU.is_ge), reads=[t_tri], writes=[t_tri])
        k.op("dve", lambda h: h.tensor_scalar(tri[:], tri[:], NEG, -NEG, ALU.mult, ALU.add), reads=[t_tri], writes=[t_tri])
        k.op("pool", lambda h: h.iota(mi[:, 0:640], pattern=[[-1, 640]], base=512, channel_multiplier=1), reads=[t_tri], writes=[t_mi])
        k.op("dve", lambda h: h.tensor_copy(W640[:], mi[:, 0:640]), reads=[t_mi], writes=[t_tri])
        k.op("dve", lambda h: h.tensor_single_scalar(W640[:], W640[:], 0.0, ALU.is_ge), reads=[t_tri], writes=[t_tri])
        k.op("pool", lambda h: h.iota(mi[:, 0:640], pattern=[[1, 640]], base=-1, channel_multiplier=-1), reads=[t_tri], writes=[t_mi])
        k.op("dve", lambda h: h.tensor_copy(wtmp[:], mi[:, 0:640]), reads=[t_mi], writes=[t_tri])
        k.op("dve", lambda h: h.tensor_single_scalar(wtmp[:], wtmp[:], 0.0, ALU.is_ge), reads=[t_tri], writes=[t_tri])
        k.op("dve", lambda h: h.tensor_tensor(W640[:], W640[:], wtmp[:], ALU.mult), reads=[t_tri], writes=[t_tri])
        k.op("dve", lambda h: h.tensor_scalar(W640[:], W640[:], NEG, -NEG, ALU.mult, ALU.add), reads=[t_tri], writes=[t_tri])
        k.op("pool", lambda h: h.iota(mi[:, 0:NB], pattern=[[64, NB]], base=64, channel_multiplier=-16), reads=[t_tri], writes=[t_mi])
        k.op("dve", lambda h: h.tensor_copy(ovl[:], mi[:, 0:NB]), reads=[t_mi], writes=[t_ovl])
        k.op("dve", lambda h: h.tensor_single_scalar(ovl[:], ovl[:], 0.0, ALU.is_gt), reads=[t_ovl], writes=[t_ovl])
        k.op("pool", lambda h: h.iota(mi[:, 0:NB], pattern=[[-64, NB]], base=32, channel_multiplier=16), reads=[t_ovl], writes=[t_mi])
        k.op("dve", lambda h: h.tensor_copy(ovt[:], mi[:, 0:NB]), reads=[t_mi], writes=[t_ovl])
        k.op("dve", lambda h: h.tensor_single_scalar(ovt[:], ovt[:], 0.0, ALU.is_gt), reads=[t_ovl], writes=[t_ovl])
        k.op("dve", lambda h: h.tensor_tensor(ovl[:], ovl[:], ovt[:], ALU.mult), reads=[t_ovl], writes=[t_ovl])

        k.barrier()
        esM.close()
        KsT = k.sb(esA, [128, 4, TP], BF16); KwT = k.sb(esA, [128, 4, TP], BF16); t_Kx = T()
        Vs = k.sb(esA, [128, NQB, 512], BF16); Vw = k.sb(esA, [128, NQB, 512], BF16); t_Vx = T()
        for kvh in range(4):
            k.dma("sp", KsT[:, kvh, :], KT[2, kvh], reads=[t_KT], writes=[t_Kx])
            k.dma("sp", KwT[:, kvh, :], KT[3, kvh], reads=[t_KT], writes=[t_Kx])
        k.dma("sp", Vs[:], Vtok[0].rearrange("(b p) m -> p b m", p=128), reads=[t_Vtok], writes=[t_Vx])
        k.dma("sp", Vw[:], Vtok[1].rearrange("(b p) m -> p b m", p=128), reads=[t_Vtok], writes=[t_Vx])
        _q = k.sb(esA, [128, 32, 128], BF16); _tq = T(); QTb = [_q, _q]; t_QTb = [_tq, _tq]
        _o = k.sb(esA, [128, 32, 128], BF16); _to = T(); OTb = [_o, _o]; t_OTb = [_to, _to]
        sm2 = [k.sb(esA, [128, TP], F32) for _ in range(2)]; t_sm2 = [T(), T()]
        pg2 = [k.sb(esA, [128, TP], BF16) for _ in range(2)]; t_pg2 = [T(), T()]
        smw2 = [k.sb(esA, [128, 640], F32) for _ in range(2)]; t_smw2 = [T(), T()]
        pgw2 = [k.sb(esA, [128, 640], BF16) for _ in range(2)]; t_pgw2 = [T(), T()]
        PTs2 = [k.sb(esA, [128, NQB, 128], BF16) for _ in range(2)]; t_PTs2b = [T(), T()]
        PTw2 = [k.sb(esA, [128, 5, 128], BF16) for _ in range(2)]; t_PTw2 = [T(), T()]
        pcT = k.sb(esA, [128, 8, 128], BF16); t_pcT = T()
        smc = k.sb(esA, [128, 128], F32); exc = k.sb(esA, [128, 128], F32); t_smc = T()
        pgc = k.sb(esA, [128, 128], BF16); t_pgc = T()
        psP = k.sb(esA, [128, 128], F32); t_psP = T()
        psT = k.sb(esA, [128, 128], F32); t_psT = T()
        impm = k.sb(esA, [128, NB], F32); impw = k.sb(esA, [128, NB], F32); m8 = k.sb(esA, [128, 16], F32); t_imp = T()
        selb = k.sb(esA, [128, NB], F32); t_selb = T()
        bfull = k.sb(esA, [128, TP], F32); t_bf = T()

        def softmax_gated(sm_ap, ex_ap, pg_ap, gate_ap, nq, t_s, t_e, t_p):
            k.op("dve", lambda h: h.reduce_max(st[0:nq, 0:1], sm_ap, AXX), reads=[t_s], writes=[t_st])
            k.op("dve", lambda h: h.tensor_scalar(st[0:nq, 1:2], st[0:nq, 0:1], -1.0, None, ALU.mult), reads=[t_st], writes=[t_st])
            k.op("act", lambda h: h.activation(ex_ap, sm_ap, AF.Exp, bias=st[0:nq, 1:2], scale=1.0, accum_out=st[0:nq, 2:3]), reads=[t_s, t_st], writes=[t_e, t_st])
            k.op("dve", lambda h: h.reciprocal(st[0:nq, 3:4], st[0:nq, 2:3]), reads=[t_st], writes=[t_st])
            k.op("dve", lambda h: h.tensor_tensor(st[0:nq, 4:5], st[0:nq, 3:4], gate_ap, ALU.mult), reads=[t_st, t_gat], writes=[t_st])
            k.op("act", lambda h: h.activation(pg_ap, ex_ap, AF.Copy, scale=st[0:nq, 4:5]), reads=[t_e, t_st], writes=[t_p])

        def transposes(dst, t_dst, src, t_src, nblk, nq):
            for j0 in range(0, nblk, 4):
                n4 = min(4, nblk - j0)
                for j in range(n4):
                    k.op("pe", lambda h: h.transpose(pT7[:, j * 128:j * 128 + nq], src[0:nq, (j0 + j) * 128:(j0 + j + 1) * 128], ident_bf[0:nq, 0:nq]),
                         reads=[t_src, t_ident], writes=[t_pb[7]], inc=(j == n4 - 1))
                k.op("act", lambda h: h.copy(dst[:, j0:j0 + n4, 0:nq], pT7[:, 0:n4 * 128].rearrange("p (a t) -> p a t", t=128)[:, :, 0:nq]),
                     reads=[t_pb[7]], writes=[t_dst])

        for qb in range(NQB):
            qi = qb % 2; nk = 128 * (qb + 1); nq = 128
            k.dma("sp", QTb[qi][:], QT[:, :, qb * 128:(qb + 1) * 128].rearrange("h p t -> p h t"), reads=[t_QT], writes=[t_QTb[qi]])
            for kvh in range(4):
                for g in range(8):
                    hh = kvh * 8 + g
                    k.op("pe", lambda h: h.matmul(pbank[0][:, 0:NC_], QTb[qi][:, hh, :], cmpKT[:, kvh, 0:NC_], start=True, stop=True), reads=[t_QTb[qi], t_cK], writes=[t_pb[0]])
                    k.op("dve", lambda h: h.scalar_tensor_tensor(smc[:, 0:NC_], pbank[0][:, 0:NC_], SCALE, cmadd[:, qb, :], ALU.mult, ALU.add), reads=[t_pb[0], t_cm], writes=[t_smc])
                    k.op("dve", lambda h: h.reduce_max(st[:, 0:1], smc[:, 0:NC_], AXX), reads=[t_smc], writes=[t_st])
                    k.op("dve", lambda h: h.tensor_scalar(st[:, 1:2], st[:, 0:1], -1.0, None, ALU.mult), reads=[t_st], writes=[t_st])
                    k.op("act", lambda h: h.activation(exc[:, 0:NC_], smc[:, 0:NC_], AF.Exp, bias=st[:, 1:2], scale=1.0), reads=[t_smc, t_st], writes=[t_smc])
                    k.op("dve", lambda h: h.tensor_tensor(exc[:, 0:NC_], exc[:, 0:NC_], cm01[:, qb, :], ALU.mult), reads=[t_smc, t_cm], writes=[t_smc])
                    k.op("dve", lambda h: h.reduce_sum(st[:, 2:3], exc[:, 0:NC_], AXX), reads=[t_smc], writes=[t_st])
                    k.op("dve", lambda h: h.tensor_scalar(st[:, 2:3], st[:, 2:3], 1e-30, None, ALU.max), reads=[t_st], writes=[t_st])
                    k.op("dve", lambda h: h.reciprocal(st[:, 3:4], st[:, 2:3]), reads=[t_st], writes=[t_st])
                    k.op("dve", lambda h: h.tensor_scalar(exc[:, 0:NC_], exc[:, 0:NC_], st[:, 3:4], None, ALU.mult), reads=[t_smc, t_st], writes=[t_smc])
                    if g == 0:
                        k.op("pool", lambda h: h.tensor_copy(psP[:, 0:NC_], exc[:, 0:NC_]), reads=[t_smc], writes=[t_psP])
                    else:
                        k.op("pool", lambda h: h.tensor_tensor(psP[:, 0:NC_], psP[:, 0:NC_], exc[:, 0:NC_], ALU.add), reads=[t_smc, t_psP], writes=[t_psP])
                    k.op("dve", lambda h: h.memset(pgc[:], 0.0), writes=[t_pgc])
                    k.op("dve", lambda h: h.tensor_scalar(pgc[:, 0:NC_], exc[:, 0:NC_], gat[:, qb, hh * 3:hh * 3 + 1], None, ALU.mult), reads=[t_smc, t_gat], writes=[t_pgc])
                    k.op("pe", lambda h: h.transpose(pT7[:, 0:128], pgc[:, :], ident_bf[:]), reads=[t_pgc, t_ident], writes=[t_pb[7]])
                    k.op("act", lambda h: h.copy(pcT[:, g, :], pT7[:, 0:128]), reads=[t_pb[7]], writes=[t_pcT])
                k.op("dve", lambda h: h.memset(psT[:], 0.0), writes=[t_psT])
                k.op("pe", lambda h: h.transpose(pbank[1][0:NC_, 0:128], psP[:, 0:NC_], ident[:]), reads=[t_psP, t_ident], writes=[t_pb[1]])
                k.op("dve", lambda h: h.tensor_copy(psT[0:NC_, :], pbank[1][0:NC_, 0:128]), reads=[t_pb[1]], writes=[t_psT])
                k.op("pe", lambda h: h.matmul(pbank[1][:, 256:256 + NB], psT[0:NC_, :], ovl[0:NC_, :], start=True, stop=True), reads=[t_psT, t_ovl], writes=[t_pb[1]])
                k.op("dve", lambda h: h.tensor_tensor(impm[:], pbank[1][:, 256:256 + NB], keep[:, qb, :], ALU.mult), reads=[t_pb[1], t_ko], writes=[t_imp])
                k.op("dve", lambda h: h.tensor_tensor(impm[:], impm[:], over[:, qb, :], ALU.add), reads=[t_imp, t_ko], writes=[t_imp])
                k.op("dve", lambda h: h.max(m8[:, 0:8], impm[:]), reads=[t_imp], writes=[t_imp])
                k.op("dve", lambda h: h.match_replace(impw[:], m8[:, 0:8], impm[:], -3.0e38), reads=[t_imp], writes=[t_imp])
                k.op("dve", lambda h: h.max(m8[:, 8:16], impw[:]), reads=[t_imp], writes=[t_imp])
                k.op("dve", lambda h: h.tensor_scalar(selb[:], impm[:], m8[:, 15:16], None, ALU.is_ge), reads=[t_imp], writes=[t_selb])
                k.op("dve", lambda h: h.tensor_scalar(selb[:], selb[:], NEG, -NEG, ALU.mult, ALU.add), reads=[t_selb], writes=[t_selb])
                nbk = 2 * (qb + 1)
                k.op("dve", lambda h: h.tensor_copy(bfull[:, 0:nk].rearrange("p (j c) -> p j c", c=64), selb[:, 0:nbk].unsqueeze(2).to_broadcast([128, nbk, 64])), reads=[t_selb], writes=[t_bf])
                k.op("dve", lambda h: h.tensor_tensor(bfull[:, nk - 128:nk], bfull[:, nk - 128:nk], tri[:], ALU.add), reads=[t_bf, t_tri], writes=[t_bf])
                kb0 = max(0, qb - 4); nkw = 128 * (qb + 1 - kb0); woff = 640 - nkw; nwb = qb + 1 - kb0
                for g in range(8):
                    hh = kvh * 8 + g
                    sm = sm2[g % 2]; t_sm = t_sm2[g % 2]; pg = pg2[g % 2]; t_pg = t_pg2[g % 2]
                    smw = smw2[g % 2]; t_smw = t_smw2[g % 2]; pgw = pgw2[g % 2]; t_pgw = t_pgw2[g % 2]
                    PTs = PTs2[g % 2]; t_PTs = t_PTs2b[g % 2]; PTw = PTw2[g % 2]; t_PTw = t_PTw2[g % 2]
                    for ci, ch in enumerate(range(0, nk, 512)):
                        w = min(512, nk - ch); pb = 2 + ci % 2
                        k.op("pe", lambda h: h.matmul(pbank[pb][:, 0:w], QTb[qi][:, hh, :], KsT[:, kvh, ch:ch + w], start=True, stop=True), reads=[t_QTb[qi], t_Kx], writes=[t_pb[pb]])
                        k.op("dve", lambda h: h.scalar_tensor_tensor(sm[:, ch:ch + w], pbank[pb][:, 0:w], SCALE, bfull[:, ch:ch + w], ALU.mult, ALU.add), reads=[t_pb[pb], t_bf], writes=[t_sm])
                    softmax_gated(sm[:, 0:nk], sm[:, 0:nk], pg[:, 0:nk], gat[:, qb, hh * 3 + 1:hh * 3 + 2], 128, t_sm, t_sm, t_pg)
                    for ci, ch in enumerate(range(0, nkw, 512)):
                        w = min(512, nkw - ch); pb = 4 + ci % 2
                        k.op("pe", lambda h: h.matmul(pbank[pb][:, 0:w], QTb[qi][:, hh, :], KwT[:, kvh, kb0 * 128 + ch:kb0 * 128 + ch + w], start=True, stop=True), reads=[t_QTb[qi], t_Kx], writes=[t_pb[pb]])
                        k.op("dve", lambda h: h.scalar_tensor_tensor(smw[:, ch:ch + w], pbank[pb][:, 0:w], SCALE, W640[:, woff + ch:woff + ch + w], ALU.mult, ALU.add), reads=[t_pb[pb], t_tri], writes=[t_smw])
                    softmax_gated(smw[:, 0:nkw], smw[:, 0:nkw], pgw[:, 0:nkw], gat[:, qb, hh * 3 + 2:hh * 3 + 3], 128, t_smw, t_smw, t_pgw)
                    transposes(PTs, t_PTs, pg, t_pg, qb + 1, 128)
                    transposes(PTw, t_PTw, pgw, t_pgw, nwb, 128)
                    k.op("pe", lambda h: h.matmul(pbank[6][:, 0:128], cmpV[0:NC_, kvh, :], pcT[0:NC_, g, :], start=True, stop=False), reads=[t_cV, t_pcT], writes=[t_pb[6]], inc=False)
                    for j in range(qb + 1):
                        k.op("pe", lambda h: h.matmul(pbank[6][:, 0:128], Vs[:, j, kvh * 128:(kvh + 1) * 128], PTs[:, j, :], start=False, stop=False), reads=[t_Vx, t_PTs], writes=[t_pb[6]], inc=False)
                    for j in range(nwb):
                        k.op("pe", lambda h: h.matmul(pbank[6][:, 0:128], Vw[:, kb0 + j, kvh * 128:(kvh + 1) * 128], PTw[:, j, :], start=False, stop=(j == nwb - 1)), reads=[t_Vx, t_PTw], writes=[t_pb[6]], inc=(j == nwb - 1))
                    k.op("act", lambda h: h.copy(OTb[qi][:, hh, :], pbank[6][:, 0:128]), reads=[t_pb[6]], writes=[t_OTb[qi]])
            k.dma("sp", OTv[:, :, qb * 128:(qb + 1) * 128], OTb[qi][:], reads=[t_OTb[qi]], writes=[t_OT])
        k.barrier()
        esA.close()
        esB = contextlib.ExitStack()
        NKP = 128 * 128; NKs = NKP + NS; NCs = 1023; NBs = 257
        gS = dscr("gS", [NS, 96]); t_gS = T()
        Qs = k.sb(esB, [128, 32, NS], BF16); QsT = k.sb(esB, [128, 4, 32], BF16); t_Qs = T()
        k.dma("sp", Qs[:], QT[:, :, TP:TP + NS].rearrange("h p t -> p h t"), reads=[t_QT], writes=[t_Qs])
        for kvh in range(4):
            k.op("dve", lambda h: h.tensor_copy(QsT[:, kvh, :].rearrange("p (i g) -> p i g", g=8), Qs[:, kvh * 8:(kvh + 1) * 8, :].rearrange("p g i -> p i g")), reads=[t_Qs], writes=[t_Qs])
        k.dma("sp", gS[:, :], gat[0:NS, NQB, :], reads=[t_gat], writes=[t_gS])
        gate_s = k.sb(esB, [32, 4, 8, 3], F32); t_gs = T()
        for i in range(NS):
            k.dma("sp", gate_s[8 * i:8 * i + 8, :, 0, :], gS[i].rearrange("(k g b) -> g k b", k=4, g=8), reads=[t_gS], writes=[t_gs])
        R32 = 8 * NS
        mis = k.sb(esB, [128, 520], I32); t_mis = T()
        keep_s = k.sb(esB, [32, NBs], F32); over_s = k.sb(esB, [32, NBs], F32); t_kos = T()
        k.op("dve", lambda h: h.memset(keep_s[:], 1.0), writes=[t_kos])
        k.op("dve", lambda h: h.memset(over_s[:], 0.0), writes=[t_kos])
        for (lo, hi) in ((0, 1), (NBs - 2, NBs)):
            k.op("dve", lambda h: h.memset(keep_s[:, lo:hi], 0.0), reads=[t_kos], writes=[t_kos])
            k.op("dve", lambda h: h.memset(over_s[:, lo:hi], BIG), reads=[t_kos], writes=[t_kos])
        tri_s = k.sb(esB, [32, NS], F32); wmask_s = k.sb(esB, [32, 512 + NS], F32); t_ms = T()
        k.op("pool", lambda h: h.iota(mis[0:32, 0:NS], pattern=[[-8, NS]], base=0, channel_multiplier=1), writes=[t_mis])
        k.op("dve", lambda h: h.tensor_copy(tri_s[:], mis[0:32, 0:NS]), reads=[t_mis], writes=[t_ms])
        k.op("dve", lambda h: h.tensor_single_scalar(tri_s[:], tri_s[:], 0.0, ALU.is_ge), reads=[t_ms], writes=[t_ms])
        k.op("dve", lambda h: h.tensor_scalar(tri_s[:], tri_s[:], NEG, -NEG, ALU.mult, ALU.add), reads=[t_ms], writes=[t_ms])
        k.op("pool", lambda h: h.iota(mis[0:32, 0:512], pattern=[[8, 512]], base=0, channel_multiplier=-1), reads=[t_ms], writes=[t_mis])
        k.op("dve", lambda h: h.tensor_copy(wmask_s[:, 0:512], mis[0:32, 0:512]), reads=[t_mis], writes=[t_ms])
        k.op("dve", lambda h: h.tensor_single_scalar(wmask_s[:, 0:512], wmask_s[:, 0:512], 0.0, ALU.is_gt), reads=[t_ms], writes=[t_ms])
        k.op("dve", lambda h: h.tensor_scalar(wmask_s[:, 0:512], wmask_s[:, 0:512], NEG, -NEG, ALU.mult, ALU.add), reads=[t_ms], writes=[t_ms])
        k.op("dve", lambda h: h.tensor_copy(wmask_s[:, 512:512 + NS], tri_s[:]), reads=[t_ms], writes=[t_ms])
        Gsum = k.sb(esB, [32, NS, 8], F32); gtmp = k.sb(esB, [32, NS, 8], F32)
        k.op("pool", lambda h: h.iota(mis[0:32, 0:32].rearrange("p (i g) -> p i g", g=8), pattern=[[-8, NS], [0, 8]], base=0, channel_multiplier=1), reads=[t_ms], writes=[t_mis])
        k.op("dve", lambda h: h.tensor_copy(Gsum[:], mis[0:32, 0:32].rearrange("p (i g) -> p i g", g=8)), reads=[t_mis], writes=[t_ms])
        k.op("dve", lambda h: h.tensor_single_scalar(gtmp[:], Gsum[:], 7.0, ALU.is_le), reads=[t_ms], writes=[t_ms])
        k.op("dve", lambda h: h.tensor_single_scalar(Gsum[:], Gsum[:], 0.0, ALU.is_ge), reads=[t_ms], writes=[t_ms])
        k.op("dve", lambda h: h.tensor_tensor(Gsum[:], Gsum[:], gtmp[:], ALU.mult), reads=[t_ms], writes=[t_ms])
        ovl_s = k.sb(esB, [128, 8, NBs], F32); ovt_s = k.sb(esB, [128, NBs], F32); t_ovs = T()
        for gi in range(8):
            cbase = 0 if gi == 0 else 128 * gi - 1
            k.op("pool", lambda h: h.iota(mis[:, 0:NBs], pattern=[[64, NBs]], base=64 - 16 * cbase, channel_multiplier=-16), reads=[t_ovs, t_ms], writes=[t_mis])
            k.op("dve", lambda h: h.tensor_copy(ovl_s[:, gi, :], mis[:, 0:NBs]), reads=[t_mis], writes=[t_ovs])
            k.op("dve", lambda h: h.tensor_single_scalar(ovl_s[:, gi, :], ovl_s[:, gi, :], 0.0, ALU.is_gt), reads=[t_ovs], writes=[t_ovs])
            k.op("pool", lambda h: h.iota(mis[:, 0:NBs], pattern=[[-64, NBs]], base=16 * cbase + 32, channel_multiplier=16), reads=[t_ovs], writes=[t_mis])
            k.op("dve", lambda h: h.tensor_copy(ovt_s[:], mis[:, 0:NBs]), reads=[t_mis], writes=[t_ovs])
            k.op("dve", lambda h: h.tensor_single_scalar(ovt_s[:], ovt_s[:], 0.0, ALU.is_gt), reads=[t_ovs], writes=[t_ovs])
            k.op("dve", lambda h: h.tensor_tensor(ovl_s[:, gi, :], ovl_s[:, gi, :], ovt_s[:], ALU.mult), reads=[t_ovs], writes=[t_ovs])

        pgb = [k.sb(esB, [128, 128], F32) for _ in range(4)]; t_pgb = [T() for _ in range(4)]
        kTp = [k.sb(esB, [128, 128], BF16) for _ in range(2)]; t_kTp = [T(), T()]
        vbp = [k.sb(esB, [128, 128], BF16) for _ in range(2)]; t_vbp = [T(), T()]
        sm_s = k.sb(esB, [32, NKs], F32); t_sms = T()
        pg_s = k.sb(esB, [32, NKs], BF16); t_pgs = T()
        PT_s = k.sb(esB, [128, 129, 32], BF16); t_PTs2 = T()
        smc_s = k.sb(esB, [32, 1024], F32); t_smcs = T()
        pgc_s = k.sb(esB, [32, 1024], BF16); t_pgcs = T()
        PcT_s = k.sb(esB, [128, 8, 32], BF16); t_PcTs = T()
        PTf = k.sb(esB, [128, 8, 32], F32); t_PTf = T()
        impr = k.sb(esB, [32, NBs], F32); impm_s = k.sb(esB, [32, NBs], F32); impw_s = k.sb(esB, [32, NBs], F32); m8s = k.sb(esB, [32, 16], F32); t_imps = T()
        selb_s = k.sb(esB, [32, NBs + 7], F32); t_selbs = T()
        smw_s = k.sb(esB, [32, 512 + NS], F32); t_smws = T()
        pgw_s = k.sb(esB, [32, 512 + NS], BF16); t_pgws = T()
        PTw_s = k.sb(esB, [128, 5, 32], BF16); t_PTws = T()
        KwT_s = k.sb(esB, [128, 512 + NS], BF16); Vw_s = k.sb(esB, [128, 5, 128], BF16); t_kws = T()
        cwk = k.sb(esB, [128, 4, 128], F32); cwv = k.sb(esB, [128, 4, 128], F32); t_cw2 = T()
        nr_t = k.sb(esB, [NS, 4, 128], F32); t_nr = T()
        OTs = k.sb(esB, [128, 32, NS], BF16); t_OTs = T()
        cwin = cache_win.rearrange("(a p) m -> p a m", p=128)

        def chunk_T(dst_ap, src_ap, nq, ncol, t_src, t_dst):
            k.op("pe", lambda h: h.transpose(pT7[0:ncol, 0:nq], src_ap, ident_bf[0:nq, 0:nq]), reads=[t_src, t_ident], writes=[t_pb[7]])
            k.op("act", lambda h: h.copy(dst_ap, pT7[0:ncol, 0:nq]), reads=[t_pb[7]], writes=[t_dst])

        for kvh in range(4):
            q_l = QsT[:, kvh, :]
            for j_, col in enumerate((1024, 1536, 2048, 2560)):
                k.dma("sp", nr_t[:, j_, :], kvS[:, col + kvh * 128:col + (kvh + 1) * 128], reads=[t_kvS], writes=[t_nr])
            for ci, ch in enumerate(range(0, NCs, 512)):
                w = min(512, NCs - ch); pb = 2 + ci % 2
                k.op("pe", lambda h: h.matmul(pbank[pb][0:R32, 0:w], q_l, cmpKT_s[:, kvh, ch:ch + w], start=True, stop=True), reads=[t_Qs, t_cKs], writes=[t_pb[pb]])
                k.op("dve", lambda h: h.tensor_scalar(smc_s[:, ch:ch + w], pbank[pb][0:R32, 0:w], SCALE, None, ALU.mult), reads=[t_pb[pb]], writes=[t_smcs])
            k.op("dve", lambda h: h.reduce_max(st[0:R32, 0:1], smc_s[:, 0:NCs], AXX), reads=[t_smcs], writes=[t_st])
            k.op("dve", lambda h: h.tensor_scalar(st[0:R32, 1:2], st[0:R32, 0:1], -1.0, None, ALU.mult), reads=[t_st], writes=[t_st])
            k.op("act", lambda h: h.activation(smc_s[:, 0:NCs], smc_s[:, 0:NCs], AF.Exp, bias=st[0:R32, 1:2], scale=1.0, accum_out=st[0:R32, 2:3]), reads=[t_smcs, t_st], writes=[t_smcs, t_st])
            k.op("dve", lambda h: h.reciprocal(st[0:R32, 3:4], st[0:R32, 2:3]), reads=[t_st], writes=[t_st])
            k.op("dve", lambda h: h.tensor_scalar(smc_s[:, 0:NCs], smc_s[:, 0:NCs], st[0:R32, 3:4], None, ALU.mult), reads=[t_smcs, t_st], writes=[t_smcs])
            k.op("dve", lambda h: h.tensor_scalar(pgc_s[:, 0:NCs], smc_s[:, 0:NCs], gate_s[:, kvh, 0, 0:1], None, ALU.mult), reads=[t_smcs, t_gs], writes=[t_pgcs])
            for gi in range(8):
                nblk = 127 if gi == 0 else 128; cbase = 0 if gi == 0 else 128 * gi - 1
                chunk_T(PcT_s[0:nblk, gi, :], pgc_s[:, cbase:cbase + nblk], R32, nblk, t_pgcs, t_PcTs)
                k.op("pe", lambda h: h.transpose(pbank[1][0:nblk, 0:R32], smc_s[:, cbase:cbase + nblk], ident[0:R32, 0:R32]), reads=[t_smcs, t_ident], writes=[t_pb[1]])
                k.op("dve", lambda h: h.tensor_copy(PTf[0:nblk, gi, :], pbank[1][0:nblk, 0:R32]), reads=[t_pb[1]], writes=[t_PTf])
            for gi in range(8):
                nblk = 127 if gi == 0 else 128
                k.op("pe", lambda h: h.matmul(pbank[0][0:R32, 0:NBs], PTf[0:nblk, gi, :], ovl_s[0:nblk, gi, :], start=(gi == 0), stop=(gi == 7)), reads=[t_PTf, t_ovs], writes=[t_pb[0]], inc=(gi == 7))
            k.op("dve", lambda h: h.tensor_copy(impr[:], pbank[0][0:R32, 0:NBs]), reads=[t_pb[0]], writes=[t_imps])
            k.op("pe", lambda h: h.matmul(pbank[0][0:R32, 0:NBs], Gsum[:].rearrange("p i g -> p (i g)"), impr[:], start=True, stop=True), reads=[t_imps, t_ms], writes=[t_pb[0]])
            k.op("dve", lambda h: h.tensor_tensor(impm_s[:], pbank[0][0:R32, 0:NBs], keep_s[:], ALU.mult), reads=[t_pb[0], t_kos], writes=[t_imps])
            k.op("dve", lambda h: h.tensor_tensor(impm_s[:], impm_s[:], over_s[:], ALU.add), reads=[t_imps, t_kos], writes=[t_imps])
            k.op("dve", lambda h: h.max(m8s[:, 0:8], impm_s[:]), reads=[t_imps], writes=[t_imps])
            k.op("dve", lambda h: h.match_replace(impw_s[:], m8s[:, 0:8], impm_s[:], -3.0e38), reads=[t_imps], writes=[t_imps])
            k.op("dve", lambda h: h.max(m8s[:, 8:16], impw_s[:]), reads=[t_imps], writes=[t_imps])
            k.op("dve", lambda h: h.memset(selb_s[:], 0.0), writes=[t_selbs])
            k.op("dve", lambda h: h.tensor_scalar(selb_s[:, 0:NBs], impm_s[:], m8s[:, 15:16], None, ALU.is_ge), reads=[t_imps], writes=[t_selbs])
            k.op("dve", lambda h: h.tensor_scalar(selb_s[:, 0:NBs], selb_s[:, 0:NBs], NEG, -NEG, ALU.mult, ALU.add), reads=[t_selbs], writes=[t_selbs])
            for pg_ in range(128):
                b2 = pg_ % 2; b4 = pg_ % 4
                k.idma(pgb[b4][:], cache_s, idx_k[kvh][:, pg_:pg_ + 1], reads=[t_idx], writes=[t_pgb[b4]])
                k.op("pe", lambda h: h.transpose(pbank[4 + b2][:, 0:128], pgb[b4][:], ident[:]), reads=[t_pgb[b4], t_ident], writes=[t_pb[4 + b2]])
                k.op("act", lambda h: h.copy(kTp[b2][:], pbank[4 + b2][:, 0:128]), reads=[t_pb[4 + b2]], writes=[t_kTp[b2]])
                k.op("pe", lambda h: h.matmul(pbank[2 + b2][0:R32, 0:128], q_l, kTp[b2][:], start=True, stop=True), reads=[t_Qs, t_kTp[b2]], writes=[t_pb[2 + b2]])
                k.op("dve", lambda h: h.scalar_tensor_tensor(sm_s[:, pg_ * 128:(pg_ + 1) * 128].rearrange("p (j c) -> p j c", c=64), pbank[2 + b2][0:R32, 0:128].rearrange("p (j c) -> p j c", c=64), SCALE,
                                                               selb_s[:, 2 * pg_:2 * pg_ + 2].unsqueeze(2).to_broadcast([R32, 2, 64]), ALU.mult, ALU.add), reads=[t_pb[2 + b2], t_selbs], writes=[t_sms])
            k.op("pe", lambda h: h.transpose(pbank[4][:, 0:NS], nr_t[:, 0, :], ident[0:NS, 0:NS]), reads=[t_nr, t_ident], writes=[t_pb[4]])
            k.op("act", lambda h: h.copy(kTp[0][:, 0:NS], pbank[4][:, 0:NS]), reads=[t_pb[4]], writes=[t_kTp[0]])
            k.op("pe", lambda h: h.matmul(pbank[2][0:R32, 0:NS], q_l, kTp[0][:, 0:NS], start=True, stop=True), reads=[t_Qs, t_kTp[0]], writes=[t_pb[2]])
            k.op("dve", lambda h: h.scalar_tensor_tensor(sm_s[:, NKP:NKs], pbank[2][0:R32, 0:NS], SCALE, tri_s[:], ALU.mult, ALU.add), reads=[t_pb[2], t_ms], writes=[t_sms])
            k.op("dve", lambda h: h.tensor_scalar(sm_s[:, NKP:NKs], sm_s[:, NKP:NKs], selb_s[:, NBs - 1:NBs], None, ALU.add), reads=[t_sms, t_selbs], writes=[t_sms])
            softmax_gated(sm_s[:, 0:NKs], sm_s[:, 0:NKs], pg_s[:, 0:NKs], gate_s[:, kvh, 0, 1:2], R32, t_sms, t_sms, t_pgs)
            for pg_ in range(128):
                chunk_T(PT_s[:, pg_, :], pg_s[:, pg_ * 128:(pg_ + 1) * 128], R32, 128, t_pgs, t_PTs2)
            chunk_T(PT_s[0:NS, 128, :], pg_s[:, NKP:NKs], R32, NS, t_pgs, t_PTs2)
            k.dma("sp", cwk[:], cwin[:, :, kvh * 128:(kvh + 1) * 128], writes=[t_cw2])
            k.dma("sp", cwv[:], cwin[:, :, 512 + kvh * 128:512 + (kvh + 1) * 128], writes=[t_cw2])
            for a_ in range(4):
                k.op("pe", lambda h: h.transpose(pbank[4][:, 0:128], cwk[:, a_, :], ident[:]), reads=[t_cw2, t_ident], writes=[t_pb[4]])
                k.op("act", lambda h: h.copy(KwT_s[:, a_ * 128:(a_ + 1) * 128], pbank[4][:, 0:128]), reads=[t_pb[4]], writes=[t_kws])
            k.op("dve", lambda h: h.tensor_copy(Vw_s[:, 0:4, :], cwv[:]), reads=[t_cw2], writes=[t_kws])
            k.op("pe", lambda h: h.transpose(pbank[4][:, 0:NS], nr_t[:, 2, :], ident[0:NS, 0:NS]), reads=[t_nr, t_ident], writes=[t_pb[4]])
            k.op("act", lambda h: h.copy(KwT_s[:, 512:512 + NS], pbank[4][:, 0:NS]), reads=[t_pb[4]], writes=[t_kws])
            k.op("dve", lambda h: h.tensor_copy(Vw_s[0:NS, 4, :], nr_t[:, 3, :]), reads=[t_nr], writes=[t_kws])
            for (ch, w) in ((0, 512), (512, NS)):
                k.op("pe", lambda h: h.matmul(pbank[3][0:R32, 0:w], q_l, KwT_s[:, ch:ch + w], start=True, stop=True), reads=[t_Qs, t_kws], writes=[t_pb[3]])
                k.op("dve", lambda h: h.scalar_tensor_tensor(smw_s[:, ch:ch + w], pbank[3][0:R32, 0:w], SCALE, wmask_s[:, ch:ch + w], ALU.mult, ALU.add), reads=[t_pb[3], t_ms], writes=[t_smws])
            softmax_gated(smw_s[:, :], smw_s[:, :], pgw_s[:, :], gate_s[:, kvh, 0, 2:3], R32, t_smws, t_smws, t_pgws)
            for a_ in range(4):
                chunk_T(PTw_s[:, a_, :], pgw_s[:, a_ * 128:(a_ + 1) * 128], R32, 128, t_pgws, t_PTws)
            chunk_T(PTw_s[0:NS, 4, :], pgw_s[:, 512:512 + NS], R32, NS, t_pgws, t_PTws)
            oT = pbank[6][:, 0:R32]
            for gi in range(8):
                nblk = 127 if gi == 0 else 128
                k.op("pe", lambda h: h.matmul(oT, cmpV_s[0:nblk, gi, kvh, :], PcT_s[0:nblk, gi, :], start=(gi == 0), stop=False), reads=[t_cKs, t_PcTs], writes=[t_pb[6]], inc=False)
            for a_ in range(4):
                k.op("pe", lambda h: h.matmul(oT, Vw_s[:, a_, :], PTw_s[:, a_, :], start=False, stop=False), reads=[t_kws, t_PTws], writes=[t_pb[6]], inc=False)
            k.op("pe", lambda h: h.matmul(oT, Vw_s[0:NS, 4, :], PTw_s[0:NS, 4, :], start=False, stop=False), reads=[t_kws, t_PTws], writes=[t_pb[6]], inc=False)
            k.op("dve", lambda h: h.tensor_copy(vbp[0][0:NS, :], nr_t[:, 1, :]), reads=[t_nr], writes=[t_vbp[0]])
            k.op("pe", lambda h: h.matmul(oT, vbp[0][0:NS, :], PT_s[0:NS, 128, :], start=False, stop=False), reads=[t_vbp[0], t_PTs2], writes=[t_pb[6]])
            for pg_ in range(128):
                b2 = pg_ % 2; b4 = pg_ % 4
                k.idma(pgb[b4][:], cache_s, idx_v[kvh][:, pg_:pg_ + 1], reads=[t_idx], writes=[t_pgb[b4]])
                k.op("dve", lambda h: h.tensor_copy(vbp[b2][:], pgb[b4][:]), reads=[t_pgb[b4]], writes=[t_vbp[b2]])
                k.op("pe", lambda h: h.matmul(oT, vbp[b2][:], PT_s[:, pg_, :], start=False, stop=(pg_ == 127)), reads=[t_vbp[b2], t_PTs2], writes=[t_pb[6]])
            k.op("act", lambda h: h.copy(OTs[:, kvh * 8:(kvh + 1) * 8, :].rearrange("p g i -> p i g"), pbank[6][:, 0:R32].rearrange("p (i g) -> p i g", g=8)), reads=[t_pb[6]], writes=[t_OTs])
        k.dma("sp", OTv[:, :, TP:TP + NS], OTs[:], reads=[t_OTs], writes=[t_OT])
        k.barrier()
        esB.close()
    k.barrier()

    with contextlib.ExitStack() as es:
        hbuf = k.sb(es, [128, KC, NT], BF16); t_h = T()
        for c8 in range(0, KC, 8):
            k.dma("sp", hbuf[:, c8:c8 + 8, :], OTv[:, c8:c8 + 8, :], reads=[t_OT], writes=[t_h])
        xc_ = [k.sb(es, [128, 512], F32) for _ in range(2)]; t_xc_ = [T(), T()]
        cnt = [0]
        def o_epi(idx, ti, pbs):
            c0, w, _s = tiles[ti]; b = cnt[0] % 2; cnt[0] += 1
            k.dma("sp", xc_[b][:, 0:w], xTv[:, idx, c0:c0 + w], reads=[t_xT], writes=[t_xc_[b]])
            k.op("dve", lambda h: h.tensor_tensor(xc_[b][:, 0:w], xc_[b][:, 0:w], pbank[pbs[0]][:, 0:w], ALU.add), reads=[t_pb[pbs[0]], t_xc_[b]], writes=[t_xc_[b]])
            k.dma("sp", xTv[:, idx, c0:c0 + w], xc_[b][:, 0:w], reads=[t_xc_[b]], writes=[t_xT])
        dense_ws(es, w_o, [[m] for m in range(KC)], hbuf, t_h, o_epi)
    k.barrier()

    ffn_layer(1)

    with contextlib.ExitStack() as es:
        xt = [k.sb(es, [128, KC, 128], F32) for _ in range(2)]; t_xt = [T(), T()]
        sq = k.sb(es, [128, 128], F32); t_sq = T()
        rstd = k.sb(es, [128, 128], F32); t_rstd = T()
        yc = [k.sb(es, [128, 4, 128], F32) for _ in range(2)]; t_yc = [T(), T()]
        yo = [k.sb(es, [128, D], F32) for _ in range(2)]; t_yo = [T(), T()]
        blocks = [(i * 128, min(128, TP - i * 128), 0) for i in range(NQB)] + [(TP, NS, 1)]
        for bi, (cc, hw, sq_) in enumerate(blocks):
            b = bi % 2
            k.dma("sp", xt[b][:, :, 0:hw], xTv[:, :, cc:cc + hw], reads=[t_xT], writes=[t_xt[b]])
            pb = 2 + b
            for c in range(KC):
                k.op("act", lambda h: h.activation(sq[:, 0:hw], xt[b][:, c, 0:hw], AF.Square), reads=[t_xt[b]], writes=[t_sq])
                k.op("pe", lambda h: h.matmul(pbank[pb][:, 0:hw], ones[:], sq[:, 0:hw], start=(c == 0), stop=(c == KC - 1)), reads=[t_sq, t_ones], writes=[t_pb[pb]])
            k.op("act", lambda h: h.activation(rstd[:, 0:hw], pbank[pb][:, 0:hw], AF.Sqrt, scale=1.0 / D, bias=epsc[:, 0:1]), reads=[t_pb[pb], t_iot], writes=[t_rstd])
            k.op("dve", lambda h: h.reciprocal(rstd[:, 0:hw], rstd[:, 0:hw]), reads=[t_rstd], writes=[t_rstd])
            for c4 in range(0, KC, 4):
                yi = (c4 // 4) % 2; n4 = min(4, KC - c4); pbt = 4 + yi
                for c in range(n4):
                    k.op("dve", lambda h: h.scalar_tensor_tensor(yc[yi][:, c, 0:hw], xt[b][:, c4 + c, 0:hw], gains[:, 5, c4 + c:c4 + c + 1], rstd[:, 0:hw], ALU.mult, ALU.mult),
                         reads=[t_xt[b], t_rstd, t_gains], writes=[t_yc[yi]])
                for c in range(n4):
                    k.op("pe", lambda h: h.transpose(pbank[pbt][0:hw, c * 128:(c + 1) * 128], yc[yi][:, c, 0:hw], ident[:]), reads=[t_yc[yi], t_ident], writes=[t_pb[pbt]], inc=(c == n4 - 1))
                k.op("act", lambda h: h.copy(yo[b][0:hw, c4 * 128:(c4 + n4) * 128], pbank[pbt][0:hw, 0:n4 * 128]), reads=[t_pb[pbt]], writes=[t_yo[b]])
            od = o_y_p[cc:cc + hw, :] if sq_ == 0 else o_y_s[0:hw, :]
            k.dma("sp", od, yo[b][0:hw, :], reads=[t_yo[b]], writes=[])

    glob.close()
    k.finish()
    return nc


def core_inputs(cfg, c, I):
    B = I["x_prompt"].shape[0]
    Q = cfg.Q
    m = {
        "xp": I["x_prompt"][c % B], "xs": I["x_sample"][c],
        "st_re": I["state_ssm_re"][0, c].reshape(Q, 128), "st_im": I["state_ssm_im"][0, c].reshape(Q, 128),
        "cache_kv": I["cache_kv"].reshape(-1, 2048), "page_tab": I["page_table"][c].reshape(1, 128),
        "st_conv": I["state_ffn_conv"][0, c], "st_conv1": I["state_ffn_conv"][1, c], "cache_win": I["cache_win"][c].reshape(512, 1024),
        "attn_g1": I["attn_norm"][1], "ffn_g1": I["ffn_norm"][1], "fin_g": I["final_norm"],
        "w_in1": I["ffn_w_in"][1], "w_down1": I["ffn_w_down"][1], "conv_w1": I["ffn_conv_w"][1], "conv_b1": I["ffn_conv_b"][1],
        "w_qg": I["w_qg"][0], "w_o": I["w_o"][0], "cmp_w1": I["cmp_w1"], "cmp_b1": I["cmp_b1"], "cmp_w2": I["cmp_w2"],
        "cmp_b2": I["cmp_b2"], "cmp_pe": I["cmp_pe"],
        "attn_g": I["attn_norm"][0], "ffn_g": I["ffn_norm"][0], "kv_g": I["kv_norm"],
        "lam_re": I["ssm_lam_re"][0].reshape(Q, 128), "lam_im": I["ssm_lam_im"][0].reshape(Q, 128),
        "log_step": I["ssm_log_step"][0].reshape(Q, 2),
        "b_re": I["ssm_b_re"][0], "b_im": I["ssm_b_im"][0],
        "c_re": I["ssm_c_re"][0].reshape(cfg.G * 16, 64), "c_im": I["ssm_c_im"][0].reshape(cfg.G * 16, 64),
        "ssm_d": I["ssm_d"][0], "w_glu": I["ssm_w_glu"][0], "w_in": I["ffn_w_in"][0], "w_down": I["ffn_w_down"][0],
        "conv_w": I["ffn_conv_w"][0], "conv_b": I["ffn_conv_b"][0], "w_kv": I["w_kv"],
    }
    return {n: np.ascontiguousarray(np.asarray(v, dtype=(np.int32 if n == "page_tab" else np.float32))) for n, v in m.items()}


def kernel(**I):
    I = {n: np.asarray(v) for n, v in I.items()}
    B, TP, D = I["x_prompt"].shape
    DB, NS, _ = I["x_sample"].shape
    DFF = I["ffn_conv_b"].shape[1]
    cfg = Cfg(D=D, TP=TP, NS=NS, DFF=DFF, NKV=I["w_kv"].shape[1], NPOOL=I["cache_kv"].shape[0])
    nc = build(cfg)
    ncores = 8
    in_maps = [core_inputs(cfg, c, I) for c in range(ncores)]
    res = run_bass_kernel_spmd(nc, in_maps, core_ids=list(range(ncores)))
    R = res.results
    G, Q = cfg.G, cfg.Q
    depth = I["ffn_w_in"].shape[0]
    f32 = np.float32
    y_p = np.stack([R[b]["o_y_p"] for b in range(B)]); y_s = np.stack([R[c]["o_y_s"] for c in range(DB)])
    ssm_re_p = np.stack([R[b]["o_ssm_re_p"].reshape(G, 64) for b in range(B)])[None]
    ssm_im_p = np.stack([R[b]["o_ssm_im_p"].reshape(G, 64) for b in range(B)])[None]
    ssm_re_s = np.stack([R[c]["o_ssm_re_s"].reshape(G, 64) for c in range(DB)])[None]
    ssm_im_s = np.stack([R[c]["o_ssm_im_s"].reshape(G, 64) for c in range(DB)])[None]
    conv_p = np.zeros((depth, B, 2, DFF), f32); conv_s = np.zeros((depth, DB, 2, DFF), f32)
    conv_p[0] = np.stack([R[b]["o_conv_p"] for b in range(B)])
    conv_s[0] = np.stack([R[c]["o_conv_s"] for c in range(DB)])
    conv_p[1] = np.stack([R[b]["o_conv_p1"] for b in range(B)])
    conv_s[1] = np.stack([R[c]["o_conv_s1"] for c in range(DB)])
    kv_p = np.stack([R[b]["o_kv_p"].reshape(TP, 4, 4, 128) for b in range(B)])
    kv_s = np.stack([R[c]["o_kv_s"].reshape(NS, 4, 4, 128) for c in range(DB)])
    win_p = np.stack([R[b]["o_win_p"].reshape(512, 2, 4, 128) for b in range(B)])
    win_s = np.stack([R[c]["o_win_s"].reshape(512, 2, 4, 128) for c in range(DB)])
    return (y_p, y_s, ssm_re_p.astype(f32), ssm_im_p.astype(f32), ssm_re_s.astype(f32), ssm_im_s.astype(f32),
            conv_p, conv_s, kv_p.astype(f32), kv_s.astype(f32), win_p.astype(f32), win_s.astype(f32))
```
